# Optimizing a Trainium2 kernel written in Bass

```python
import jax, jax.numpy as jnp
from jax import lax
import numpy as np

D_MODEL = 1024
BATCH = 8
SEQ = 4096
DEPTH = 2

N_MEM = 256
HEAD_DIM = 64
MIX_HEADS = 12
MIX_WIDTH = MIX_HEADS * HEAD_DIM
MEM_HEADS = 4
MEM_WIDTH = MEM_HEADS * HEAD_DIM
CAT_WIDTH = MIX_WIDTH + MEM_WIDTH
D_FF = 2816
CONV_WIDTH = 4
LRU_C = 8.0
BLOCK_Q = 128
NORM_EPS = 1e-6
F32 = jnp.float32

kernel_name = "hybrid_rglru_fox_macaron_memxattn"


def rmsnorm(x, g):
    xf = x.astype(F32)
    y = xf * lax.rsqrt(jnp.mean(xf * xf, axis=-1, keepdims=True) + NORM_EPS)
    return (y * g.astype(F32)).astype(x.dtype)


def swiglu(x, w_in, w_out):
    gate, up = jnp.split(x @ w_in, 2, axis=-1)
    return (jax.nn.silu(gate) * up) @ w_out


def memory_keys_values(mem, norm_g, w_kv, k_norm_g):
    B, N, _ = mem.shape
    k, v = jnp.split(rmsnorm(mem, norm_g) @ w_kv, 2, axis=-1)
    k = rmsnorm(k.reshape(B, N, MEM_HEADS, HEAD_DIM), k_norm_g)
    v = v.reshape(B, N, MEM_HEADS, HEAD_DIM)
    return k, v


def memory_cross_attention(q, mk, mv):
    B, S = q.shape[:2]
    s = jnp.einsum('bshd,bnhd->bhsn', q, mk, preferred_element_type=F32) * (HEAD_DIM ** -0.5)
    p = jax.nn.softmax(s, axis=-1)
    o = jnp.einsum('bhsn,bnhd->bshd', p.astype(mv.dtype), mv)
    return o.reshape(B, S, MEM_WIDTH)


def causal_depthwise_conv(u, w, b):
    S = u.shape[1]
    up = jnp.pad(u, ((0, 0), (CONV_WIDTH - 1, 0), (0, 0)))
    y = b
    for tap in range(CONV_WIDTH):
        y = y + up[:, tap:tap + S] * w[tap]
    return y


def _linear_recurrence_combine(c1, c2):
    a1, b1 = c1
    a2, b2 = c2
    return a1 * a2, a2 * b1 + b2


def rg_lru(xc, w_rg, b_rg, w_ig, b_ig, lam):
    B, S, W = xc.shape
    xh = xc.reshape(B, S, MIX_HEADS, HEAD_DIM)
    r = jax.nn.sigmoid(jnp.einsum('bshi,hij->bshj', xh, w_rg).reshape(B, S, W) + b_rg).astype(F32)
    gi = jax.nn.sigmoid(jnp.einsum('bshi,hij->bshj', xh, w_ig).reshape(B, S, W) + b_ig).astype(F32)
    log_a = -LRU_C * r * jax.nn.softplus(-lam.astype(F32))
    a = jnp.exp(log_a)
    bx = jnp.sqrt(-jnp.expm1(2.0 * log_a)) * (gi * xc.astype(F32))
    _, hs = lax.associative_scan(_linear_recurrence_combine, (a, bx), axis=1)
    return hs.astype(xc.dtype)


def forgetting_attention(q, k, v, f_logit, b_f, q_g, k_g):
    B, S, _ = q.shape
    q = rmsnorm(q.reshape(B, S, MIX_HEADS, HEAD_DIM), q_g).transpose(0, 2, 1, 3)
    k = rmsnorm(k.reshape(B, S, MIX_HEADS, HEAD_DIM), k_g).transpose(0, 2, 1, 3)
    v = v.reshape(B, S, MIX_HEADS, HEAD_DIM).transpose(0, 2, 1, 3)
    log_f = jax.nn.log_sigmoid(f_logit.astype(F32) + b_f.astype(F32))
    cum = jnp.cumsum(log_f, axis=1).transpose(0, 2, 1)
    nb = S // BLOCK_Q
    q_blocks = q.reshape(B, MIX_HEADS, nb, BLOCK_Q, HEAD_DIM).transpose(2, 0, 1, 3, 4)
    c_blocks = cum.reshape(B, MIX_HEADS, nb, BLOCK_Q).transpose(2, 0, 1, 3)
    key_pos = jnp.arange(S)
    scale = HEAD_DIM ** -0.5

    def one_block(args):
        qb, cb, bi = args
        s = jnp.einsum('bhqd,bhkd->bhqk', qb, k, preferred_element_type=F32) * scale
        s = s + cb[..., None] - cum[:, :, None, :]
        q_pos = bi * BLOCK_Q + jnp.arange(BLOCK_Q)
        s = jnp.where(key_pos[None, :] <= q_pos[:, None], s, -jnp.inf)
        p = jax.nn.softmax(s, axis=-1)
        return jnp.einsum('bhqk,bhkd->bhqd', p.astype(v.dtype), v)

    o = lax.map(one_block, (q_blocks, c_blocks, jnp.arange(nb)))
    return o.transpose(1, 0, 3, 2, 4).reshape(B, S, MIX_WIDTH)


def setup_inputs(seed: int = 0) -> dict:
    key = jax.random.key(seed)
    ks = iter(jax.random.split(key, 40))
    n_lru = (DEPTH + 1) // 2
    n_fox = DEPTH // 2

    def nrm(shape, fan_in):
        return jax.random.normal(next(ks), shape, F32) * (fan_in ** -0.5)

    def gain(shape):
        return 1.0 + 0.1 * jax.random.normal(next(ks), shape, F32)

    def small(shape, s=0.1):
        return s * jax.random.normal(next(ks), shape, F32)

    x = jax.random.normal(next(ks), (BATCH, SEQ, D_MODEL), F32)
    mem = jax.random.normal(next(ks), (BATCH, N_MEM, D_MODEL), F32)
    u = jax.random.uniform(next(ks), (n_lru, MIX_WIDTH), F32, minval=0.9, maxval=0.999)
    a0 = u ** (1.0 / LRU_C)
    lru_lambda = jnp.log(a0) - jnp.log1p(-a0)
    return {
        "x": x,
        "mem": mem,
        "mem_norm_g": gain((D_MODEL,)),
        "mem_w_kv": nrm((D_MODEL, 2 * MEM_WIDTH), D_MODEL),
        "mem_k_norm_g": gain((HEAD_DIM,)),
        "ffn1_norm_g": gain((DEPTH, D_MODEL)),
        "ffn1_w_in": nrm((DEPTH, D_MODEL, 2 * D_FF), D_MODEL),
        "ffn1_w_out": nrm((DEPTH, D_FF, D_MODEL), D_FF),
        "mix_norm_g": gain((DEPTH, D_MODEL)),
        "mix_w_out": nrm((DEPTH, CAT_WIDTH, D_MODEL), CAT_WIDTH),
        "memq_norm_g": gain((DEPTH, HEAD_DIM)),
        "ffn2_norm_g": gain((DEPTH, D_MODEL)),
        "ffn2_w_in": nrm((DEPTH, D_MODEL, 2 * D_FF), D_MODEL),
        "ffn2_w_out": nrm((DEPTH, D_FF, D_MODEL), D_FF),
        "lru_w_in": nrm((n_lru, D_MODEL, 2 * MIX_WIDTH + MEM_WIDTH), D_MODEL),
        "lru_conv_w": nrm((n_lru, CONV_WIDTH, MIX_WIDTH), CONV_WIDTH),
        "lru_conv_b": small((n_lru, MIX_WIDTH), 0.01),
        "lru_w_rg": nrm((n_lru, MIX_HEADS, HEAD_DIM, HEAD_DIM), HEAD_DIM),
        "lru_b_rg": small((n_lru, MIX_WIDTH)),
        "lru_w_ig": nrm((n_lru, MIX_HEADS, HEAD_DIM, HEAD_DIM), HEAD_DIM),
        "lru_b_ig": small((n_lru, MIX_WIDTH)),
        "lru_lambda": lru_lambda,
        "fox_w_in": nrm((n_fox, D_MODEL, 3 * MIX_WIDTH + MIX_HEADS + MEM_WIDTH), D_MODEL),
        "fox_b_f": jax.random.uniform(next(ks), (n_fox, MIX_HEADS), F32, minval=1.0, maxval=6.0),
        "fox_q_norm_g": gain((n_fox, HEAD_DIM)),
        "fox_k_norm_g": gain((n_fox, HEAD_DIM)),
    }


def reference(x, mem, mem_norm_g, mem_w_kv, mem_k_norm_g,
              ffn1_norm_g, ffn1_w_in, ffn1_w_out,
              mix_norm_g, mix_w_out, memq_norm_g,
              ffn2_norm_g, ffn2_w_in, ffn2_w_out,
              lru_w_in, lru_conv_w, lru_conv_b, lru_w_rg, lru_b_rg, lru_w_ig, lru_b_ig, lru_lambda,
              fox_w_in, fox_b_f, fox_q_norm_g, fox_k_norm_g):
    B, S, _ = x.shape
    mem_k, mem_v = memory_keys_values(mem, mem_norm_g, mem_w_kv, mem_k_norm_g)
    h = x
    for i in range(DEPTH):
        j = i // 2
        h = h + 0.5 * swiglu(rmsnorm(h, ffn1_norm_g[i]), ffn1_w_in[i], ffn1_w_out[i])
        hn = rmsnorm(h, mix_norm_g[i])
        if i % 2 == 0:
            x_br, g_br, q_mem = jnp.split(hn @ lru_w_in[j], [MIX_WIDTH, 2 * MIX_WIDTH], axis=-1)
            xc = causal_depthwise_conv(x_br, lru_conv_w[j], lru_conv_b[j])
            tok = rg_lru(xc, lru_w_rg[j], lru_b_rg[j], lru_w_ig[j], lru_b_ig[j], lru_lambda[j])
            tok = tok * jax.nn.gelu(g_br)
        else:
            q, k, v, f_logit, q_mem = jnp.split(
                hn @ fox_w_in[j],
                [MIX_WIDTH, 2 * MIX_WIDTH, 3 * MIX_WIDTH, 3 * MIX_WIDTH + MIX_HEADS], axis=-1)
            tok = forgetting_attention(q, k, v, f_logit, fox_b_f[j], fox_q_norm_g[j], fox_k_norm_g[j])
        q_mem = rmsnorm(q_mem.reshape(B, S, MEM_HEADS, HEAD_DIM), memq_norm_g[i])
        cross = memory_cross_attention(q_mem, mem_k, mem_v)
        h = h + jnp.concatenate([tok, cross], axis=-1) @ mix_w_out[i]
        h = h + 0.5 * swiglu(rmsnorm(h, ffn2_norm_g[i]), ffn2_w_in[i], ffn2_w_out[i])
    return h
```

```python
import contextlib
import numpy as np
import concourse.bass as bass
import concourse.mybir as mybir
from concourse.bass_utils import run_bass_kernel_spmd

F32 = mybir.dt.float32
BF16 = mybir.dt.bfloat16
AF = mybir.ActivationFunctionType
ALU = mybir.AluOpType

D = 1024
S = 4096
DFF = 2816
NMEM = 256
HD = 64
MIXW = 768
KC = D // 128
FC = DFF // 128
EPS = 1e-6
SYNC_SAME = True


class Buf:
    __slots__ = ("ap", "lw", "rd", "name")

    def __init__(self, ap, name=""):
        self.ap = ap if isinstance(ap, bass.AP) else ap[:]
        self.lw = None
        self.rd = {}
        self.name = name


class Q:
    def __init__(self, name, sem, is_pe=False):
        self.name = name
        self.sem = sem
        self.is_pe = is_pe
        self.count = 0
        self.prog = []
        self.waited = {}
        self.dma_sems = []
        self.dma_vals = []
        self.dma_next = 0


class FW:
    def __init__(self, nc, stack, n_dma_sems=8):
        self.nc = nc
        self.q = {}
        for name, is_pe in (("pe", True), ("act", False), ("dve", False), ("pool", False), ("sp", False)):
            sem = stack.enter_context(nc.semaphore(f"prog_{name}"))
            self.q[name] = Q(name, sem, is_pe)
        for name in ("sp", "act", "pool"):
            q = self.q[name]
            for i in range(n_dma_sems):
                q.dma_sems.append(stack.enter_context(nc.semaphore(f"dma_{name}_{i}")))
                q.dma_vals.append(0)

    def _deps(self, q, reads, writes, extra=()):
        deps = {}

        def need(tok):
            if tok is None:
                return
            k = id(tok[0])
            if k not in deps or deps[k][1] < tok[1]:
                deps[k] = tok

        for b in reads:
            need(b.lw)
        for b in writes:
            need(b.lw)
            for tok in b.rd.values():
                need(tok)
        for tok in extra:
            need(tok)
        for k, (sem, val, owner) in deps.items():
            if owner is q and (q.is_pe or not SYNC_SAME):
                continue
            if q.waited.get(k, 0) >= val:
                continue
            q.waited[k] = val
            q.prog.append(("wait", sem, val))

    def op(self, qname, fn, reads=(), writes=(), signal=True):
        q = self.q[qname]
        self._deps(q, reads, writes)
        if signal:
            q.count += 1
            tok = (q.sem, q.count, q)
        else:
            tok = (q.sem, q.count + 1, q)
        q.prog.append(("op", fn, signal))
        for b in writes:
            b.lw = tok
            b.rd = {}
        for b in reads:
            b.rd[q.name] = tok

    def dma(self, qname, out_ap, in_ap, reads=(), writes=()):
        q = self.q[qname]
        i = q.dma_next
        q.dma_next = (i + 1) % len(q.dma_sems)
        sem = q.dma_sems[i]
        cur = q.dma_vals[i]
        extra = [(sem, cur, None)] if cur > 0 else []
        self._deps(q, reads, writes, extra)
        q.dma_vals[i] = cur + 16
        tok = (sem, cur + 16, None)
        q.prog.append(("dma", out_ap, in_ap, sem))
        for b in writes:
            b.lw = tok
            b.rd = {}
        for b in reads:
            b.rd[("dma", id(sem))] = tok

    def finish(self):
        for name in ("sp", "act", "pool"):
            q = self.q[name]
            for sem, val in zip(q.dma_sems, q.dma_vals):
                if val > 0 and q.waited.get(id(sem), 0) < val:
                    q.prog.append(("wait", sem, val))

    def replay(self, qname, eng):
        q = self.q[qname]
        for item in q.prog:
            if item[0] == "wait":
                eng.wait_ge(item[1], item[2])
            elif item[0] == "op":
                inst = item[1](eng)
                if item[2]:
                    inst.then_inc(q.sem, 1)
            else:
                eng.dma_start(out=item[1], in_=item[2]).then_inc(item[3], 16)


class Ring:
    def __init__(self, bufs):
        self.bufs = bufs
        self.i = 0

    def next(self):
        b = self.bufs[self.i]
        self.i = (self.i + 1) % len(self.bufs)
        return b


WT_ELEMS = 2048


def _tiles_of(W, col_ranges, nk_split):
    out = []
    Kin = W.shape[0]
    kcs = Kin // 128
    Wr = W.reshape(kcs, 128, W.shape[1])
    for (c0, nc_) in col_ranges:
        k0 = 0
        for nk in nk_split:
            t = Wr[k0:k0 + nk, :, c0:c0 + nc_]
            t = np.transpose(t, (1, 0, 2)).reshape(128, nk * nc_)
            out.append(t)
            k0 += nk
        assert k0 == kcs
    return out


class WeightPlan:
    def __init__(self):
        self.tiles = []
        self.arrays = []

    def add(self, arrs):
        idx0 = len(self.arrays)
        self.arrays.extend(arrs)
        return list(range(idx0, idx0 + len(arrs)))

    def pack(self):
        out = np.zeros((len(self.arrays), 128, WT_ELEMS), np.float32)
        for i, a in enumerate(self.arrays):
            out[i, :, :a.shape[1]] = a
        return out


def weight_specs():
    sp = []
    for L in range(2):
        for nm, src_in, src_out in (("f1", "ffn1_w_in", "ffn1_w_out"), ("f2", "ffn2_w_in", "ffn2_w_out")):
            for g in range(DFF // 256):
                sp.append(((nm + "i", L, "g", g), src_in, L, g * 256, 256, 0, 8))
                sp.append(((nm + "i", L, "u", g), src_in, L, DFF + g * 256, 256, 0, 8))
            for oc in range(8):
                sp.append(((nm + "o", L, oc, 0), src_out, L, oc * 128, 128, 0, 11))
                sp.append(((nm + "o", L, oc, 1), src_out, L, oc * 128, 128, 11, 11))
        for g in range(4):
            sp.append((("mo", L, g), "mix_w_out", L, g * 256, 256, 0, 8))
    for g in range(7):
        sp.append((("mi", 0, g), "lru_w_in", 0, g * 256, 256, 0, 8))
    for g in range(6):
        sp.append((("mi", 1, g), "fox_w_in", 0, g * 256, 256, 0, 8))
    for g in range(3):
        sp.append((("mv", g), "fox_w_in", 0, 1536 + g * 256, 256, 0, 8))
    sp.append((("mf",), "fox_w_in", 0, 2304, 12, 0, 8))
    sp.append((("mq",), "fox_w_in", 0, 2316, 256, 0, 8))
    for g in range(2):
        sp.append((("kv", g), "mem_w_kv", None, g * 256, 256, 0, 8))
    for c in range(6):
        sp.append((("rg", c), "lru_w_rg", 0, c, 0, 0, 1))
        sp.append((("ig", c), "lru_w_ig", 0, c, 0, 0, 1))
    return sp


def pack_weights(inp):
    sp = weight_specs()
    out = np.zeros((len(sp), 128, WT_ELEMS), np.float32)
    index = {}
    for i, (key, src, L, c0, ncols, k0, nk) in enumerate(sp):
        index[key] = i
        W = inp[src]
        if key[0] in ("rg", "ig"):
            blk = W[0]
            c = c0
            out[i, 0:64, 0:64] = blk[2 * c]
            out[i, 64:128, 64:128] = blk[2 * c + 1]
            continue
        if L is not None:
            W = W[L]
        Wr = W.reshape(W.shape[0] // 128, 128, W.shape[1])
        t = Wr[k0:k0 + nk, :, c0:c0 + ncols]
        out[i, :, :nk * ncols] = np.transpose(t, (1, 0, 2)).reshape(128, nk * ncols)
    return out, index


def vec_cols():
    cols = {}
    n = 0
    for L in range(2):
        for nm in ("ffn1_norm_g", "mix_norm_g", "ffn2_norm_g"):
            cols[(nm, L)] = n
            n += 8
    cols["mem_norm_g"] = n; n += 8
    cols["mem_k_norm_g"] = n; n += 1
    cols[("memq_norm_g", 0)] = n; n += 1
    cols[("memq_norm_g", 1)] = n; n += 1
    cols["conv_w"] = n; n += 24
    cols["conv_b"] = n; n += 6
    cols["b_rg"] = n; n += 6
    cols["b_ig"] = n; n += 6
    cols["lam"] = n; n += 6
    cols["b_f"] = n; n += 1
    cols["fox_q_g"] = n; n += 1
    cols["fox_k_g"] = n; n += 1
    cols["_n"] = n
    return cols


def pack_vecs(inp):
    cols = vec_cols()
    v = np.zeros((128, cols["_n"]), np.float32)

    def fm(a):
        return a.reshape(-1, 128).T

    for L in range(2):
        for nm in ("ffn1_norm_g", "mix_norm_g", "ffn2_norm_g"):
            v[:, cols[(nm, L)]:cols[(nm, L)] + 8] = fm(inp[nm][L])
    v[:, cols["mem_norm_g"]:cols["mem_norm_g"] + 8] = fm(inp["mem_norm_g"])
    rep = lambda a: np.concatenate([a, a])
    v[:, cols["mem_k_norm_g"]] = rep(inp["mem_k_norm_g"])
    for L in range(2):
        v[:, cols[("memq_norm_g", L)]] = rep(inp["memq_norm_g"][L])
    for tap in range(4):
        v[:, cols["conv_w"] + tap * 6: cols["conv_w"] + tap * 6 + 6] = fm(inp["lru_conv_w"][0, tap])
    v[:, cols["conv_b"]:cols["conv_b"] + 6] = fm(inp["lru_conv_b"][0])
    v[:, cols["b_rg"]:cols["b_rg"] + 6] = fm(inp["lru_b_rg"][0])
    v[:, cols["b_ig"]:cols["b_ig"] + 6] = fm(inp["lru_b_ig"][0])
    v[:, cols["lam"]:cols["lam"] + 6] = fm(inp["lru_lambda"][0])
    v[0:12, cols["b_f"]] = inp["fox_b_f"][0]
    v[:, cols["fox_q_g"]] = rep(inp["fox_q_norm_g"][0])
    v[:, cols["fox_k_g"]] = rep(inp["fox_k_norm_g"][0])
    return v


NEG = -30000.0


def make_consts():
    c = {}
    c["ident"] = np.eye(128, dtype=np.float32)
    ones = np.ones((128, 128), np.float32)
    blk = np.zeros((128, 128), np.float32)
    blk[0:64, 0:64] = 1.0
    blk[64:128, 64:128] = 1.0
    c["ones_blk"] = np.concatenate([ones, blk], axis=1)
    md = np.zeros((128, 4, 512), np.float32)
    for kk in range(4):
        key = kk * 128 + np.arange(128)[:, None]
        qry = np.arange(512)[None, :]
        md[:, kk, :] = np.where(key <= qry, 0.0, NEG)
    c["maskd"] = md.reshape(128, 2048)
    sel = np.zeros((128, 12, 128), np.float32)
    for h in range(12):
        sel[h, h, :] = 1.0
    c["sel"] = sel.reshape(128, 12 * 128)
    return c


TT = 512
NCONST_VEC = None


class Prog:
    def __init__(self, n_tiles=S // TT, stages=("ffn1a", "mix0", "ffn2a", "ffn1b", "mix1", "ffn2b"), dbg=False):
        self.n_tiles = n_tiles
        self.stages = stages
        self.dbg = dbg
        self.stack = contextlib.ExitStack()
        nc = self.nc = bass.Bass("TRN2", target_bir_lowering=False)
        self.fw = FW(nc, self.stack)
        self.vc = vec_cols()
        self.widx = {k[0]: i for i, k in enumerate(weight_specs())}
        self.wspec = {k[0]: k for k in weight_specs()}
        nW = len(self.widx)
        dt = nc.dram_tensor
        self.x_d = dt("x", [S, D], F32, kind="ExternalInput").ap()
        self.mem_d = dt("mem", [NMEM, D], F32, kind="ExternalInput").ap()
        self.wts_d = dt("wts", [nW, 128, WT_ELEMS], F32, kind="ExternalInput").ap()
        self.vecs_d = dt("vecs", [128, self.vc["_n"]], F32, kind="ExternalInput").ap()
        self.ident_d = dt("ident", [128, 128], F32, kind="ExternalInput").ap()
        self.onesblk_d = dt("ones_blk", [128, 256], F32, kind="ExternalInput").ap()
        self.tri_d = dt("tri", [128, 128], F32, kind="ExternalInput").ap()
        self.oh_d = dt("onehot", [128, 16], F32, kind="ExternalInput").ap()
        self.out_d = dt("out", [S, D], F32, kind="ExternalOutput").ap()
        self.kc_d = dt("kcache", [6, 128, S], BF16, kind="Internal").ap()
        self.vc_d = dt("vcache", [6, 128, S // 128, 128], BF16, kind="Internal").ap()
        if dbg:
            self.dbg_d = dt("dbg", [128, 8 * 512], F32, kind="ExternalOutput").ap()
        self.kc_bufs = {}
        self.vc_bufs = {}
        self._alloc()
        self._prologue()
        for t in range(n_tiles):
            self._tile(t)
        self.fw.finish()
        self._emit()

    def sb(self, name, shape, dtype):
        return self.stack.enter_context(self.nc.sbuf_tensor("sb_" + name, shape, dtype))

    def _alloc(self):
        nc = self.nc
        sb = self.sb
        B = Buf
        self.ident = B(sb("ident", [128, 128], F32))
        self.onesblk = B(sb("onesblk", [128, 256], BF16))
        self.tri = B(sb("tri", [128, 128], F32))
        self.oh = B(sb("oh", [128, 16], F32))
        self.ones12 = B(sb("ones12", [128, 512], F32))
        self.vecs = B(sb("vecs", [128, self.vc["_n"]], F32))
        self.cL = B(sb("cL", [128, 12], F32))
        self.negbf = B(sb("negbf", [128, 1], F32))
        self.epsb = B(sb("epsb", [128, 1], F32))
        hT = sb("hT", [128, 8, TT], F32)
        self.hT = [B(hT[:, k, :], f"hT{k}") for k in range(8)]
        xn = sb("xn", [128, 8, TT], BF16)
        self.xn = [B(xn[:, k, :], f"xn{k}") for k in range(8)]
        act = sb("act", [128, FC, TT], BF16)
        self.act = [B(act[:, k, :], f"act{k}") for k in range(FC)]
        cat = sb("cat", [128, 8, TT], BF16)
        self.cat = [B(cat[:, k, :], f"cat{k}") for k in range(8)]
        self.wstage = Ring([B(sb(f"wst{i}", [128, WT_ELEMS], F32), f"wst{i}") for i in range(3)])
        self.wbf = Ring([B(sb(f"wbf{i}", [128, WT_ELEMS], BF16), f"wbf{i}") for i in range(4)])
        self.xin = B(sb("xin", [128, 4, D], F32), "xin")
        self.yout = B(sb("yout", [128, 4, D], F32), "yout") if False else self.xin
        self.mkT = B(sb("mkT", [128, 2, NMEM], BF16), "mkT")
        self.mv = B(sb("mv", [128, 2, 256], BF16), "mv")
        self.tmpf = Ring([B(sb(f"tmpf{i}", [128, TT], F32), f"tmpf{i}") for i in range(6)])
        self.tmpb = Ring([B(sb(f"tmpb{i}", [128, TT], BF16), f"tmpb{i}") for i in range(4)])
        xbr = sb("xbr", [128, 6, TT + 4], F32)
        self.xbr_t = xbr
        self.xbr = [B(xbr[:, c, :], f"xbr{c}") for c in range(6)]
        self.hstate = [B(sb(f"hstate{c}", [128, 1], F32), f"hstate{c}") for c in range(6)]
        qT = sb("qT", [128, 6, TT], BF16)
        self.qT = [B(qT[:, c, :], f"qT{c}") for c in range(6)]
        self.kst = Ring([B(sb(f"kst{i}", [128, 1024], BF16), f"kst{i}") for i in range(3)])
        self.vst = Ring([B(sb(f"vst{i}", [128, 8, 128], BF16), f"vst{i}") for i in range(3)])
        self.cum = B(sb("cum", [128, TT], F32), "cum")
        self.cumstate = B(sb("cumstate", [128, 1], F32), "cumstate")
        self.negcumT = B(sb("negcumT", [128, S // 128, 12], F32), "negcumT")
        self.vtok = B(sb("vtok", [128, 4, MIXW], BF16), "vtok")
        self.ps = Ring([B(self.stack.enter_context(nc.psum_tensor(f"ps{i}", [128, 512], F32)), f"ps{i}")
                        for i in range(4)])
        self.psacc = Ring([B(self.stack.enter_context(nc.psum_tensor(f"pa{i}", [128, 512], F32)), f"pa{i}")
                           for i in range(4)])
        qmn = sb("qmn", [128, 2, TT], BF16)
        self.qmn = [B(qmn[:, c, :], f"qmn{c}") for c in range(2)]
        self.cq = [B(sb(f"cq{i}", [128, TT], F32), f"cq{i}") for i in range(2)]

    def vcol(self, key, off=0, n=1, p0=0, p1=128):
        c = self.vc[key] + off
        return self.vecs.ap[p0:p1, c:c + n]

    def wtile(self, key):
        fw = self.fw
        _, src, L, c0, ncols, k0, nk = self.wspec[key]
        if key[0] in ("rg", "ig"):
            nk, ncols = 1, 128
        n = nk * ncols
        st = self.wstage.next()
        fw.dma("sp", st.ap[:, 0:n], self.wts_d[self.widx[key], :, 0:n], writes=[st])
        wb = self.wbf.next()
        fw.op("pool", lambda e, o=wb.ap[:, 0:n], i=st.ap[:, 0:n]: e.tensor_copy(o, i), reads=[st], writes=[wb])
        return wb, wb.ap[:, 0:n].rearrange("p (k n) -> p k n", k=nk)

    def mm(self, ps, out_ap, lhsT, rhs, start, stop, reads, signal=None):
        self.fw.op("pe", lambda e: e.matmul(out_ap, lhsT, rhs, start=start, stop=stop),
                   reads=reads, writes=[ps], signal=(stop if signal is None else signal))

    def mm1(self, ps, out_ap, lhsT, rhs, start, stop, reads):
        self.fw.op("pe", lambda e: e.matmul(out_ap, lhsT, rhs, start=start, stop=stop),
                   reads=reads, writes=[ps], signal=True)

    def act_(self, func, out_b, in_b, out_ap=None, in_ap=None, extra_reads=(), **kw):
        o = out_b.ap if out_ap is None else out_ap
        i = in_b.ap if in_ap is None else in_ap
        self.fw.op("act", lambda e: e.activation(o, i, func, **kw), reads=[in_b] + list(extra_reads), writes=[out_b])

    def tt(self, eng, out_b, a_b, b_b, op, out_ap=None, a_ap=None, b_ap=None):
        o = out_b.ap if out_ap is None else out_ap
        a = a_b.ap if a_ap is None else a_ap
        b = b_b.ap if b_ap is None else b_ap
        self.fw.op(eng, lambda e: e.tensor_tensor(o, a, b, op), reads=[a_b, b_b], writes=[out_b])

    def ts(self, eng, out_b, in_b, s1, s2, op0, op1, out_ap=None, in_ap=None, extra_reads=()):
        o = out_b.ap if out_ap is None else out_ap
        i = in_b.ap if in_ap is None else in_ap
        if op1 is None:
            fn = lambda e: e.tensor_scalar(o, i, s1, None, op0)
        else:
            fn = lambda e: e.tensor_scalar(o, i, s1, s2, op0, op1)
        self.fw.op(eng, fn, reads=[in_b] + list(extra_reads), writes=[out_b])

    def stt(self, out_b, in0_b, scalar, in1_b, op0, op1, out_ap=None, in0_ap=None, in1_ap=None, extra_reads=()):
        o = out_b.ap if out_ap is None else out_ap
        a = in0_b.ap if in0_ap is None else in0_ap
        b = in1_b.ap if in1_ap is None else in1_ap
        self.fw.op("dve", lambda e: e.scalar_tensor_tensor(o, a, scalar, b, op0, op1),
                   reads=[in0_b, in1_b] + list(extra_reads), writes=[out_b])

    def copy(self, eng, out_b, in_b, out_ap=None, in_ap=None):
        o = out_b.ap if out_ap is None else out_ap
        i = in_b.ap if in_ap is None else in_ap
        if eng == "act":
            fn = lambda e: e.copy(o, i)
        else:
            fn = lambda e: e.tensor_copy(o, i)
        self.fw.op(eng, fn, reads=[in_b], writes=[out_b])

    def rmsnorm_fm(self, src, gkey, dst, n=TT):
        ps = self.ps.next()
        for k in range(8):
            sq = self.tmpb.next()
            self.act_(AF.Square, sq, src[k], out_ap=sq.ap[:, 0:n], in_ap=src[k].ap[:, 0:n])
            self.mm(ps, ps.ap[:, 0:n], self.onesblk.ap[:, 0:128], sq.ap[:, 0:n], k == 0, k == 7, [sq, self.onesblk],
                    signal=True)
        rstd = self.tmpf.next()
        self.act_(AF.Sqrt, rstd, ps, out_ap=rstd.ap[:, 0:n], in_ap=ps.ap[:, 0:n], bias=self.epsb.ap[:, 0:1],
                  scale=1.0 / D, extra_reads=[self.epsb])
        self.fw.op("dve", lambda e: e.reciprocal(rstd.ap[:, 0:n], rstd.ap[:, 0:n]), reads=[rstd], writes=[rstd])
        for k in range(8):
            self.stt(dst[k], src[k], self.vcol(gkey, k), rstd, ALU.mult, ALU.mult,
                     out_ap=dst[k].ap[:, 0:n], in0_ap=src[k].ap[:, 0:n], in1_ap=rstd.ap[:, 0:n],
                     extra_reads=[self.vecs])

    def headnorm(self, ps, gkey, out_b, out_ap, n=TT):
        sq = self.tmpb.next()
        self.act_(AF.Square, sq, ps, out_ap=sq.ap[:, 0:n], in_ap=ps.ap[:, 0:n])
        pn = self.ps.next()
        self.mm(pn, pn.ap[:, 0:n], self.onesblk.ap[:, 128:256], sq.ap[:, 0:n], True, True, [sq, self.onesblk])
        rstd = self.tmpf.next()
        self.act_(AF.Sqrt, rstd, pn, out_ap=rstd.ap[:, 0:n], in_ap=pn.ap[:, 0:n], bias=self.epsb.ap[:, 0:1],
                  scale=1.0 / HD, extra_reads=[self.epsb])
        self.fw.op("dve", lambda e: e.reciprocal(rstd.ap[:, 0:n], rstd.ap[:, 0:n]), reads=[rstd], writes=[rstd])
        self.stt(out_b, ps, self.vcol(gkey), rstd, ALU.mult, ALU.mult,
                 out_ap=out_ap, in0_ap=ps.ap[:, 0:n], in1_ap=rstd.ap[:, 0:n], extra_reads=[self.vecs])

    def load_x(self, t):
        fw = self.fw
        fw.dma("act", self.xin.ap, self.x_d[t * TT:(t + 1) * TT, :].rearrange("(c p) d -> p c d", p=128),
               writes=[self.xin])
        for k in range(8):
            ps = self.ps.next()
            for tc in range(4):
                o = ps.ap[:, tc * 128:(tc + 1) * 128]
                i = self.xin.ap[:, tc, k * 128:(k + 1) * 128]
                fw.op("pe", lambda e, o=o, i=i: e.transpose(o, i, self.ident.ap),
                      reads=[self.xin, self.ident], writes=[ps], signal=(tc == 3))
            self.copy("act" if k % 2 else "dve", self.hT[k], ps)

    def store_out(self, t):
        fw = self.fw
        for tc in range(4):
            for half in range(2):
                ps = self.ps.next()
                for kk in range(4):
                    k = half * 4 + kk
                    o = ps.ap[:, kk * 128:(kk + 1) * 128]
                    i = self.hT[k].ap[:, tc * 128:(tc + 1) * 128]
                    fw.op("pe", lambda e, o=o, i=i: e.transpose(o, i, self.ident.ap),
                          reads=[self.hT[k], self.ident], writes=[ps], signal=(kk == 3))
                self.copy("act" if half else "dve", self.xin, ps,
                          out_ap=self.xin.ap[:, tc, half * 512:(half + 1) * 512])
        fw.dma("act", self.out_d[t * TT:(t + 1) * TT, :].rearrange("(c p) d -> p c d", p=128), self.xin.ap,
               reads=[self.xin])

    def ffn(self, L, nm, gname):
        fw = self.fw
        self.rmsnorm_fm(self.hT, (gname, L), self.xn)
        for g in range(DFF // 256):
            wg_b, wg = self.wtile((nm + "i", L, "g", g))
            wu_b, wu = self.wtile((nm + "i", L, "u", g))
            for fc in range(2):
                f = g * 2 + fc
                pg = self.ps.next()
                pu = self.ps.next()
                for k in range(8):
                    self.mm(pg, pg.ap, wg[:, k, fc * 128:(fc + 1) * 128], self.xn[k].ap, k == 0, k == 7,
                            [wg_b, self.xn[k]])
                for k in range(8):
                    self.mm(pu, pu.ap, wu[:, k, fc * 128:(fc + 1) * 128], self.xn[k].ap, k == 0, k == 7,
                            [wu_b, self.xn[k]])
                sg = self.tmpf.next()
                self.act_(AF.Silu, sg, pg)
                self.tt("dve", self.act[f], sg, pu, ALU.mult)
        for oc in range(8):
            w0b, w0 = self.wtile((nm + "o", L, oc, 0))
            w1b, w1 = self.wtile((nm + "o", L, oc, 1))
            py = self.ps.next()
            for k in range(FC):
                wb, w = (w0b, w0) if k < 11 else (w1b, w1)
                self.mm(py, py.ap, w[:, k % 11, :], self.act[k].ap, k == 0, k == FC - 1, [wb, self.act[k]])
            self.stt(self.hT[oc], py, 0.5, self.hT[oc], ALU.mult, ALU.add)

    def _prologue(self):
        fw = self.fw
        fw.dma("act", self.ident.ap, self.ident_d, writes=[self.ident])
        fw.dma("act", self.tri.ap, self.tri_d, writes=[self.tri])
        fw.dma("act", self.oh.ap, self.oh_d, writes=[self.oh])
        fw.dma("act", self.vecs.ap, self.vecs_d, writes=[self.vecs])
        st = self.tmpf.next()
        fw.dma("act", st.ap[:, 0:256], self.onesblk_d, writes=[st])
        self.copy("dve", self.onesblk, st, in_ap=st.ap[:, 0:256])
        fw.op("dve", lambda e: e.memset(self.epsb.ap, EPS), writes=[self.epsb])
        fw.op("dve", lambda e: e.memset(self.ones12.ap, 1.0), writes=[self.ones12])
        if "mix0" in self.stages or "mix1" in self.stages:
            self._prologue_mix()

    def _tile(self, t):
        st = self.stages
        self.load_x(t)
        if "ffn1a" in st:
            self.ffn(0, "f1", "ffn1_norm_g")
        if "mix0" in st:
            self.mix_lru(t)
        if "ffn2a" in st:
            self.ffn(0, "f2", "ffn2_norm_g")
        if "ffn1b" in st:
            self.ffn(1, "f1", "ffn1_norm_g")
        if "mix1" in st:
            self.mix_fox(t)
        if "ffn2b" in st:
            self.ffn(1, "f2", "ffn2_norm_g")
        self.store_out(t)

    def _emit(self):
        nc = self.nc
        fw = self.fw
        with nc.Block() as block:
            @block.tensor
            def _(e):
                fw.replay("pe", e)

            @block.scalar
            def _(e):
                fw.replay("act", e)

            @block.vector
            def _(e):
                fw.replay("dve", e)

            @block.gpsimd
            def _(e):
                fw.replay("pool", e)

            @block.sync
            def _(e):
                fw.replay("sp", e)
        self.stack.close()


_CACHE = {}


def host_inputs(inp):
    wts, _ = pack_weights(inp)
    vecs = pack_vecs(inp)
    c = make_consts()
    tri = np.where(np.arange(128)[:, None] <= np.arange(128)[None, :], 0.0, NEG).astype(np.float32)
    oh = np.zeros((128, 16), np.float32)
    for h in range(12):
        oh[h, h] = 1.0
    shared = {"wts": wts, "vecs": vecs, "ident": c["ident"], "ones_blk": c["ones_blk"], "tri": tri, "onehot": oh}
    return shared


def kernel(**inputs):
    inp = {k: np.asarray(v) for k, v in inputs.items()}
    shared = host_inputs(inp)
    if "prog" not in _CACHE:
        _CACHE["prog"] = Prog()
    prog = _CACHE["prog"]
    x = np.ascontiguousarray(inp["x"], dtype=np.float32)
    mem = np.ascontiguousarray(inp["mem"], dtype=np.float32)
    in_maps = []
    for b in range(8):
        m = dict(shared)
        m["x"] = x[b]
        m["mem"] = mem[b]
        in_maps.append(m)
    res = run_bass_kernel_spmd(prog.nc, in_maps, core_ids=list(range(8)))
    out = np.stack([np.asarray(r["out"], dtype=np.float32).reshape(S, D) for r in res.results], axis=0)
    return out


def _add_mixers():
    def _prologue_mix(self):
        fw = self.fw
        t = self.tmpf.next()
        lam = self.vecs.ap[:, self.vc["lam"]:self.vc["lam"] + 6]
        one = self.ones12.ap[:, 0:1]
        fw.op("act", lambda e: e.activation(t.ap[:, 0:6], lam, AF.Exp, scale=-1.0), reads=[self.vecs], writes=[t])
        fw.op("act", lambda e: e.activation(t.ap[:, 0:6], t.ap[:, 0:6], AF.Ln, bias=one), reads=[t, self.ones12], writes=[t])
        self.ts("dve", self.cL, t, -8.0, None, ALU.mult, None, out_ap=self.cL.ap[:, 0:6], in_ap=t.ap[:, 0:6])
        self.ts("dve", self.cL, t, -16.0, None, ALU.mult, None, out_ap=self.cL.ap[:, 6:12], in_ap=t.ap[:, 0:6])
        self.ts("dve", self.negbf, self.vecs, -1.0, None, ALU.mult, None, in_ap=self.vcol("b_f"))
        for c in range(6):
            fw.op("dve", lambda e, a=self.hstate[c].ap: e.memset(a, 0.0), writes=[self.hstate[c]])
            fw.op("dve", lambda e, a=self.xbr[c].ap[:, 0:4]: e.memset(a, 0.0), writes=[self.xbr[c]])
        fw.op("dve", lambda e: e.memset(self.cumstate.ap, 0.0), writes=[self.cumstate])
        fw.dma("act", self.xin.ap[:, 0:2, :], self.mem_d.rearrange("(c p) d -> p c d", p=128), writes=[self.xin])
        for k in range(8):
            ps = self.ps.next()
            for tc in range(2):
                o = ps.ap[:, tc * 128:(tc + 1) * 128]
                i = self.xin.ap[:, tc, k * 128:(k + 1) * 128]
                fw.op("pe", lambda e, o=o, i=i: e.transpose(o, i, self.ident.ap),
                      reads=[self.xin, self.ident], writes=[ps], signal=(tc == 1))
            self.copy("act" if k % 2 else "dve", self.hT[k], ps, out_ap=self.hT[k].ap[:, 0:256], in_ap=ps.ap[:, 0:256])
        self.rmsnorm_fm(self.hT, "mem_norm_g", self.xn, n=NMEM)
        wkb, wk = self.wtile(("kv", 0))
        wvb, wv = self.wtile(("kv", 1))
        for c in range(2):
            ps = self.ps.next()
            for k in range(8):
                self.mm(ps, ps.ap[:, 0:NMEM], wk[:, k, c * 128:(c + 1) * 128], self.xn[k].ap[:, 0:NMEM], k == 0, k == 7,
                        [wkb, self.xn[k]])
            self.headnorm(ps, "mem_k_norm_g", self.mkT, self.mkT.ap[:, c, :], n=NMEM)
        for nch in range(2):
            ps = self.ps.next()
            for k in range(8):
                self.mm(ps, ps.ap[:, 0:256], self.xn[k].ap[:, nch * 128:(nch + 1) * 128], wv[:, k, :], k == 0, k == 7,
                        [wvb, self.xn[k]])
            self.copy("act", self.mv, ps, out_ap=self.mv.ap[:, nch, :], in_ap=ps.ap[:, 0:256])

    def cross_attn(self, L):
        for c in range(2):
            o = self.psacc.next()
            den = self.psacc.next()
            for h2 in range(2):
                h = 2 * c + h2
                b0 = 64 * h2
                for nch in range(2):
                    s = self.ps.next()
                    self.mm1(s, s.ap, self.mkT.ap[b0:b0 + 64, c, nch * 128:(nch + 1) * 128],
                             self.qmn[c].ap[b0:b0 + 64, :], True, True, [self.mkT, self.qmn[c]])
                    e_ = self.tmpb.next()
                    self.act_(AF.Exp, e_, s, scale=0.125)
                    self.mm1(o, o.ap[b0:b0 + 64, :], self.mv.ap[:, nch, h * 64:(h + 1) * 64], e_.ap,
                             nch == 0, nch == 1, [self.mv, e_])
                    self.mm1(den, den.ap[b0:b0 + 64, :], self.onesblk.ap[:, 0:64], e_.ap,
                             nch == 0, nch == 1, [self.onesblk, e_])
            rec = self.tmpf.next()
            self.fw.op("dve", lambda e, r=rec, d=den: e.reciprocal(r.ap, d.ap), reads=[den], writes=[rec])
            self.tt("dve", self.cat[6 + c], o, rec, ALU.mult)

    def out_proj(self, L):
        for g in range(4):
            wb, w = self.wtile(("mo", L, g))
            for fc in range(2):
                oc = 2 * g + fc
                ps = self.ps.next()
                for k in range(8):
                    self.mm(ps, ps.ap, w[:, k, fc * 128:(fc + 1) * 128], self.cat[k].ap, k == 0, k == 7,
                            [wb, self.cat[k]])
                self.tt("dve", self.hT[oc], ps, self.hT[oc], ALU.add)

    def mix_lru(self, t):
        fw = self.fw
        self.rmsnorm_fm(self.hT, ("mix_norm_g", 0), self.xn)
        if t > 0:
            for c in range(6):
                b = self.xbr[c]
                fw.op("pool", lambda e, b=b: e.tensor_copy(b.ap[:, 0:3], b.ap[:, TT:TT + 3]), reads=[b], writes=[b])
        for g in range(7):
            wb, w = self.wtile(("mi", 0, g))
            for fc in range(2):
                oc = 2 * g + fc
                ps = self.ps.next()
                for k in range(8):
                    self.mm(ps, ps.ap, w[:, k, fc * 128:(fc + 1) * 128], self.xn[k].ap, k == 0, k == 7, [wb, self.xn[k]])
                if oc < 6:
                    self.copy("act", self.xbr[oc], ps, out_ap=self.xbr[oc].ap[:, 3:TT + 3])
                elif oc < 12:
                    c = oc - 6
                    u = self.tmpf.next()
                    self.act_(AF.Square, u, ps)
                    self.ts("dve", u, u, 0.044715, 1.0, ALU.mult, ALU.add)
                    self.tt("dve", u, u, ps, ALU.mult)
                    self.act_(AF.Sigmoid, u, u, scale=1.5957691216057308)
                    self.tt("dve", self.cat[c], u, ps, ALU.mult)
                else:
                    c = oc - 12
                    self.headnorm(ps, ("memq_norm_g", 0), self.qmn[c], self.qmn[c].ap)
        one = self.ones12.ap[:, 0:1]
        for c in range(6):
            xb = self.xbr[c]
            acc = self.tmpf.next()
            cw = lambda tap: self.vcol("conv_w", tap * 6 + c)
            self.ts("dve", acc, xb, cw(0), self.vcol("conv_b", c), ALU.mult, ALU.add, in_ap=xb.ap[:, 0:TT],
                    extra_reads=[self.vecs])
            for tap in range(1, 4):
                self.stt(acc, xb, cw(tap), acc, ALU.mult, ALU.add, in0_ap=xb.ap[:, tap:tap + TT], extra_reads=[self.vecs])
            xcb = self.tmpb.next()
            self.copy("pool", xcb, acc)
            wrb, wr = self.wtile(("rg", c))
            wib, wi = self.wtile(("ig", c))
            pr = self.ps.next()
            self.mm(pr, pr.ap, wr[:, 0, :], xcb.ap, True, True, [wrb, xcb])
            pi = self.ps.next()
            self.mm(pi, pi.ap, wi[:, 0, :], xcb.ap, True, True, [wib, xcb])
            r = self.tmpf.next()
            self.act_(AF.Sigmoid, r, pr, bias=self.vcol("b_rg", c), extra_reads=[self.vecs])
            gi = self.tmpf.next()
            self.act_(AF.Sigmoid, gi, pi, bias=self.vcol("b_ig", c), extra_reads=[self.vecs])
            a = self.tmpf.next()
            self.act_(AF.Exp, a, r, scale=self.cL.ap[:, c:c + 1], extra_reads=[self.cL])
            m = self.tmpf.next()
            self.act_(AF.Exp, m, r, scale=self.cL.ap[:, 6 + c:7 + c], extra_reads=[self.cL])
            self.act_(AF.Sqrt, m, m, scale=-1.0, bias=one, extra_reads=[self.ones12])
            self.tt("pool", gi, gi, acc, ALU.mult)
            self.tt("dve", gi, gi, m, ALU.mult)
            hs = self.tmpf.next()
            hst = self.hstate[c]
            fw.op("dve", lambda e, hs=hs, a=a, gi=gi, hst=hst: e.tensor_tensor_scan(hs.ap, a.ap, gi.ap, hst.ap, ALU.mult, ALU.add),
                  reads=[a, gi, hst], writes=[hs])
            self.copy("pool", hst, hs, in_ap=hs.ap[:, TT - 1:TT])
            self.tt("dve", self.cat[c], hs, self.cat[c], ALU.mult)
        self.cross_attn(0)
        self.out_proj(0)

    Prog._prologue_mix = _prologue_mix
    Prog.cross_attn = cross_attn
    Prog.out_proj = out_proj
    Prog.mix_lru = mix_lru


_add_mixers()


def _add_fox():
    def mix_fox(self, t):
        fw = self.fw
        self.rmsnorm_fm(self.hT, ("mix_norm_g", 1), self.xn)
        for g in range(6):
            wb, w = self.wtile(("mi", 1, g))
            for fc in range(2):
                oc = 2 * g + fc
                ps = self.ps.next()
                for k in range(8):
                    self.mm(ps, ps.ap, w[:, k, fc * 128:(fc + 1) * 128], self.xn[k].ap, k == 0, k == 7, [wb, self.xn[k]])
                if oc < 6:
                    self.headnorm(ps, "fox_q_g", self.qT[oc], self.qT[oc].ap)
                else:
                    c = oc - 6
                    kn = self.tmpb.next()
                    self.headnorm(ps, "fox_k_g", kn, kn.ap)
                    kb = Buf(self.kc_d[c, :, t * TT:(t + 1) * TT], f"kc{c}_{t}")
                    self.kc_bufs[(c, t)] = kb
                    fw.dma("act", kb.ap, kn.ap, reads=[kn], writes=[kb])
        for g in range(3):
            wb, w = self.wtile(("mv", g))
            for tc in range(4):
                ps = self.ps.next()
                for k in range(8):
                    self.mm(ps, ps.ap[:, 0:256], self.xn[k].ap[:, tc * 128:(tc + 1) * 128], w[:, k, :], k == 0, k == 7,
                            [wb, self.xn[k]])
                self.copy("act" if tc % 2 else "dve", self.vtok, ps,
                          out_ap=self.vtok.ap[:, tc, g * 256:(g + 1) * 256], in_ap=ps.ap[:, 0:256])
        for c in range(6):
            vb = Buf(self.vc_d[c, :, 4 * t:4 * t + 4, :], f"vc{c}_{t}")
            self.vc_bufs[(c, t)] = vb
            fw.dma("act", vb.ap, self.vtok.ap[:, :, c * 128:(c + 1) * 128], reads=[self.vtok], writes=[vb])
        wb, w = self.wtile(("mf",))
        ps = self.ps.next()
        for k in range(8):
            self.mm(ps, ps.ap[0:12, :], w[:, k, 0:12], self.xn[k].ap, k == 0, k == 7, [wb, self.xn[k]])
        lf = self.tmpf.next()
        self.act_(AF.Exp, lf, ps, out_ap=lf.ap[0:12, :], in_ap=ps.ap[0:12, :], scale=-1.0, bias=self.negbf.ap[0:12, :],
                  extra_reads=[self.negbf])
        self.act_(AF.Ln, lf, lf, out_ap=lf.ap[0:12, :], in_ap=lf.ap[0:12, :], bias=self.ones12.ap[0:12, 0:1],
                  extra_reads=[self.ones12])
        fw.op("dve", lambda e, lf=lf: e.tensor_tensor_scan(self.cum.ap[0:12, :], self.ones12.ap[0:12, :], lf.ap[0:12, :],
                                                            self.cumstate.ap[0:12, :], ALU.mult, ALU.subtract),
              reads=[lf, self.ones12, self.cumstate], writes=[self.cum])
        self.copy("pool", self.cumstate, self.cum, out_ap=self.cumstate.ap[0:12, :], in_ap=self.cum.ap[0:12, TT - 1:TT])
        for tc in range(4):
            tp = self.ps.next()
            fw.op("pe", lambda e, tp=tp, tc=tc: e.transpose(tp.ap[:, 0:12], self.cum.ap[0:12, tc * 128:(tc + 1) * 128],
                                                           self.ident.ap[0:12, 0:12]),
                  reads=[self.cum, self.ident], writes=[tp])
            self.ts("dve", self.negcumT, tp, -1.0, None, ALU.mult, None,
                    out_ap=self.negcumT.ap[:, 4 * t + tc, :], in_ap=tp.ap[:, 0:12])
        wb, w = self.wtile(("mq",))
        for c in range(2):
            ps = self.ps.next()
            for k in range(8):
                self.mm(ps, ps.ap, w[:, k, c * 128:(c + 1) * 128], self.xn[k].ap, k == 0, k == 7, [wb, self.xn[k]])
            self.headnorm(ps, ("memq_norm_g", 1), self.qmn[c], self.qmn[c].ap)
        nkeys = (t + 1) * TT
        for c in range(6):
            o = self.psacc.next()
            den = self.psacc.next()
            for h2 in range(2):
                h = 2 * c + h2
                sel = self.tmpf.next()
                self.ts("dve", sel, self.cum, self.oh.ap[0:12, h:h + 1], None, ALU.mult, None,
                        out_ap=sel.ap[0:12, :], in_ap=self.cum.ap[0:12, :], extra_reads=[self.oh])
                pb = self.ps.next()
                self.mm1(pb, pb.ap, self.ones12.ap[0:12, 0:128], sel.ap[0:12, :], True, True, [self.ones12, sel])
                self.copy("act", self.cq[h2], pb)
            for kb0 in range(0, nkeys, 1024):
                wk = min(1024, nkeys - kb0)
                kbuf = self.kst.next()
                vbuf = self.vst.next()
                tiles = range(kb0 // TT, (kb0 + wk) // TT)
                fw.dma("sp", kbuf.ap[:, 0:wk], self.kc_d[c, :, kb0:kb0 + wk],
                       reads=[self.kc_bufs[(c, tt_)] for tt_ in tiles], writes=[kbuf])
                fw.dma("sp", vbuf.ap[:, 0:wk // 128, :], self.vc_d[c, :, kb0 // 128:(kb0 + wk) // 128, :],
                       reads=[self.vc_bufs[(c, tt_)] for tt_ in tiles], writes=[vbuf])
                for h2 in range(2):
                    h = 2 * c + h2
                    b0 = 64 * h2
                    for k8 in range(wk // 128):
                        kcg = kb0 // 128 + k8
                        diag = kcg - 4 * t
                        q0 = 128 * diag if diag >= 0 else 0
                        s = self.ps.next()
                        self.mm1(s, s.ap[:, q0:TT], kbuf.ap[b0:b0 + 64, k8 * 128:(k8 + 1) * 128],
                                 self.qT[c].ap[b0:b0 + 64, q0:TT], True, True, [kbuf, self.qT[c]])
                        tm = self.tmpf.next()
                        self.stt(tm, s, 0.125, self.cq[h2], ALU.mult, ALU.add, out_ap=tm.ap[:, q0:TT],
                                 in0_ap=s.ap[:, q0:TT], in1_ap=self.cq[h2].ap[:, q0:TT])
                        if diag >= 0:
                            self.tt("pool", tm, tm, self.tri, ALU.add, out_ap=tm.ap[:, q0:q0 + 128],
                                    a_ap=tm.ap[:, q0:q0 + 128])
                        p = self.tmpb.next()
                        self.act_(AF.Exp, p, tm, out_ap=p.ap[:, q0:TT], in_ap=tm.ap[:, q0:TT],
                                  bias=self.negcumT.ap[:, kcg, h:h + 1], extra_reads=[self.negcumT])
                        first = kcg == 0
                        last = kcg == 4 * t + 3
                        self.mm1(o, o.ap[b0:b0 + 64, q0:TT], vbuf.ap[:, k8, b0:b0 + 64], p.ap[:, q0:TT], first, last,
                                 [vbuf, p])
                        self.mm1(den, den.ap[b0:b0 + 64, q0:TT], self.onesblk.ap[:, 0:64], p.ap[:, q0:TT], first, last,
                                 [self.onesblk, p])
            rec = self.tmpf.next()
            fw.op("dve", lambda e, r=rec, d=den: e.reciprocal(r.ap, d.ap), reads=[den], writes=[rec])
            self.tt("dve", self.cat[c], o, rec, ALU.mult)
        self.cross_attn(1)
        self.out_proj(1)

    Prog.mix_fox = mix_fox


_add_fox()
```

```python
import contextlib
import numpy as np
import concourse.bass as bass
import concourse.mybir as mybir
from concourse.bass_utils import run_bass_kernel_spmd

F32 = mybir.dt.float32
BF16 = mybir.dt.bfloat16
AF = mybir.ActivationFunctionType
ALU = mybir.AluOpType

D = 1024
S = 4096
DFF = 2816
NMEM = 256
HD = 64
MIXW = 768
KC = D // 128
FC = DFF // 128
EPS = 1e-6
SYNC_SAME = True


class Buf:
    __slots__ = ("ap", "lw", "rd", "name")

    def __init__(self, ap, name=""):
        self.ap = ap if isinstance(ap, bass.AP) else ap[:]
        self.lw = None
        self.rd = {}
        self.name = name


class Q:
    def __init__(self, name, sem, is_pe=False):
        self.name = name
        self.sem = sem
        self.is_pe = is_pe
        self.count = 0
        self.prog = []
        self.waited = {}
        self.dma_sems = []
        self.dma_vals = []
        self.dma_next = 0


class FW:
    def __init__(self, nc, stack, n_dma_sems=8):
        self.nc = nc
        self.q = {}
        for name, is_pe in (("pe", True), ("act", False), ("dve", False), ("pool", False), ("sp", False)):
            sem = stack.enter_context(nc.semaphore(f"prog_{name}"))
            self.q[name] = Q(name, sem, is_pe)
        for name in ("sp", "act", "pool"):
            q = self.q[name]
            for i in range(n_dma_sems):
                q.dma_sems.append(stack.enter_context(nc.semaphore(f"dma_{name}_{i}")))
                q.dma_vals.append(0)

    def _deps(self, q, reads, writes, extra=()):
        deps = {}

        def need(tok):
            if tok is None:
                return
            k = id(tok[0])
            if k not in deps or deps[k][1] < tok[1]:
                deps[k] = tok

        for b in reads:
            need(b.lw)
        for b in writes:
            need(b.lw)
            for tok in b.rd.values():
                need(tok)
        for tok in extra:
            need(tok)
        for k, (sem, val, owner) in deps.items():
            if owner is q and (q.is_pe or not SYNC_SAME):
                continue
            if q.waited.get(k, 0) >= val:
                continue
            q.waited[k] = val
            q.prog.append(("wait", sem, val))

    def op(self, qname, fn, reads=(), writes=(), signal=True):
        q = self.q[qname]
        self._deps(q, reads, writes)
        if signal:
            q.count += 1
            tok = (q.sem, q.count, q)
        else:
            tok = (q.sem, q.count + 1, q)
        q.prog.append(("op", fn, signal))
        for b in writes:
            b.lw = tok
            b.rd = {}
        for b in reads:
            b.rd[q.name] = tok

    def dma(self, qname, out_ap, in_ap, reads=(), writes=()):
        q = self.q[qname]
        i = q.dma_next
        q.dma_next = (i + 1) % len(q.dma_sems)
        sem = q.dma_sems[i]
        cur = q.dma_vals[i]
        extra = [(sem, cur, None)] if cur > 0 else []
        self._deps(q, reads, writes, extra)
        q.dma_vals[i] = cur + 16
        tok = (sem, cur + 16, None)
        q.prog.append(("dma", out_ap, in_ap, sem))
        for b in writes:
            b.lw = tok
            b.rd = {}
        for b in reads:
            b.rd[("dma", id(sem))] = tok

    def finish(self):
        for name in ("sp", "act", "pool"):
            q = self.q[name]
            for sem, val in zip(q.dma_sems, q.dma_vals):
                if val > 0 and q.waited.get(id(sem), 0) < val:
                    q.prog.append(("wait", sem, val))

    def replay(self, qname, eng):
        q = self.q[qname]
        for item in q.prog:
            if item[0] == "wait":
                eng.wait_ge(item[1], item[2])
            elif item[0] == "op":
                inst = item[1](eng)
                if item[2]:
                    inst.then_inc(q.sem, 1)
            else:
                eng.dma_start(out=item[1], in_=item[2]).then_inc(item[3], 16)


class Ring:
    def __init__(self, bufs):
        self.bufs = bufs
        self.i = 0

    def next(self):
        b = self.bufs[self.i]
        self.i = (self.i + 1) % len(self.bufs)
        return b


WT_ELEMS = 2048


def _tiles_of(W, col_ranges, nk_split):
    out = []
    Kin = W.shape[0]
    kcs = Kin // 128
    Wr = W.reshape(kcs, 128, W.shape[1])
    for (c0, nc_) in col_ranges:
        k0 = 0
        for nk in nk_split:
            t = Wr[k0:k0 + nk, :, c0:c0 + nc_]
            t = np.transpose(t, (1, 0, 2)).reshape(128, nk * nc_)
            out.append(t)
            k0 += nk
        assert k0 == kcs
    return out


class WeightPlan:
    def __init__(self):
        self.tiles = []
        self.arrays = []

    def add(self, arrs):
        idx0 = len(self.arrays)
        self.arrays.extend(arrs)
        return list(range(idx0, idx0 + len(arrs)))

    def pack(self):
        out = np.zeros((len(self.arrays), 128, WT_ELEMS), np.float32)
        for i, a in enumerate(self.arrays):
            out[i, :, :a.shape[1]] = a
        return out


def weight_specs():
    sp = []
    for L in range(2):
        for nm, src_in, src_out in (("f1", "ffn1_w_in", "ffn1_w_out"), ("f2", "ffn2_w_in", "ffn2_w_out")):
            for g in range(DFF // 256):
                sp.append(((nm + "i", L, "g", g), src_in, L, g * 256, 256, 0, 8))
                sp.append(((nm + "i", L, "u", g), src_in, L, DFF + g * 256, 256, 0, 8))
            for oc in range(8):
                sp.append(((nm + "o", L, oc, 0), src_out, L, oc * 128, 128, 0, 11))
                sp.append(((nm + "o", L, oc, 1), src_out, L, oc * 128, 128, 11, 11))
        for g in range(4):
            sp.append((("mo", L, g), "mix_w_out", L, g * 256, 256, 0, 8))
    for g in range(7):
        sp.append((("mi", 0, g), "lru_w_in", 0, g * 256, 256, 0, 8))
    for g in range(6):
        sp.append((("mi", 1, g), "fox_w_in", 0, g * 256, 256, 0, 8))
    for g in range(3):
        sp.append((("mv", g), "fox_w_in", 0, 1536 + g * 256, 256, 0, 8))
    sp.append((("mf",), "fox_w_in", 0, 2304, 12, 0, 8))
    sp.append((("mq",), "fox_w_in", 0, 2316, 256, 0, 8))
    for g in range(2):
        sp.append((("kv", g), "mem_w_kv", None, g * 256, 256, 0, 8))
    for c in range(6):
        sp.append((("rg", c), "lru_w_rg", 0, c, 0, 0, 1))
        sp.append((("ig", c), "lru_w_ig", 0, c, 0, 0, 1))
    return sp


def pack_weights(inp):
    sp = weight_specs()
    out = np.zeros((len(sp), 128, WT_ELEMS), np.float32)
    index = {}
    for i, (key, src, L, c0, ncols, k0, nk) in enumerate(sp):
        index[key] = i
        W = inp[src]
        if key[0] in ("rg", "ig"):
            blk = W[0]
            c = c0
            out[i, 0:64, 0:64] = blk[2 * c]
            out[i, 64:128, 64:128] = blk[2 * c + 1]
            continue
        if L is not None:
            W = W[L]
        Wr = W.reshape(W.shape[0] // 128, 128, W.shape[1])
        t = Wr[k0:k0 + nk, :, c0:c0 + ncols]
        out[i, :, :nk * ncols] = np.transpose(t, (1, 0, 2)).reshape(128, nk * ncols)
    return out, index


def vec_cols():
    cols = {}
    n = 0
    for L in range(2):
        for nm in ("ffn1_norm_g", "mix_norm_g", "ffn2_norm_g"):
            cols[(nm, L)] = n
            n += 8
    cols["mem_norm_g"] = n; n += 8
    cols["mem_k_norm_g"] = n; n += 1
    cols[("memq_norm_g", 0)] = n; n += 1
    cols[("memq_norm_g", 1)] = n; n += 1
    cols["conv_w"] = n; n += 24
    cols["conv_b"] = n; n += 6
    cols["b_rg"] = n; n += 6
    cols["b_ig"] = n; n += 6
    cols["lam"] = n; n += 6
    cols["b_f"] = n; n += 1
    cols["fox_q_g"] = n; n += 1
    cols["fox_k_g"] = n; n += 1
    cols["_n"] = n
    return cols


def pack_vecs(inp):
    cols = vec_cols()
    v = np.zeros((128, cols["_n"]), np.float32)

    def fm(a):
        return a.reshape(-1, 128).T

    for L in range(2):
        for nm in ("ffn1_norm_g", "mix_norm_g", "ffn2_norm_g"):
            v[:, cols[(nm, L)]:cols[(nm, L)] + 8] = fm(inp[nm][L])
    v[:, cols["mem_norm_g"]:cols["mem_norm_g"] + 8] = fm(inp["mem_norm_g"])
    rep = lambda a: np.concatenate([a, a])
    v[:, cols["mem_k_norm_g"]] = rep(inp["mem_k_norm_g"])
    for L in range(2):
        v[:, cols[("memq_norm_g", L)]] = rep(inp["memq_norm_g"][L])
    for tap in range(4):
        v[:, cols["conv_w"] + tap * 6: cols["conv_w"] + tap * 6 + 6] = fm(inp["lru_conv_w"][0, tap])
    v[:, cols["conv_b"]:cols["conv_b"] + 6] = fm(inp["lru_conv_b"][0])
    v[:, cols["b_rg"]:cols["b_rg"] + 6] = fm(inp["lru_b_rg"][0])
    v[:, cols["b_ig"]:cols["b_ig"] + 6] = fm(inp["lru_b_ig"][0])
    v[:, cols["lam"]:cols["lam"] + 6] = fm(inp["lru_lambda"][0])
    v[0:12, cols["b_f"]] = inp["fox_b_f"][0]
    v[:, cols["fox_q_g"]] = rep(inp["fox_q_norm_g"][0])
    v[:, cols["fox_k_g"]] = rep(inp["fox_k_norm_g"][0])
    return v


NEG = -30000.0


def make_consts():
    c = {}
    c["ident"] = np.eye(128, dtype=np.float32)
    ones = np.ones((128, 128), np.float32)
    blk = np.zeros((128, 128), np.float32)
    blk[0:64, 0:64] = 1.0
    blk[64:128, 64:128] = 1.0
    c["ones_blk"] = np.concatenate([ones, blk], axis=1)
    md = np.zeros((128, 4, 512), np.float32)
    for kk in range(4):
        key = kk * 128 + np.arange(128)[:, None]
        qry = np.arange(512)[None, :]
        md[:, kk, :] = np.where(key <= qry, 0.0, NEG)
    c["maskd"] = md.reshape(128, 2048)
    sel = np.zeros((128, 12, 128), np.float32)
    for h in range(12):
        sel[h, h, :] = 1.0
    c["sel"] = sel.reshape(128, 12 * 128)
    return c


TT = 512
NCONST_VEC = None


class Prog:
    def __init__(self, n_tiles=S // TT, stages=("ffn1a", "mix0", "ffn2a", "ffn1b", "mix1", "ffn2b"), dbg=False):
        self.n_tiles = n_tiles
        self.stages = stages
        self.dbg = dbg
        self.stack = contextlib.ExitStack()
        nc = self.nc = bass.Bass("TRN2", target_bir_lowering=False)
        self.fw = FW(nc, self.stack)
        self.vc = vec_cols()
        self.widx = {k[0]: i for i, k in enumerate(weight_specs())}
        self.wspec = {k[0]: k for k in weight_specs()}
        nW = len(self.widx)
        dt = nc.dram_tensor
        self.x_d = dt("x", [S, D], F32, kind="ExternalInput").ap()
        self.mem_d = dt("mem", [NMEM, D], F32, kind="ExternalInput").ap()
        self.wts_d = dt("wts", [nW, 128, WT_ELEMS], F32, kind="ExternalInput").ap()
        self.vecs_d = dt("vecs", [128, self.vc["_n"]], F32, kind="ExternalInput").ap()
        self.ident_d = dt("ident", [128, 128], F32, kind="ExternalInput").ap()
        self.onesblk_d = dt("ones_blk", [128, 256], F32, kind="ExternalInput").ap()
        self.tri_d = dt("tri", [128, 128], F32, kind="ExternalInput").ap()
        self.oh_d = dt("onehot", [128, 16], F32, kind="ExternalInput").ap()
        self.out_d = dt("out", [S, D], F32, kind="ExternalOutput").ap()
        self.kc_d = dt("kcache", [6, 128, S], BF16, kind="Internal").ap()
        self.vc_d = dt("vcache", [6, 128, S // 128, 128], BF16, kind="Internal").ap()
        self.wbf_d = dt("wbf16", [nW, 128, WT_ELEMS], BF16, kind="Internal").ap()
        self.wconv = {}
        self.ncast = 0
        if dbg:
            self.dbg_d = dt("dbg", [128, 8 * 512], F32, kind="ExternalOutput").ap()
        self.kc_bufs = {}
        self.vc_bufs = {}
        self._alloc()
        self._prologue()
        for t in range(n_tiles):
            self._tile(t)
        self.fw.finish()
        self._emit()

    def sb(self, name, shape, dtype):
        return self.stack.enter_context(self.nc.sbuf_tensor("sb_" + name, shape, dtype))

    def _alloc(self):
        nc = self.nc
        sb = self.sb
        B = Buf
        self.ident = B(sb("ident", [128, 128], F32))
        self.onesblk = B(sb("onesblk", [128, 256], BF16))
        self.tri = B(sb("tri", [128, 128], F32))
        self.oh = B(sb("oh", [128, 16], F32))
        self.ones12 = B(sb("ones12", [128, 512], F32))
        self.vecs = B(sb("vecs", [128, self.vc["_n"]], F32))
        self.cL = B(sb("cL", [128, 12], F32))
        self.negbf = B(sb("negbf", [128, 1], F32))
        self.epsb = B(sb("epsb", [128, 1], F32))
        hT = sb("hT", [128, 8, TT], F32)
        self.hT = [B(hT[:, k, :], f"hT{k}") for k in range(8)]
        xn = sb("xn", [128, 8, TT], BF16)
        self.xn = [B(xn[:, k, :], f"xn{k}") for k in range(8)]
        act = sb("act", [128, FC, TT], BF16)
        self.act = [B(act[:, k, :], f"act{k}") for k in range(FC)]
        cat = sb("cat", [128, 8, TT], BF16)
        self.cat = [B(cat[:, k, :], f"cat{k}") for k in range(8)]
        self.wstage = Ring([B(sb(f"wst{i}", [128, WT_ELEMS], F32), f"wst{i}") for i in range(2)])
        self.wbf = Ring([B(sb(f"wbf{i}", [128, WT_ELEMS], BF16), f"wbf{i}") for i in range(6)])
        self.xin = B(sb("xin", [128, 4, D], F32), "xin")
        self.yout = B(sb("yout", [128, 4, D], F32), "yout") if False else self.xin
        self.mkT = B(sb("mkT", [128, 2, NMEM], BF16), "mkT")
        self.mv = B(sb("mv", [128, 2, 256], BF16), "mv")
        self.tmpf = Ring([B(sb(f"tmpf{i}", [128, TT], F32), f"tmpf{i}") for i in range(6)])
        self.tmpb = Ring([B(sb(f"tmpb{i}", [128, TT], BF16), f"tmpb{i}") for i in range(4)])
        xbr = sb("xbr", [128, 6, TT + 4], F32)
        self.xbr_t = xbr
        self.xbr = [B(xbr[:, c, :], f"xbr{c}") for c in range(6)]
        self.hstate = [B(sb(f"hstate{c}", [128, 1], F32), f"hstate{c}") for c in range(6)]
        qT = sb("qT", [128, 6, TT], BF16)
        self.qT = [B(qT[:, c, :], f"qT{c}") for c in range(6)]
        self.kst = Ring([B(sb(f"kst{i}", [128, 1024], BF16), f"kst{i}") for i in range(3)])
        self.vst = Ring([B(sb(f"vst{i}", [128, 8, 128], BF16), f"vst{i}") for i in range(3)])
        self.cum = B(sb("cum", [128, TT], F32), "cum")
        self.cumstate = B(sb("cumstate", [128, 1], F32), "cumstate")
        self.negcumT = B(sb("negcumT", [128, S // 128, 12], F32), "negcumT")
        self.vtok = B(sb("vtok", [128, 4, MIXW], BF16), "vtok")
        self.ps = Ring([B(self.stack.enter_context(nc.psum_tensor(f"ps{i}", [128, 512], F32)), f"ps{i}")
                        for i in range(4)])
        self.psacc = Ring([B(self.stack.enter_context(nc.psum_tensor(f"pa{i}", [128, 512], F32)), f"pa{i}")
                           for i in range(4)])
        qmn = sb("qmn", [128, 2, TT], BF16)
        self.qmn = [B(qmn[:, c, :], f"qmn{c}") for c in range(2)]
        self.cq = [B(sb(f"cq{i}", [128, TT], F32), f"cq{i}") for i in range(2)]

    def vcol(self, key, off=0, n=1, p0=0, p1=128):
        c = self.vc[key] + off
        return self.vecs.ap[p0:p1, c:c + n]

    def wtile(self, key):
        fw = self.fw
        _, src, L, c0, ncols, k0, nk = self.wspec[key]
        if key[0] in ("rg", "ig"):
            nk, ncols = 1, 128
        n = nk * ncols
        idx = self.widx[key]
        wb = self.wbf.next()
        if key in self.wconv:
            db = self.wconv[key]
            fw.dma("sp", wb.ap[:, 0:n], db.ap, reads=[db], writes=[wb])
        else:
            st = self.wstage.next()
            fw.dma("sp", st.ap[:, 0:n], self.wts_d[idx, :, 0:n], writes=[st])
            eng = "pool" if self.ncast % 2 == 0 else "act"
            self.ncast += 1
            self.copy(eng, wb, st, out_ap=wb.ap[:, 0:n], in_ap=st.ap[:, 0:n])
            if key[0] != "kv" and self.n_tiles > 1:
                db = Buf(self.wbf_d[idx, :, 0:n], "wd")
                self.wconv[key] = db
                fw.dma(eng, db.ap, wb.ap[:, 0:n], reads=[wb], writes=[db])
        return wb, wb.ap[:, 0:n].rearrange("p (k n) -> p k n", k=nk)

    def mm(self, ps, out_ap, lhsT, rhs, start, stop, reads, signal=None):
        self.fw.op("pe", lambda e: e.matmul(out_ap, lhsT, rhs, start=start, stop=stop),
                   reads=reads, writes=[ps], signal=(stop if signal is None else signal))

    def mm1(self, ps, out_ap, lhsT, rhs, start, stop, reads):
        self.fw.op("pe", lambda e: e.matmul(out_ap, lhsT, rhs, start=start, stop=stop),
                   reads=reads, writes=[ps], signal=True)

    def act_(self, func, out_b, in_b, out_ap=None, in_ap=None, extra_reads=(), **kw):
        o = out_b.ap if out_ap is None else out_ap
        i = in_b.ap if in_ap is None else in_ap
        self.fw.op("act", lambda e: e.activation(o, i, func, **kw), reads=[in_b] + list(extra_reads), writes=[out_b])

    def tt(self, eng, out_b, a_b, b_b, op, out_ap=None, a_ap=None, b_ap=None):
        o = out_b.ap if out_ap is None else out_ap
        a = a_b.ap if a_ap is None else a_ap
        b = b_b.ap if b_ap is None else b_ap
        self.fw.op(eng, lambda e: e.tensor_tensor(o, a, b, op), reads=[a_b, b_b], writes=[out_b])

    def ts(self, eng, out_b, in_b, s1, s2, op0, op1, out_ap=None, in_ap=None, extra_reads=()):
        o = out_b.ap if out_ap is None else out_ap
        i = in_b.ap if in_ap is None else in_ap
        if op1 is None:
            fn = lambda e: e.tensor_scalar(o, i, s1, None, op0)
        else:
            fn = lambda e: e.tensor_scalar(o, i, s1, s2, op0, op1)
        self.fw.op(eng, fn, reads=[in_b] + list(extra_reads), writes=[out_b])

    def stt(self, out_b, in0_b, scalar, in1_b, op0, op1, out_ap=None, in0_ap=None, in1_ap=None, extra_reads=()):
        o = out_b.ap if out_ap is None else out_ap
        a = in0_b.ap if in0_ap is None else in0_ap
        b = in1_b.ap if in1_ap is None else in1_ap
        self.fw.op("dve", lambda e: e.scalar_tensor_tensor(o, a, scalar, b, op0, op1),
                   reads=[in0_b, in1_b] + list(extra_reads), writes=[out_b])

    def copy(self, eng, out_b, in_b, out_ap=None, in_ap=None):
        o = out_b.ap if out_ap is None else out_ap
        i = in_b.ap if in_ap is None else in_ap
        if eng == "act":
            fn = lambda e: e.copy(o, i)
        else:
            fn = lambda e: e.tensor_copy(o, i)
        self.fw.op(eng, fn, reads=[in_b], writes=[out_b])

    def rmsnorm_fm(self, src, gkey, dst, n=TT):
        ps = self.ps.next()
        for k in range(8):
            sq = self.tmpb.next()
            self.act_(AF.Square, sq, src[k], out_ap=sq.ap[:, 0:n], in_ap=src[k].ap[:, 0:n])
            self.mm(ps, ps.ap[:, 0:n], self.onesblk.ap[:, 0:128], sq.ap[:, 0:n], k == 0, k == 7, [sq, self.onesblk],
                    signal=True)
        rstd = self.tmpf.next()
        self.act_(AF.Ln, rstd, ps, out_ap=rstd.ap[:, 0:n], in_ap=ps.ap[:, 0:n], bias=self.epsb.ap[:, 0:1],
                  scale=1.0 / D, extra_reads=[self.epsb])
        self.act_(AF.Exp, rstd, rstd, out_ap=rstd.ap[:, 0:n], in_ap=rstd.ap[:, 0:n], scale=-0.5)
        for k in range(8):
            self.stt(dst[k], src[k], self.vcol(gkey, k), rstd, ALU.mult, ALU.mult,
                     out_ap=dst[k].ap[:, 0:n], in0_ap=src[k].ap[:, 0:n], in1_ap=rstd.ap[:, 0:n],
                     extra_reads=[self.vecs])

    def headnorm(self, ps, gkey, out_b, out_ap, n=TT):
        sq = self.tmpb.next()
        self.act_(AF.Square, sq, ps, out_ap=sq.ap[:, 0:n], in_ap=ps.ap[:, 0:n])
        pn = self.ps.next()
        self.mm(pn, pn.ap[:, 0:n], self.onesblk.ap[:, 128:256], sq.ap[:, 0:n], True, True, [sq, self.onesblk])
        rstd = self.tmpf.next()
        self.act_(AF.Ln, rstd, pn, out_ap=rstd.ap[:, 0:n], in_ap=pn.ap[:, 0:n], bias=self.epsb.ap[:, 0:1],
                  scale=1.0 / HD, extra_reads=[self.epsb])
        self.act_(AF.Exp, rstd, rstd, out_ap=rstd.ap[:, 0:n], in_ap=rstd.ap[:, 0:n], scale=-0.5)
        self.stt(out_b, ps, self.vcol(gkey), rstd, ALU.mult, ALU.mult,
                 out_ap=out_ap, in0_ap=ps.ap[:, 0:n], in1_ap=rstd.ap[:, 0:n], extra_reads=[self.vecs])

    def load_x(self, t):
        fw = self.fw
        fw.dma("act", self.xin.ap, self.x_d[t * TT:(t + 1) * TT, :].rearrange("(c p) d -> p c d", p=128),
               writes=[self.xin])
        for k in range(8):
            ps = self.ps.next()
            for tc in range(4):
                o = ps.ap[:, tc * 128:(tc + 1) * 128]
                i = self.xin.ap[:, tc, k * 128:(k + 1) * 128]
                fw.op("pe", lambda e, o=o, i=i: e.transpose(o, i, self.ident.ap),
                      reads=[self.xin, self.ident], writes=[ps], signal=(tc == 3))
            self.copy("act" if k % 2 else "dve", self.hT[k], ps)

    def store_out(self, t):
        fw = self.fw
        for tc in range(4):
            for half in range(2):
                ps = self.ps.next()
                for kk in range(4):
                    k = half * 4 + kk
                    o = ps.ap[:, kk * 128:(kk + 1) * 128]
                    i = self.hT[k].ap[:, tc * 128:(tc + 1) * 128]
                    fw.op("pe", lambda e, o=o, i=i: e.transpose(o, i, self.ident.ap),
                          reads=[self.hT[k], self.ident], writes=[ps], signal=(kk == 3))
                self.copy("act" if half else "dve", self.xin, ps,
                          out_ap=self.xin.ap[:, tc, half * 512:(half + 1) * 512])
        fw.dma("act", self.out_d[t * TT:(t + 1) * TT, :].rearrange("(c p) d -> p c d", p=128), self.xin.ap,
               reads=[self.xin])

    def ffn(self, L, nm, gname):
        fw = self.fw
        self.rmsnorm_fm(self.hT, (gname, L), self.xn)
        for g in range(DFF // 256):
            wg_b, wg = self.wtile((nm + "i", L, "g", g))
            wu_b, wu = self.wtile((nm + "i", L, "u", g))
            for fc in range(2):
                f = g * 2 + fc
                pg = self.ps.next()
                pu = self.ps.next()
                for k in range(8):
                    self.mm(pg, pg.ap, wg[:, k, fc * 128:(fc + 1) * 128], self.xn[k].ap, k == 0, k == 7,
                            [wg_b, self.xn[k]])
                for k in range(8):
                    self.mm(pu, pu.ap, wu[:, k, fc * 128:(fc + 1) * 128], self.xn[k].ap, k == 0, k == 7,
                            [wu_b, self.xn[k]])
                sg = self.tmpf.next()
                self.act_(AF.Silu, sg, pg)
                self.tt("dve", self.act[f], sg, pu, ALU.mult)
        for oc in range(8):
            w0b, w0 = self.wtile((nm + "o", L, oc, 0))
            w1b, w1 = self.wtile((nm + "o", L, oc, 1))
            py = self.ps.next()
            for k in range(FC):
                wb, w = (w0b, w0) if k < 11 else (w1b, w1)
                self.mm(py, py.ap, w[:, k % 11, :], self.act[k].ap, k == 0, k == FC - 1, [wb, self.act[k]])
            self.stt(self.hT[oc], py, 0.5, self.hT[oc], ALU.mult, ALU.add)

    def _prologue(self):
        fw = self.fw
        fw.dma("act", self.ident.ap, self.ident_d, writes=[self.ident])
        fw.dma("act", self.tri.ap, self.tri_d, writes=[self.tri])
        fw.dma("act", self.oh.ap, self.oh_d, writes=[self.oh])
        fw.dma("act", self.vecs.ap, self.vecs_d, writes=[self.vecs])
        st = self.tmpf.next()
        fw.dma("act", st.ap[:, 0:256], self.onesblk_d, writes=[st])
        self.copy("dve", self.onesblk, st, in_ap=st.ap[:, 0:256])
        fw.op("dve", lambda e: e.memset(self.epsb.ap, EPS), writes=[self.epsb])
        fw.op("dve", lambda e: e.memset(self.ones12.ap, 1.0), writes=[self.ones12])
        if "mix0" in self.stages or "mix1" in self.stages:
            self._prologue_mix()

    def _tile(self, t):
        st = self.stages
        self.load_x(t)
        if "ffn1a" in st:
            self.ffn(0, "f1", "ffn1_norm_g")
        if "mix0" in st:
            self.mix_lru(t)
        if "ffn2a" in st:
            self.ffn(0, "f2", "ffn2_norm_g")
        if "ffn1b" in st:
            self.ffn(1, "f1", "ffn1_norm_g")
        if "mix1" in st:
            self.mix_fox(t)
        if "ffn2b" in st:
            self.ffn(1, "f2", "ffn2_norm_g")
        self.store_out(t)

    def _emit(self):
        nc = self.nc
        fw = self.fw
        with nc.Block() as block:
            @block.tensor
            def _(e):
                fw.replay("pe", e)

            @block.scalar
            def _(e):
                fw.replay("act", e)

            @block.vector
            def _(e):
                fw.replay("dve", e)

            @block.gpsimd
            def _(e):
                fw.replay("pool", e)

            @block.sync
            def _(e):
                fw.replay("sp", e)
        self.stack.close()


_CACHE = {}


def host_inputs(inp):
    wts, _ = pack_weights(inp)
    vecs = pack_vecs(inp)
    c = make_consts()
    tri = np.where(np.arange(128)[:, None] <= np.arange(128)[None, :], 0.0, NEG).astype(np.float32)
    oh = np.zeros((128, 16), np.float32)
    for h in range(12):
        oh[h, h] = 1.0
    shared = {"wts": wts, "vecs": vecs, "ident": c["ident"], "ones_blk": c["ones_blk"], "tri": tri, "onehot": oh}
    return shared


def kernel(**inputs):
    inp = {k: np.asarray(v) for k, v in inputs.items()}
    shared = host_inputs(inp)
    if "prog" not in _CACHE:
        _CACHE["prog"] = Prog()
    prog = _CACHE["prog"]
    x = np.ascontiguousarray(inp["x"], dtype=np.float32)
    mem = np.ascontiguousarray(inp["mem"], dtype=np.float32)
    in_maps = []
    for b in range(8):
        m = dict(shared)
        m["x"] = x[b]
        m["mem"] = mem[b]
        in_maps.append(m)
    res = run_bass_kernel_spmd(prog.nc, in_maps, core_ids=list(range(8)))
    out = np.stack([np.asarray(r["out"], dtype=np.float32).reshape(S, D) for r in res.results], axis=0)
    return out


def _add_mixers():
    def _prologue_mix(self):
        fw = self.fw
        t = self.tmpf.next()
        lam = self.vecs.ap[:, self.vc["lam"]:self.vc["lam"] + 6]
        one = self.ones12.ap[:, 0:1]
        fw.op("act", lambda e: e.activation(t.ap[:, 0:6], lam, AF.Exp, scale=-1.0), reads=[self.vecs], writes=[t])
        fw.op("act", lambda e: e.activation(t.ap[:, 0:6], t.ap[:, 0:6], AF.Ln, bias=one), reads=[t, self.ones12], writes=[t])
        self.ts("dve", self.cL, t, -8.0, None, ALU.mult, None, out_ap=self.cL.ap[:, 0:6], in_ap=t.ap[:, 0:6])
        self.ts("dve", self.cL, t, -16.0, None, ALU.mult, None, out_ap=self.cL.ap[:, 6:12], in_ap=t.ap[:, 0:6])
        self.ts("dve", self.negbf, self.vecs, -1.0, None, ALU.mult, None, in_ap=self.vcol("b_f"))
        for c in range(6):
            fw.op("dve", lambda e, a=self.hstate[c].ap: e.memset(a, 0.0), writes=[self.hstate[c]])
            fw.op("dve", lambda e, a=self.xbr[c].ap[:, 0:4]: e.memset(a, 0.0), writes=[self.xbr[c]])
        fw.op("dve", lambda e: e.memset(self.cumstate.ap, 0.0), writes=[self.cumstate])
        fw.dma("act", self.xin.ap[:, 0:2, :], self.mem_d.rearrange("(c p) d -> p c d", p=128), writes=[self.xin])
        for k in range(8):
            ps = self.ps.next()
            for tc in range(2):
                o = ps.ap[:, tc * 128:(tc + 1) * 128]
                i = self.xin.ap[:, tc, k * 128:(k + 1) * 128]
                fw.op("pe", lambda e, o=o, i=i: e.transpose(o, i, self.ident.ap),
                      reads=[self.xin, self.ident], writes=[ps], signal=(tc == 1))
            self.copy("act" if k % 2 else "dve", self.hT[k], ps, out_ap=self.hT[k].ap[:, 0:256], in_ap=ps.ap[:, 0:256])
        self.rmsnorm_fm(self.hT, "mem_norm_g", self.xn, n=NMEM)
        wkb, wk = self.wtile(("kv", 0))
        wvb, wv = self.wtile(("kv", 1))
        for c in range(2):
            ps = self.ps.next()
            for k in range(8):
                self.mm(ps, ps.ap[:, 0:NMEM], wk[:, k, c * 128:(c + 1) * 128], self.xn[k].ap[:, 0:NMEM], k == 0, k == 7,
                        [wkb, self.xn[k]])
            self.headnorm(ps, "mem_k_norm_g", self.mkT, self.mkT.ap[:, c, :], n=NMEM)
        for nch in range(2):
            ps = self.ps.next()
            for k in range(8):
                self.mm(ps, ps.ap[:, 0:256], self.xn[k].ap[:, nch * 128:(nch + 1) * 128], wv[:, k, :], k == 0, k == 7,
                        [wvb, self.xn[k]])
            self.copy("act", self.mv, ps, out_ap=self.mv.ap[:, nch, :], in_ap=ps.ap[:, 0:256])

    def cross_attn(self, L):
        for c in range(2):
            o = self.psacc.next()
            den = self.psacc.next()
            for h2 in range(2):
                h = 2 * c + h2
                b0 = 64 * h2
                for nch in range(2):
                    s = self.ps.next()
                    self.mm1(s, s.ap, self.mkT.ap[b0:b0 + 64, c, nch * 128:(nch + 1) * 128],
                             self.qmn[c].ap[b0:b0 + 64, :], True, True, [self.mkT, self.qmn[c]])
                    e_ = self.tmpb.next()
                    self.act_(AF.Exp, e_, s, scale=0.125)
                    self.mm1(o, o.ap[b0:b0 + 64, :], self.mv.ap[:, nch, h * 64:(h + 1) * 64], e_.ap,
                             nch == 0, nch == 1, [self.mv, e_])
                    self.mm1(den, den.ap[b0:b0 + 64, :], self.onesblk.ap[:, 0:64], e_.ap,
                             nch == 0, nch == 1, [self.onesblk, e_])
            rec = self.tmpf.next()
            self.act_(AF.Ln, rec, den)
            self.act_(AF.Exp, rec, rec, scale=-1.0)
            self.tt("dve", self.cat[6 + c], o, rec, ALU.mult)

    def out_proj(self, L):
        for g in range(4):
            wb, w = self.wtile(("mo", L, g))
            for fc in range(2):
                oc = 2 * g + fc
                ps = self.ps.next()
                for k in range(8):
                    self.mm(ps, ps.ap, w[:, k, fc * 128:(fc + 1) * 128], self.cat[k].ap, k == 0, k == 7,
                            [wb, self.cat[k]])
                self.tt("dve", self.hT[oc], ps, self.hT[oc], ALU.add)

    def mix_lru(self, t):
        fw = self.fw
        self.rmsnorm_fm(self.hT, ("mix_norm_g", 0), self.xn)
        if t > 0:
            for c in range(6):
                b = self.xbr[c]
                fw.op("pool", lambda e, b=b: e.tensor_copy(b.ap[:, 0:3], b.ap[:, TT:TT + 3]), reads=[b], writes=[b])
        for g in range(7):
            wb, w = self.wtile(("mi", 0, g))
            for fc in range(2):
                oc = 2 * g + fc
                ps = self.ps.next()
                for k in range(8):
                    self.mm(ps, ps.ap, w[:, k, fc * 128:(fc + 1) * 128], self.xn[k].ap, k == 0, k == 7, [wb, self.xn[k]])
                if oc < 6:
                    self.copy("act", self.xbr[oc], ps, out_ap=self.xbr[oc].ap[:, 3:TT + 3])
                elif oc < 12:
                    c = oc - 6
                    u = self.tmpf.next()
                    self.act_(AF.Square, u, ps)
                    self.ts("dve", u, u, 0.044715, 1.0, ALU.mult, ALU.add)
                    self.tt("dve", u, u, ps, ALU.mult)
                    self.act_(AF.Sigmoid, u, u, scale=1.5957691216057308)
                    self.tt("dve", self.cat[c], u, ps, ALU.mult)
                else:
                    c = oc - 12
                    self.headnorm(ps, ("memq_norm_g", 0), self.qmn[c], self.qmn[c].ap)
        one = self.ones12.ap[:, 0:1]
        for c in range(6):
            xb = self.xbr[c]
            acc = self.tmpf.next()
            cw = lambda tap: self.vcol("conv_w", tap * 6 + c)
            self.ts("dve", acc, xb, cw(0), self.vcol("conv_b", c), ALU.mult, ALU.add, in_ap=xb.ap[:, 0:TT],
                    extra_reads=[self.vecs])
            for tap in range(1, 4):
                self.stt(acc, xb, cw(tap), acc, ALU.mult, ALU.add, in0_ap=xb.ap[:, tap:tap + TT], extra_reads=[self.vecs])
            xcb = self.tmpb.next()
            self.copy("pool", xcb, acc)
            wrb, wr = self.wtile(("rg", c))
            wib, wi = self.wtile(("ig", c))
            pr = self.ps.next()
            self.mm(pr, pr.ap, wr[:, 0, :], xcb.ap, True, True, [wrb, xcb])
            pi = self.ps.next()
            self.mm(pi, pi.ap, wi[:, 0, :], xcb.ap, True, True, [wib, xcb])
            r = self.tmpf.next()
            self.act_(AF.Sigmoid, r, pr, bias=self.vcol("b_rg", c), extra_reads=[self.vecs])
            gi = self.tmpf.next()
            self.act_(AF.Sigmoid, gi, pi, bias=self.vcol("b_ig", c), extra_reads=[self.vecs])
            a = self.tmpf.next()
            self.act_(AF.Exp, a, r, scale=self.cL.ap[:, c:c + 1], extra_reads=[self.cL])
            m = self.tmpf.next()
            self.act_(AF.Exp, m, r, scale=self.cL.ap[:, 6 + c:7 + c], extra_reads=[self.cL])
            self.act_(AF.Sqrt, m, m, scale=-1.0, bias=one, extra_reads=[self.ones12])
            self.tt("pool", gi, gi, acc, ALU.mult)
            self.tt("dve", gi, gi, m, ALU.mult)
            hs = self.tmpf.next()
            hst = self.hstate[c]
            fw.op("dve", lambda e, hs=hs, a=a, gi=gi, hst=hst: e.tensor_tensor_scan(hs.ap, a.ap, gi.ap, hst.ap, ALU.mult, ALU.add),
                  reads=[a, gi, hst], writes=[hs])
            self.copy("pool", hst, hs, in_ap=hs.ap[:, TT - 1:TT])
            self.tt("dve", self.cat[c], hs, self.cat[c], ALU.mult)
        self.cross_attn(0)
        self.out_proj(0)

    Prog._prologue_mix = _prologue_mix
    Prog.cross_attn = cross_attn
    Prog.out_proj = out_proj
    Prog.mix_lru = mix_lru


_add_mixers()


def _add_fox():
    def mix_fox(self, t):
        fw = self.fw
        self.rmsnorm_fm(self.hT, ("mix_norm_g", 1), self.xn)
        for g in range(6):
            wb, w = self.wtile(("mi", 1, g))
            for fc in range(2):
                oc = 2 * g + fc
                ps = self.ps.next()
                for k in range(8):
                    self.mm(ps, ps.ap, w[:, k, fc * 128:(fc + 1) * 128], self.xn[k].ap, k == 0, k == 7, [wb, self.xn[k]])
                if oc < 6:
                    self.headnorm(ps, "fox_q_g", self.qT[oc], self.qT[oc].ap)
                else:
                    c = oc - 6
                    kn = self.tmpb.next()
                    self.headnorm(ps, "fox_k_g", kn, kn.ap)
                    kb = Buf(self.kc_d[c, :, t * TT:(t + 1) * TT], f"kc{c}_{t}")
                    self.kc_bufs[(c, t)] = kb
                    fw.dma("act", kb.ap, kn.ap, reads=[kn], writes=[kb])
        for g in range(3):
            wb, w = self.wtile(("mv", g))
            for tc in range(4):
                ps = self.ps.next()
                for k in range(8):
                    self.mm(ps, ps.ap[:, 0:256], self.xn[k].ap[:, tc * 128:(tc + 1) * 128], w[:, k, :], k == 0, k == 7,
                            [wb, self.xn[k]])
                self.copy("act" if tc % 2 else "dve", self.vtok, ps,
                          out_ap=self.vtok.ap[:, tc, g * 256:(g + 1) * 256], in_ap=ps.ap[:, 0:256])
        for c in range(6):
            vb = Buf(self.vc_d[c, :, 4 * t:4 * t + 4, :], f"vc{c}_{t}")
            self.vc_bufs[(c, t)] = vb
            fw.dma("act", vb.ap, self.vtok.ap[:, :, c * 128:(c + 1) * 128], reads=[self.vtok], writes=[vb])
        wb, w = self.wtile(("mf",))
        ps = self.ps.next()
        for k in range(8):
            self.mm(ps, ps.ap[0:12, :], w[:, k, 0:12], self.xn[k].ap, k == 0, k == 7, [wb, self.xn[k]])
        lf = self.tmpf.next()
        self.act_(AF.Exp, lf, ps, out_ap=lf.ap[0:12, :], in_ap=ps.ap[0:12, :], scale=-1.0, bias=self.negbf.ap[0:12, :],
                  extra_reads=[self.negbf])
        self.act_(AF.Ln, lf, lf, out_ap=lf.ap[0:12, :], in_ap=lf.ap[0:12, :], bias=self.ones12.ap[0:12, 0:1],
                  extra_reads=[self.ones12])
        fw.op("dve", lambda e, lf=lf: e.tensor_tensor_scan(self.cum.ap[0:12, :], self.ones12.ap[0:12, :], lf.ap[0:12, :],
                                                            self.cumstate.ap[0:12, :], ALU.mult, ALU.subtract),
              reads=[lf, self.ones12, self.cumstate], writes=[self.cum])
        self.copy("pool", self.cumstate, self.cum, out_ap=self.cumstate.ap[0:12, :], in_ap=self.cum.ap[0:12, TT - 1:TT])
        for tc in range(4):
            tp = self.ps.next()
            fw.op("pe", lambda e, tp=tp, tc=tc: e.transpose(tp.ap[:, 0:12], self.cum.ap[0:12, tc * 128:(tc + 1) * 128],
                                                           self.ident.ap[0:12, 0:12]),
                  reads=[self.cum, self.ident], writes=[tp])
            self.ts("dve", self.negcumT, tp, -1.0, None, ALU.mult, None,
                    out_ap=self.negcumT.ap[:, 4 * t + tc, :], in_ap=tp.ap[:, 0:12])
        wb, w = self.wtile(("mq",))
        for c in range(2):
            ps = self.ps.next()
            for k in range(8):
                self.mm(ps, ps.ap, w[:, k, c * 128:(c + 1) * 128], self.xn[k].ap, k == 0, k == 7, [wb, self.xn[k]])
            self.headnorm(ps, ("memq_norm_g", 1), self.qmn[c], self.qmn[c].ap)
        nkeys = (t + 1) * TT
        for c in range(6):
            o = self.psacc.next()
            den = self.psacc.next()
            for h2 in range(2):
                h = 2 * c + h2
                sel = self.tmpf.next()
                self.ts("dve", sel, self.cum, self.oh.ap[0:12, h:h + 1], None, ALU.mult, None,
                        out_ap=sel.ap[0:12, :], in_ap=self.cum.ap[0:12, :], extra_reads=[self.oh])
                pb = self.ps.next()
                self.mm1(pb, pb.ap, self.ones12.ap[0:12, 0:128], sel.ap[0:12, :], True, True, [self.ones12, sel])
                self.copy("act", self.cq[h2], pb)
            for kb0 in range(0, nkeys, 1024):
                wk = min(1024, nkeys - kb0)
                kbuf = self.kst.next()
                vbuf = self.vst.next()
                tiles = range(kb0 // TT, (kb0 + wk) // TT)
                fw.dma("sp", kbuf.ap[:, 0:wk], self.kc_d[c, :, kb0:kb0 + wk],
                       reads=[self.kc_bufs[(c, tt_)] for tt_ in tiles], writes=[kbuf])
                fw.dma("sp", vbuf.ap[:, 0:wk // 128, :], self.vc_d[c, :, kb0 // 128:(kb0 + wk) // 128, :],
                       reads=[self.vc_bufs[(c, tt_)] for tt_ in tiles], writes=[vbuf])
                for h2 in range(2):
                    h = 2 * c + h2
                    b0 = 64 * h2
                    for k8 in range(wk // 128):
                        kcg = kb0 // 128 + k8
                        diag = kcg - 4 * t
                        q0 = 128 * diag if diag >= 0 else 0
                        s = self.ps.next()
                        self.mm1(s, s.ap[:, q0:TT], kbuf.ap[b0:b0 + 64, k8 * 128:(k8 + 1) * 128],
                                 self.qT[c].ap[b0:b0 + 64, q0:TT], True, True, [kbuf, self.qT[c]])
                        tm = self.tmpf.next()
                        self.stt(tm, s, 0.125, self.cq[h2], ALU.mult, ALU.add, out_ap=tm.ap[:, q0:TT],
                                 in0_ap=s.ap[:, q0:TT], in1_ap=self.cq[h2].ap[:, q0:TT])
                        if diag >= 0:
                            self.tt("pool", tm, tm, self.tri, ALU.add, out_ap=tm.ap[:, q0:q0 + 128],
                                    a_ap=tm.ap[:, q0:q0 + 128])
                        p = self.tmpb.next()
                        self.act_(AF.Exp, p, tm, out_ap=p.ap[:, q0:TT], in_ap=tm.ap[:, q0:TT],
                                  bias=self.negcumT.ap[:, kcg, h:h + 1], extra_reads=[self.negcumT])
                        first = kcg == 0
                        last = kcg == 4 * t + 3
                        self.mm1(o, o.ap[b0:b0 + 64, q0:TT], vbuf.ap[:, k8, b0:b0 + 64], p.ap[:, q0:TT], first, last,
                                 [vbuf, p])
                        self.mm1(den, den.ap[b0:b0 + 64, q0:TT], self.onesblk.ap[:, 0:64], p.ap[:, q0:TT], first, last,
                                 [self.onesblk, p])
            rec = self.tmpf.next()
            self.act_(AF.Ln, rec, den)
            self.act_(AF.Exp, rec, rec, scale=-1.0)
            self.tt("dve", self.cat[c], o, rec, ALU.mult)
        self.cross_attn(1)
        self.out_proj(1)

    Prog.mix_fox = mix_fox


_add_fox()
```

```python
import contextlib
import numpy as np
import concourse.bass as bass
import concourse.mybir as mybir
from concourse.bass_utils import run_bass_kernel_spmd

F32 = mybir.dt.float32
BF16 = mybir.dt.bfloat16
AF = mybir.ActivationFunctionType
ALU = mybir.AluOpType

D = 1024
S = 4096
DFF = 2816
NMEM = 256
HD = 64
MIXW = 768
KC = D // 128
FC = DFF // 128
EPS = 1e-6
SYNC_SAME = True


class Buf:
    __slots__ = ("ap", "lw", "rd", "name")

    def __init__(self, ap, name=""):
        self.ap = ap if isinstance(ap, bass.AP) else ap[:]
        self.lw = None
        self.rd = {}
        self.name = name


class Q:
    def __init__(self, name, sem, is_pe=False):
        self.name = name
        self.sem = sem
        self.is_pe = is_pe
        self.count = 0
        self.prog = []
        self.waited = {}
        self.dma_sems = []
        self.dma_vals = []
        self.dma_next = 0


class FW:
    def __init__(self, nc, stack, n_dma_sems=8):
        self.nc = nc
        self.q = {}
        for name, is_pe in (("pe", True), ("act", False), ("dve", False), ("pool", False), ("sp", False)):
            sem = stack.enter_context(nc.semaphore(f"prog_{name}"))
            self.q[name] = Q(name, sem, is_pe)
        for name in ("sp", "act", "pool"):
            q = self.q[name]
            for i in range(n_dma_sems):
                q.dma_sems.append(stack.enter_context(nc.semaphore(f"dma_{name}_{i}")))
                q.dma_vals.append(0)

    def _deps(self, q, reads, writes, extra=()):
        deps = {}

        def need(tok):
            if tok is None:
                return
            k = id(tok[0])
            if k not in deps or deps[k][1] < tok[1]:
                deps[k] = tok

        for b in reads:
            need(b.lw)
        for b in writes:
            need(b.lw)
            for tok in b.rd.values():
                need(tok)
        for tok in extra:
            need(tok)
        for k, (sem, val, owner) in deps.items():
            if owner is q and (q.is_pe or not SYNC_SAME):
                continue
            if q.waited.get(k, 0) >= val:
                continue
            q.waited[k] = val
            q.prog.append(("wait", sem, val))

    def op(self, qname, fn, reads=(), writes=(), signal=True):
        q = self.q[qname]
        self._deps(q, reads, writes)
        if signal:
            q.count += 1
            tok = (q.sem, q.count, q)
        else:
            tok = (q.sem, q.count + 1, q)
        q.prog.append(("op", fn, signal))
        for b in writes:
            b.lw = tok
            b.rd = {}
        for b in reads:
            b.rd[q.name] = tok

    def dma(self, qname, out_ap, in_ap, reads=(), writes=()):
        q = self.q[qname]
        i = q.dma_next
        q.dma_next = (i + 1) % len(q.dma_sems)
        sem = q.dma_sems[i]
        cur = q.dma_vals[i]
        extra = [(sem, cur, None)] if cur > 0 else []
        self._deps(q, reads, writes, extra)
        q.dma_vals[i] = cur + 16
        tok = (sem, cur + 16, None)
        q.prog.append(("dma", out_ap, in_ap, sem))
        for b in writes:
            b.lw = tok
            b.rd = {}
        for b in reads:
            b.rd[("dma", id(sem))] = tok

    def finish(self):
        for name in ("sp", "act", "pool"):
            q = self.q[name]
            for sem, val in zip(q.dma_sems, q.dma_vals):
                if val > 0 and q.waited.get(id(sem), 0) < val:
                    q.prog.append(("wait", sem, val))

    def replay(self, qname, eng):
        q = self.q[qname]
        for item in q.prog:
            if item[0] == "wait":
                eng.wait_ge(item[1], item[2])
            elif item[0] == "op":
                inst = item[1](eng)
                if item[2]:
                    inst.then_inc(q.sem, 1)
            else:
                eng.dma_start(out=item[1], in_=item[2]).then_inc(item[3], 16)


class Ring:
    def __init__(self, bufs):
        self.bufs = bufs
        self.i = 0

    def next(self):
        b = self.bufs[self.i]
        self.i = (self.i + 1) % len(self.bufs)
        return b


WT_ELEMS = 2048


def _tiles_of(W, col_ranges, nk_split):
    out = []
    Kin = W.shape[0]
    kcs = Kin // 128
    Wr = W.reshape(kcs, 128, W.shape[1])
    for (c0, nc_) in col_ranges:
        k0 = 0
        for nk in nk_split:
            t = Wr[k0:k0 + nk, :, c0:c0 + nc_]
            t = np.transpose(t, (1, 0, 2)).reshape(128, nk * nc_)
            out.append(t)
            k0 += nk
        assert k0 == kcs
    return out


class WeightPlan:
    def __init__(self):
        self.tiles = []
        self.arrays = []

    def add(self, arrs):
        idx0 = len(self.arrays)
        self.arrays.extend(arrs)
        return list(range(idx0, idx0 + len(arrs)))

    def pack(self):
        out = np.zeros((len(self.arrays), 128, WT_ELEMS), np.float32)
        for i, a in enumerate(self.arrays):
            out[i, :, :a.shape[1]] = a
        return out


def weight_specs():
    sp = []
    for L in range(2):
        for nm, src_in, src_out in (("f1", "ffn1_w_in", "ffn1_w_out"), ("f2", "ffn2_w_in", "ffn2_w_out")):
            for g in range(DFF // 256):
                sp.append(((nm + "i", L, "g", g), src_in, L, g * 256, 256, 0, 8))
                sp.append(((nm + "i", L, "u", g), src_in, L, DFF + g * 256, 256, 0, 8))
            for oc in range(8):
                sp.append(((nm + "o", L, oc, 0), src_out, L, oc * 128, 128, 0, 11))
                sp.append(((nm + "o", L, oc, 1), src_out, L, oc * 128, 128, 11, 11))
        for g in range(4):
            sp.append((("mo", L, g), "mix_w_out", L, g * 256, 256, 0, 8))
    for g in range(7):
        sp.append((("mi", 0, g), "lru_w_in", 0, g * 256, 256, 0, 8))
    for g in range(6):
        sp.append((("mi", 1, g), "fox_w_in", 0, g * 256, 256, 0, 8))
    for g in range(3):
        sp.append((("mv", g), "fox_w_in", 0, 1536 + g * 256, 256, 0, 8))
    sp.append((("mf",), "fox_w_in", 0, 2304, 12, 0, 8))
    sp.append((("mq",), "fox_w_in", 0, 2316, 256, 0, 8))
    for g in range(2):
        sp.append((("kv", g), "mem_w_kv", None, g * 256, 256, 0, 8))
    for c in range(6):
        sp.append((("rg", c), "lru_w_rg", 0, c, 0, 0, 1))
        sp.append((("ig", c), "lru_w_ig", 0, c, 0, 0, 1))
    return sp


def pack_weights(inp):
    sp = weight_specs()
    out = np.zeros((len(sp), 128, WT_ELEMS), np.float32)
    index = {}
    for i, (key, src, L, c0, ncols, k0, nk) in enumerate(sp):
        index[key] = i
        W = inp[src]
        if key[0] in ("rg", "ig"):
            blk = W[0]
            c = c0
            out[i, 0:64, 0:64] = blk[2 * c]
            out[i, 64:128, 64:128] = blk[2 * c + 1]
            continue
        if L is not None:
            W = W[L]
        Wr = W.reshape(W.shape[0] // 128, 128, W.shape[1])
        t = Wr[k0:k0 + nk, :, c0:c0 + ncols]
        out[i, :, :nk * ncols] = np.transpose(t, (1, 0, 2)).reshape(128, nk * ncols)
    return out, index


def vec_cols():
    cols = {}
    n = 0
    for L in range(2):
        for nm in ("ffn1_norm_g", "mix_norm_g", "ffn2_norm_g"):
            cols[(nm, L)] = n
            n += 8
    cols["mem_norm_g"] = n; n += 8
    cols["mem_k_norm_g"] = n; n += 1
    cols[("memq_norm_g", 0)] = n; n += 1
    cols[("memq_norm_g", 1)] = n; n += 1
    cols["conv_w"] = n; n += 24
    cols["conv_b"] = n; n += 6
    cols["b_rg"] = n; n += 6
    cols["b_ig"] = n; n += 6
    cols["lam"] = n; n += 6
    cols["b_f"] = n; n += 1
    cols["fox_q_g"] = n; n += 1
    cols["fox_k_g"] = n; n += 1
    cols["_n"] = n
    return cols


def pack_vecs(inp):
    cols = vec_cols()
    v = np.zeros((128, cols["_n"]), np.float32)

    def fm(a):
        return a.reshape(-1, 128).T

    for L in range(2):
        for nm in ("ffn1_norm_g", "mix_norm_g", "ffn2_norm_g"):
            v[:, cols[(nm, L)]:cols[(nm, L)] + 8] = fm(inp[nm][L])
    v[:, cols["mem_norm_g"]:cols["mem_norm_g"] + 8] = fm(inp["mem_norm_g"])
    rep = lambda a: np.concatenate([a, a])
    v[:, cols["mem_k_norm_g"]] = rep(inp["mem_k_norm_g"])
    for L in range(2):
        v[:, cols[("memq_norm_g", L)]] = rep(inp["memq_norm_g"][L])
    for tap in range(4):
        v[:, cols["conv_w"] + tap * 6: cols["conv_w"] + tap * 6 + 6] = fm(inp["lru_conv_w"][0, tap])
    v[:, cols["conv_b"]:cols["conv_b"] + 6] = fm(inp["lru_conv_b"][0])
    v[:, cols["b_rg"]:cols["b_rg"] + 6] = fm(inp["lru_b_rg"][0])
    v[:, cols["b_ig"]:cols["b_ig"] + 6] = fm(inp["lru_b_ig"][0])
    v[:, cols["lam"]:cols["lam"] + 6] = fm(inp["lru_lambda"][0])
    v[0:12, cols["b_f"]] = inp["fox_b_f"][0]
    v[:, cols["fox_q_g"]] = rep(inp["fox_q_norm_g"][0])
    v[:, cols["fox_k_g"]] = rep(inp["fox_k_norm_g"][0])
    return v


NEG = -30000.0


def make_consts():
    c = {}
    c["ident"] = np.eye(128, dtype=np.float32)
    ones = np.ones((128, 128), np.float32)
    blk = np.zeros((128, 128), np.float32)
    blk[0:64, 0:64] = 1.0
    blk[64:128, 64:128] = 1.0
    c["ones_blk"] = np.concatenate([ones, blk], axis=1)
    md = np.zeros((128, 4, 512), np.float32)
    for kk in range(4):
        key = kk * 128 + np.arange(128)[:, None]
        qry = np.arange(512)[None, :]
        md[:, kk, :] = np.where(key <= qry, 0.0, NEG)
    c["maskd"] = md.reshape(128, 2048)
    sel = np.zeros((128, 12, 128), np.float32)
    for h in range(12):
        sel[h, h, :] = 1.0
    c["sel"] = sel.reshape(128, 12 * 128)
    return c


TT = 512
NCONST_VEC = None


class Prog:
    def __init__(self, n_tiles=S // TT, stages=("ffn1a", "mix0", "ffn2a", "ffn1b", "mix1", "ffn2b"), dbg=False):
        self.n_tiles = n_tiles
        self.stages = stages
        self.dbg = dbg
        self.stack = contextlib.ExitStack()
        nc = self.nc = bass.Bass("TRN2", target_bir_lowering=False)
        self.fw = FW(nc, self.stack)
        self.vc = vec_cols()
        self.widx = {k[0]: i for i, k in enumerate(weight_specs())}
        self.wspec = {k[0]: k for k in weight_specs()}
        nW = len(self.widx)
        dt = nc.dram_tensor
        self.x_d = dt("x", [S, D], F32, kind="ExternalInput").ap()
        self.mem_d = dt("mem", [NMEM, D], F32, kind="ExternalInput").ap()
        self.wts_d = dt("wts", [nW, 128, WT_ELEMS], F32, kind="ExternalInput").ap()
        self.vecs_d = dt("vecs", [128, self.vc["_n"]], F32, kind="ExternalInput").ap()
        self.ident_d = dt("ident", [128, 128], F32, kind="ExternalInput").ap()
        self.onesblk_d = dt("ones_blk", [128, 256], F32, kind="ExternalInput").ap()
        self.tri_d = dt("tri", [128, 128], F32, kind="ExternalInput").ap()
        self.oh_d = dt("onehot", [128, 16], F32, kind="ExternalInput").ap()
        self.out_d = dt("out", [S, D], F32, kind="ExternalOutput").ap()
        self.kc_d = dt("kcache", [6, 128, S], BF16, kind="Internal").ap()
        self.vc_d = dt("vcache", [6, 128, S // 128, 128], BF16, kind="Internal").ap()
        self.wbf_d = dt("wbf16", [nW, 128, WT_ELEMS], BF16, kind="Internal").ap()
        self.wconv = {}
        self.ncast = 0
        if dbg:
            self.dbg_d = dt("dbg", [128, 8 * 512], F32, kind="ExternalOutput").ap()
        self.kc_bufs = {}
        self.vc_bufs = {}
        self._alloc()
        self._prologue()
        for t in range(n_tiles):
            self._tile(t)
        self.fw.finish()
        self._emit()

    def sb(self, name, shape, dtype):
        return self.stack.enter_context(self.nc.sbuf_tensor("sb_" + name, shape, dtype))

    def _alloc(self):
        nc = self.nc
        sb = self.sb
        B = Buf
        self.ident = B(sb("ident", [128, 128], F32))
        self.onesblk = B(sb("onesblk", [128, 256], BF16))
        self.tri = B(sb("tri", [128, 128], F32))
        self.oh = B(sb("oh", [128, 16], F32))
        self.ones12 = B(sb("ones12", [128, 512], F32))
        self.vecs = B(sb("vecs", [128, self.vc["_n"]], F32))
        self.cL = B(sb("cL", [128, 12], F32))
        self.negbf = B(sb("negbf", [128, 1], F32))
        self.epsb = B(sb("epsb", [128, 1], F32))
        hT = sb("hT", [128, 8, TT], F32)
        self.hT = [B(hT[:, k, :], f"hT{k}") for k in range(8)]
        xn = sb("xn", [128, 8, TT], BF16)
        self.xn = [B(xn[:, k, :], f"xn{k}") for k in range(8)]
        act = sb("act", [128, FC, TT], BF16)
        self.act = [B(act[:, k, :], f"act{k}") for k in range(FC)]
        cat = sb("cat", [128, 8, TT], BF16)
        self.cat = [B(cat[:, k, :], f"cat{k}") for k in range(8)]
        self.wstage = Ring([B(sb(f"wst{i}", [128, WT_ELEMS], F32), f"wst{i}") for i in range(2)])
        self.wbf = Ring([B(sb(f"wbf{i}", [128, WT_ELEMS], BF16), f"wbf{i}") for i in range(6)])
        self.xin = B(sb("xin", [128, 4, D], F32), "xin")
        self.yout = B(sb("yout", [128, 4, D], F32), "yout") if False else self.xin
        self.mkT = B(sb("mkT", [128, 2, NMEM], BF16), "mkT")
        self.mv = B(sb("mv", [128, 2, 256], BF16), "mv")
        self.tmpf = Ring([B(sb(f"tmpf{i}", [128, TT], F32), f"tmpf{i}") for i in range(6)])
        self.tmpb = Ring([B(sb(f"tmpb{i}", [128, TT], BF16), f"tmpb{i}") for i in range(4)])
        xbr = sb("xbr", [128, 6, TT + 4], F32)
        self.xbr_t = xbr
        self.xbr = [B(xbr[:, c, :], f"xbr{c}") for c in range(6)]
        self.hstate = [B(sb(f"hstate{c}", [128, 1], F32), f"hstate{c}") for c in range(6)]
        qT = sb("qT", [128, 6, TT], BF16)
        self.qT = [B(qT[:, c, :], f"qT{c}") for c in range(6)]
        self.kst = Ring([B(sb(f"kst{i}", [128, 1024], BF16), f"kst{i}") for i in range(3)])
        self.vst = Ring([B(sb(f"vst{i}", [128, 8, 128], BF16), f"vst{i}") for i in range(3)])
        self.cum = B(sb("cum", [128, TT], F32), "cum")
        self.cumstate = B(sb("cumstate", [128, 1], F32), "cumstate")
        self.negcumT = B(sb("negcumT", [128, S // 128, 12], F32), "negcumT")
        self.vtok = B(sb("vtok", [128, 4, MIXW], BF16), "vtok")
        self.ps = Ring([B(self.stack.enter_context(nc.psum_tensor(f"ps{i}", [128, 512], F32)), f"ps{i}")
                        for i in range(4)])
        self.psacc = Ring([B(self.stack.enter_context(nc.psum_tensor(f"pa{i}", [128, 512], F32)), f"pa{i}")
                           for i in range(4)])
        qmn = sb("qmn", [128, 2, TT], BF16)
        self.qmn = [B(qmn[:, c, :], f"qmn{c}") for c in range(2)]
        self.cq = [B(sb(f"cq{i}", [128, TT], F32), f"cq{i}") for i in range(2)]

    def vcol(self, key, off=0, n=1, p0=0, p1=128):
        c = self.vc[key] + off
        return self.vecs.ap[p0:p1, c:c + n]

    def wtile(self, key):
        fw = self.fw
        _, src, L, c0, ncols, k0, nk = self.wspec[key]
        if key[0] in ("rg", "ig"):
            nk, ncols = 1, 128
        n = nk * ncols
        idx = self.widx[key]
        wb = self.wbf.next()
        if key in self.wconv:
            db = self.wconv[key]
            fw.dma("sp", wb.ap[:, 0:n], db.ap, reads=[db], writes=[wb])
        else:
            st = self.wstage.next()
            fw.dma("sp", st.ap[:, 0:n], self.wts_d[idx, :, 0:n], writes=[st])
            eng = "act"
            self.ncast += 1
            self.copy(eng, wb, st, out_ap=wb.ap[:, 0:n], in_ap=st.ap[:, 0:n])
            if key[0] != "kv" and self.n_tiles > 1:
                db = Buf(self.wbf_d[idx, :, 0:n], "wd")
                self.wconv[key] = db
                fw.dma(eng, db.ap, wb.ap[:, 0:n], reads=[wb], writes=[db])
        return wb, wb.ap[:, 0:n].rearrange("p (k n) -> p k n", k=nk)

    def mm(self, ps, out_ap, lhsT, rhs, start, stop, reads, signal=None):
        self.fw.op("pe", lambda e: e.matmul(out_ap, lhsT, rhs, start=start, stop=stop),
                   reads=reads, writes=[ps], signal=(stop if signal is None else signal))

    def mm1(self, ps, out_ap, lhsT, rhs, start, stop, reads):
        self.fw.op("pe", lambda e: e.matmul(out_ap, lhsT, rhs, start=start, stop=stop),
                   reads=reads, writes=[ps], signal=True)

    def act_(self, func, out_b, in_b, out_ap=None, in_ap=None, extra_reads=(), **kw):
        o = out_b.ap if out_ap is None else out_ap
        i = in_b.ap if in_ap is None else in_ap
        self.fw.op("act", lambda e: e.activation(o, i, func, **kw), reads=[in_b] + list(extra_reads), writes=[out_b])

    def tt(self, eng, out_b, a_b, b_b, op, out_ap=None, a_ap=None, b_ap=None):
        o = out_b.ap if out_ap is None else out_ap
        a = a_b.ap if a_ap is None else a_ap
        b = b_b.ap if b_ap is None else b_ap
        self.fw.op(eng, lambda e: e.tensor_tensor(o, a, b, op), reads=[a_b, b_b], writes=[out_b])

    def ts(self, eng, out_b, in_b, s1, s2, op0, op1, out_ap=None, in_ap=None, extra_reads=()):
        o = out_b.ap if out_ap is None else out_ap
        i = in_b.ap if in_ap is None else in_ap
        if op1 is None:
            fn = lambda e: e.tensor_scalar(o, i, s1, None, op0)
        else:
            fn = lambda e: e.tensor_scalar(o, i, s1, s2, op0, op1)
        self.fw.op(eng, fn, reads=[in_b] + list(extra_reads), writes=[out_b])

    def stt(self, out_b, in0_b, scalar, in1_b, op0, op1, out_ap=None, in0_ap=None, in1_ap=None, extra_reads=()):
        o = out_b.ap if out_ap is None else out_ap
        a = in0_b.ap if in0_ap is None else in0_ap
        b = in1_b.ap if in1_ap is None else in1_ap
        self.fw.op("dve", lambda e: e.scalar_tensor_tensor(o, a, scalar, b, op0, op1),
                   reads=[in0_b, in1_b] + list(extra_reads), writes=[out_b])

    def copy(self, eng, out_b, in_b, out_ap=None, in_ap=None):
        o = out_b.ap if out_ap is None else out_ap
        i = in_b.ap if in_ap is None else in_ap
        if eng == "act":
            fn = lambda e: e.copy(o, i)
        else:
            fn = lambda e: e.tensor_copy(o, i)
        self.fw.op(eng, fn, reads=[in_b], writes=[out_b])

    def rmsnorm_fm(self, src, gkey, dst, n=TT):
        ps = self.ps.next()
        for k in range(8):
            sq = self.tmpb.next()
            self.act_(AF.Square, sq, src[k], out_ap=sq.ap[:, 0:n], in_ap=src[k].ap[:, 0:n])
            self.mm(ps, ps.ap[:, 0:n], self.onesblk.ap[:, 0:128], sq.ap[:, 0:n], k == 0, k == 7, [sq, self.onesblk],
                    signal=True)
        rstd = self.tmpf.next()
        self.act_(AF.Ln, rstd, ps, out_ap=rstd.ap[:, 0:n], in_ap=ps.ap[:, 0:n], bias=self.epsb.ap[:, 0:1],
                  scale=1.0 / D, extra_reads=[self.epsb])
        self.act_(AF.Exp, rstd, rstd, out_ap=rstd.ap[:, 0:n], in_ap=rstd.ap[:, 0:n], scale=-0.5)
        for k in range(8):
            self.stt(dst[k], src[k], self.vcol(gkey, k), rstd, ALU.mult, ALU.mult,
                     out_ap=dst[k].ap[:, 0:n], in0_ap=src[k].ap[:, 0:n], in1_ap=rstd.ap[:, 0:n],
                     extra_reads=[self.vecs])

    def headnorm(self, ps, gkey, out_b, out_ap, n=TT):
        sq = self.tmpb.next()
        self.act_(AF.Square, sq, ps, out_ap=sq.ap[:, 0:n], in_ap=ps.ap[:, 0:n])
        pn = self.ps.next()
        self.mm(pn, pn.ap[:, 0:n], self.onesblk.ap[:, 128:256], sq.ap[:, 0:n], True, True, [sq, self.onesblk])
        rstd = self.tmpf.next()
        self.act_(AF.Ln, rstd, pn, out_ap=rstd.ap[:, 0:n], in_ap=pn.ap[:, 0:n], bias=self.epsb.ap[:, 0:1],
                  scale=1.0 / HD, extra_reads=[self.epsb])
        self.act_(AF.Exp, rstd, rstd, out_ap=rstd.ap[:, 0:n], in_ap=rstd.ap[:, 0:n], scale=-0.5)
        self.stt(out_b, ps, self.vcol(gkey), rstd, ALU.mult, ALU.mult,
                 out_ap=out_ap, in0_ap=ps.ap[:, 0:n], in1_ap=rstd.ap[:, 0:n], extra_reads=[self.vecs])

    def load_x(self, t):
        fw = self.fw
        fw.dma("act", self.xin.ap, self.x_d[t * TT:(t + 1) * TT, :].rearrange("(c p) d -> p c d", p=128),
               writes=[self.xin])
        for k in range(8):
            ps = self.ps.next()
            for tc in range(4):
                o = ps.ap[:, tc * 128:(tc + 1) * 128]
                i = self.xin.ap[:, tc, k * 128:(k + 1) * 128]
                fw.op("pe", lambda e, o=o, i=i: e.transpose(o, i, self.ident.ap),
                      reads=[self.xin, self.ident], writes=[ps], signal=(tc == 3))
            self.copy("act" if k % 2 else "dve", self.hT[k], ps)

    def store_out(self, t):
        fw = self.fw
        for tc in range(4):
            for half in range(2):
                ps = self.ps.next()
                for kk in range(4):
                    k = half * 4 + kk
                    o = ps.ap[:, kk * 128:(kk + 1) * 128]
                    i = self.hT[k].ap[:, tc * 128:(tc + 1) * 128]
                    fw.op("pe", lambda e, o=o, i=i: e.transpose(o, i, self.ident.ap),
                          reads=[self.hT[k], self.ident], writes=[ps], signal=(kk == 3))
                self.copy("act" if half else "dve", self.xin, ps,
                          out_ap=self.xin.ap[:, tc, half * 512:(half + 1) * 512])
        fw.dma("act", self.out_d[t * TT:(t + 1) * TT, :].rearrange("(c p) d -> p c d", p=128), self.xin.ap,
               reads=[self.xin])

    def ffn(self, L, nm, gname):
        fw = self.fw
        self.rmsnorm_fm(self.hT, (gname, L), self.xn)
        for g in range(DFF // 256):
            wg_b, wg = self.wtile((nm + "i", L, "g", g))
            wu_b, wu = self.wtile((nm + "i", L, "u", g))
            for fc in range(2):
                f = g * 2 + fc
                pg = self.ps.next()
                pu = self.ps.next()
                for k in range(8):
                    self.mm(pg, pg.ap, wg[:, k, fc * 128:(fc + 1) * 128], self.xn[k].ap, k == 0, k == 7,
                            [wg_b, self.xn[k]])
                for k in range(8):
                    self.mm(pu, pu.ap, wu[:, k, fc * 128:(fc + 1) * 128], self.xn[k].ap, k == 0, k == 7,
                            [wu_b, self.xn[k]])
                sg = self.tmpf.next()
                self.act_(AF.Silu, sg, pg)
                self.tt("dve", self.act[f], sg, pu, ALU.mult)
        for oc in range(8):
            w0b, w0 = self.wtile((nm + "o", L, oc, 0))
            w1b, w1 = self.wtile((nm + "o", L, oc, 1))
            py = self.ps.next()
            for k in range(FC):
                wb, w = (w0b, w0) if k < 11 else (w1b, w1)
                self.mm(py, py.ap, w[:, k % 11, :], self.act[k].ap, k == 0, k == FC - 1, [wb, self.act[k]])
            self.stt(self.hT[oc], py, 0.5, self.hT[oc], ALU.mult, ALU.add)

    def _prologue(self):
        fw = self.fw
        fw.dma("act", self.ident.ap, self.ident_d, writes=[self.ident])
        fw.dma("act", self.tri.ap, self.tri_d, writes=[self.tri])
        fw.dma("act", self.oh.ap, self.oh_d, writes=[self.oh])
        fw.dma("act", self.vecs.ap, self.vecs_d, writes=[self.vecs])
        st = self.tmpf.next()
        fw.dma("act", st.ap[:, 0:256], self.onesblk_d, writes=[st])
        self.copy("dve", self.onesblk, st, in_ap=st.ap[:, 0:256])
        fw.op("dve", lambda e: e.memset(self.epsb.ap, EPS), writes=[self.epsb])
        fw.op("dve", lambda e: e.memset(self.ones12.ap, 1.0), writes=[self.ones12])
        if "mix0" in self.stages or "mix1" in self.stages:
            self._prologue_mix()

    def _tile(self, t):
        st = self.stages
        self.load_x(t)
        if "ffn1a" in st:
            self.ffn(0, "f1", "ffn1_norm_g")
        if "mix0" in st:
            self.mix_lru(t)
        if "ffn2a" in st:
            self.ffn(0, "f2", "ffn2_norm_g")
        if "ffn1b" in st:
            self.ffn(1, "f1", "ffn1_norm_g")
        if "mix1" in st:
            self.mix_fox(t)
        if "ffn2b" in st:
            self.ffn(1, "f2", "ffn2_norm_g")
        self.store_out(t)

    def _emit(self):
        nc = self.nc
        fw = self.fw
        with nc.Block() as block:
            @block.tensor
            def _(e):
                fw.replay("pe", e)

            @block.scalar
            def _(e):
                fw.replay("act", e)

            @block.vector
            def _(e):
                fw.replay("dve", e)

            @block.gpsimd
            def _(e):
                fw.replay("pool", e)

            @block.sync
            def _(e):
                fw.replay("sp", e)
        self.stack.close()


_CACHE = {}


def host_inputs(inp):
    wts, _ = pack_weights(inp)
    vecs = pack_vecs(inp)
    c = make_consts()
    tri = np.where(np.arange(128)[:, None] <= np.arange(128)[None, :], 0.0, NEG).astype(np.float32)
    oh = np.zeros((128, 16), np.float32)
    for h in range(12):
        oh[h, h] = 1.0
    shared = {"wts": wts, "vecs": vecs, "ident": c["ident"], "ones_blk": c["ones_blk"], "tri": tri, "onehot": oh}
    return shared


def kernel(**inputs):
    inp = {k: np.asarray(v) for k, v in inputs.items()}
    shared = host_inputs(inp)
    if "prog" not in _CACHE:
        _CACHE["prog"] = Prog()
    prog = _CACHE["prog"]
    x = np.ascontiguousarray(inp["x"], dtype=np.float32)
    mem = np.ascontiguousarray(inp["mem"], dtype=np.float32)
    in_maps = []
    for b in range(8):
        m = dict(shared)
        m["x"] = x[b]
        m["mem"] = mem[b]
        in_maps.append(m)
    res = run_bass_kernel_spmd(prog.nc, in_maps, core_ids=list(range(8)))
    out = np.stack([np.asarray(r["out"], dtype=np.float32).reshape(S, D) for r in res.results], axis=0)
    return out


def _add_mixers():
    def _prologue_mix(self):
        fw = self.fw
        t = self.tmpf.next()
        lam = self.vecs.ap[:, self.vc["lam"]:self.vc["lam"] + 6]
        one = self.ones12.ap[:, 0:1]
        fw.op("act", lambda e: e.activation(t.ap[:, 0:6], lam, AF.Exp, scale=-1.0), reads=[self.vecs], writes=[t])
        fw.op("act", lambda e: e.activation(t.ap[:, 0:6], t.ap[:, 0:6], AF.Ln, bias=one), reads=[t, self.ones12], writes=[t])
        self.ts("dve", self.cL, t, -8.0, None, ALU.mult, None, out_ap=self.cL.ap[:, 0:6], in_ap=t.ap[:, 0:6])
        self.ts("dve", self.cL, t, -16.0, None, ALU.mult, None, out_ap=self.cL.ap[:, 6:12], in_ap=t.ap[:, 0:6])
        self.ts("dve", self.negbf, self.vecs, -1.0, None, ALU.mult, None, in_ap=self.vcol("b_f"))
        for c in range(6):
            fw.op("dve", lambda e, a=self.hstate[c].ap: e.memset(a, 0.0), writes=[self.hstate[c]])
            fw.op("dve", lambda e, a=self.xbr[c].ap[:, 0:4]: e.memset(a, 0.0), writes=[self.xbr[c]])
        fw.op("dve", lambda e: e.memset(self.cumstate.ap, 0.0), writes=[self.cumstate])
        fw.dma("act", self.xin.ap[:, 0:2, :], self.mem_d.rearrange("(c p) d -> p c d", p=128), writes=[self.xin])
        for k in range(8):
            ps = self.ps.next()
            for tc in range(2):
                o = ps.ap[:, tc * 128:(tc + 1) * 128]
                i = self.xin.ap[:, tc, k * 128:(k + 1) * 128]
                fw.op("pe", lambda e, o=o, i=i: e.transpose(o, i, self.ident.ap),
                      reads=[self.xin, self.ident], writes=[ps], signal=(tc == 1))
            self.copy("act" if k % 2 else "dve", self.hT[k], ps, out_ap=self.hT[k].ap[:, 0:256], in_ap=ps.ap[:, 0:256])
        self.rmsnorm_fm(self.hT, "mem_norm_g", self.xn, n=NMEM)
        wkb, wk = self.wtile(("kv", 0))
        wvb, wv = self.wtile(("kv", 1))
        for c in range(2):
            ps = self.ps.next()
            for k in range(8):
                self.mm(ps, ps.ap[:, 0:NMEM], wk[:, k, c * 128:(c + 1) * 128], self.xn[k].ap[:, 0:NMEM], k == 0, k == 7,
                        [wkb, self.xn[k]])
            self.headnorm(ps, "mem_k_norm_g", self.mkT, self.mkT.ap[:, c, :], n=NMEM)
        for nch in range(2):
            ps = self.ps.next()
            for k in range(8):
                self.mm(ps, ps.ap[:, 0:256], self.xn[k].ap[:, nch * 128:(nch + 1) * 128], wv[:, k, :], k == 0, k == 7,
                        [wvb, self.xn[k]])
            self.copy("act", self.mv, ps, out_ap=self.mv.ap[:, nch, :], in_ap=ps.ap[:, 0:256])

    def cross_attn(self, L):
        for c in range(2):
            o = self.psacc.next()
            den = self.psacc.next()
            for h2 in range(2):
                h = 2 * c + h2
                b0 = 64 * h2
                for nch in range(2):
                    s = self.ps.next()
                    self.mm1(s, s.ap, self.mkT.ap[b0:b0 + 64, c, nch * 128:(nch + 1) * 128],
                             self.qmn[c].ap[b0:b0 + 64, :], True, True, [self.mkT, self.qmn[c]])
                    e_ = self.tmpb.next()
                    self.act_(AF.Exp, e_, s, scale=0.125)
                    self.mm1(o, o.ap[b0:b0 + 64, :], self.mv.ap[:, nch, h * 64:(h + 1) * 64], e_.ap,
                             nch == 0, nch == 1, [self.mv, e_])
                    self.mm1(den, den.ap[b0:b0 + 64, :], self.onesblk.ap[:, 0:64], e_.ap,
                             nch == 0, nch == 1, [self.onesblk, e_])
            rec = self.tmpf.next()
            self.act_(AF.Ln, rec, den)
            self.act_(AF.Exp, rec, rec, scale=-1.0)
            self.tt("dve", self.cat[6 + c], o, rec, ALU.mult)

    def out_proj(self, L):
        for g in range(4):
            wb, w = self.wtile(("mo", L, g))
            for fc in range(2):
                oc = 2 * g + fc
                ps = self.ps.next()
                for k in range(8):
                    self.mm(ps, ps.ap, w[:, k, fc * 128:(fc + 1) * 128], self.cat[k].ap, k == 0, k == 7,
                            [wb, self.cat[k]])
                self.tt("dve", self.hT[oc], ps, self.hT[oc], ALU.add)

    def mix_lru(self, t):
        fw = self.fw
        self.rmsnorm_fm(self.hT, ("mix_norm_g", 0), self.xn)
        if t > 0:
            for c in range(6):
                b = self.xbr[c]
                fw.op("pool", lambda e, b=b: e.tensor_copy(b.ap[:, 0:3], b.ap[:, TT:TT + 3]), reads=[b], writes=[b])
        for g in range(7):
            wb, w = self.wtile(("mi", 0, g))
            for fc in range(2):
                oc = 2 * g + fc
                ps = self.ps.next()
                for k in range(8):
                    self.mm(ps, ps.ap, w[:, k, fc * 128:(fc + 1) * 128], self.xn[k].ap, k == 0, k == 7, [wb, self.xn[k]])
                if oc < 6:
                    self.copy("act", self.xbr[oc], ps, out_ap=self.xbr[oc].ap[:, 3:TT + 3])
                elif oc < 12:
                    c = oc - 6
                    u = self.tmpf.next()
                    self.act_(AF.Square, u, ps)
                    self.ts("dve", u, u, 0.044715, 1.0, ALU.mult, ALU.add)
                    self.tt("dve", u, u, ps, ALU.mult)
                    self.act_(AF.Sigmoid, u, u, scale=1.5957691216057308)
                    self.tt("dve", self.cat[c], u, ps, ALU.mult)
                else:
                    c = oc - 12
                    self.headnorm(ps, ("memq_norm_g", 0), self.qmn[c], self.qmn[c].ap)
        one = self.ones12.ap[:, 0:1]
        for c in range(6):
            xb = self.xbr[c]
            acc = self.tmpf.next()
            cw = lambda tap: self.vcol("conv_w", tap * 6 + c)
            self.ts("dve", acc, xb, cw(0), self.vcol("conv_b", c), ALU.mult, ALU.add, in_ap=xb.ap[:, 0:TT],
                    extra_reads=[self.vecs])
            for tap in range(1, 4):
                self.stt(acc, xb, cw(tap), acc, ALU.mult, ALU.add, in0_ap=xb.ap[:, tap:tap + TT], extra_reads=[self.vecs])
            xcb = self.tmpb.next()
            self.copy("pool", xcb, acc)
            wrb, wr = self.wtile(("rg", c))
            wib, wi = self.wtile(("ig", c))
            pr = self.ps.next()
            self.mm(pr, pr.ap, wr[:, 0, :], xcb.ap, True, True, [wrb, xcb])
            pi = self.ps.next()
            self.mm(pi, pi.ap, wi[:, 0, :], xcb.ap, True, True, [wib, xcb])
            r = self.tmpf.next()
            self.act_(AF.Sigmoid, r, pr, bias=self.vcol("b_rg", c), extra_reads=[self.vecs])
            gi = self.tmpf.next()
            self.act_(AF.Sigmoid, gi, pi, bias=self.vcol("b_ig", c), extra_reads=[self.vecs])
            a = self.tmpf.next()
            self.act_(AF.Exp, a, r, scale=self.cL.ap[:, c:c + 1], extra_reads=[self.cL])
            m = self.tmpf.next()
            self.act_(AF.Exp, m, r, scale=self.cL.ap[:, 6 + c:7 + c], extra_reads=[self.cL])
            self.act_(AF.Sqrt, m, m, scale=-1.0, bias=one, extra_reads=[self.ones12])
            self.tt("pool", gi, gi, acc, ALU.mult)
            self.tt("dve", gi, gi, m, ALU.mult)
            hs = self.tmpf.next()
            hst = self.hstate[c]
            fw.op("dve", lambda e, hs=hs, a=a, gi=gi, hst=hst: e.tensor_tensor_scan(hs.ap, a.ap, gi.ap, hst.ap, ALU.mult, ALU.add),
                  reads=[a, gi, hst], writes=[hs])
            self.copy("pool", hst, hs, in_ap=hs.ap[:, TT - 1:TT])
            self.tt("dve", self.cat[c], hs, self.cat[c], ALU.mult)
        self.cross_attn(0)
        self.out_proj(0)

    Prog._prologue_mix = _prologue_mix
    Prog.cross_attn = cross_attn
    Prog.out_proj = out_proj
    Prog.mix_lru = mix_lru


_add_mixers()


def _add_fox():
    def mix_fox(self, t):
        fw = self.fw
        self.rmsnorm_fm(self.hT, ("mix_norm_g", 1), self.xn)
        for g in range(6):
            wb, w = self.wtile(("mi", 1, g))
            for fc in range(2):
                oc = 2 * g + fc
                ps = self.ps.next()
                for k in range(8):
                    self.mm(ps, ps.ap, w[:, k, fc * 128:(fc + 1) * 128], self.xn[k].ap, k == 0, k == 7, [wb, self.xn[k]])
                if oc < 6:
                    self.headnorm(ps, "fox_q_g", self.qT[oc], self.qT[oc].ap)
                else:
                    c = oc - 6
                    kn = self.tmpb.next()
                    self.headnorm(ps, "fox_k_g", kn, kn.ap)
                    kb = Buf(self.kc_d[c, :, t * TT:(t + 1) * TT], f"kc{c}_{t}")
                    self.kc_bufs[(c, t)] = kb
                    fw.dma("act", kb.ap, kn.ap, reads=[kn], writes=[kb])
        for g in range(3):
            wb, w = self.wtile(("mv", g))
            for tc in range(4):
                ps = self.ps.next()
                for k in range(8):
                    self.mm(ps, ps.ap[:, 0:256], self.xn[k].ap[:, tc * 128:(tc + 1) * 128], w[:, k, :], k == 0, k == 7,
                            [wb, self.xn[k]])
                self.copy("act" if tc % 2 else "dve", self.vtok, ps,
                          out_ap=self.vtok.ap[:, tc, g * 256:(g + 1) * 256], in_ap=ps.ap[:, 0:256])
        for c in range(6):
            vb = Buf(self.vc_d[c, :, 4 * t:4 * t + 4, :], f"vc{c}_{t}")
            self.vc_bufs[(c, t)] = vb
            fw.dma("act", vb.ap, self.vtok.ap[:, :, c * 128:(c + 1) * 128], reads=[self.vtok], writes=[vb])
        wb, w = self.wtile(("mf",))
        ps = self.ps.next()
        for k in range(8):
            self.mm(ps, ps.ap[0:12, :], w[:, k, 0:12], self.xn[k].ap, k == 0, k == 7, [wb, self.xn[k]])
        lf = self.tmpf.next()
        self.act_(AF.Exp, lf, ps, out_ap=lf.ap[0:12, :], in_ap=ps.ap[0:12, :], scale=-1.0, bias=self.negbf.ap[0:12, :],
                  extra_reads=[self.negbf])
        self.act_(AF.Ln, lf, lf, out_ap=lf.ap[0:12, :], in_ap=lf.ap[0:12, :], bias=self.ones12.ap[0:12, 0:1],
                  extra_reads=[self.ones12])
        fw.op("dve", lambda e, lf=lf: e.tensor_tensor_scan(self.cum.ap[0:12, :], self.ones12.ap[0:12, :], lf.ap[0:12, :],
                                                            self.cumstate.ap[0:12, :], ALU.mult, ALU.subtract),
              reads=[lf, self.ones12, self.cumstate], writes=[self.cum])
        self.copy("pool", self.cumstate, self.cum, out_ap=self.cumstate.ap[0:12, :], in_ap=self.cum.ap[0:12, TT - 1:TT])
        for tc in range(4):
            tp = self.ps.next()
            fw.op("pe", lambda e, tp=tp, tc=tc: e.transpose(tp.ap[:, 0:12], self.cum.ap[0:12, tc * 128:(tc + 1) * 128],
                                                           self.ident.ap[0:12, 0:12]),
                  reads=[self.cum, self.ident], writes=[tp])
            self.ts("dve", self.negcumT, tp, -1.0, None, ALU.mult, None,
                    out_ap=self.negcumT.ap[:, 4 * t + tc, :], in_ap=tp.ap[:, 0:12])
        wb, w = self.wtile(("mq",))
        for c in range(2):
            ps = self.ps.next()
            for k in range(8):
                self.mm(ps, ps.ap, w[:, k, c * 128:(c + 1) * 128], self.xn[k].ap, k == 0, k == 7, [wb, self.xn[k]])
            self.headnorm(ps, ("memq_norm_g", 1), self.qmn[c], self.qmn[c].ap)
        nkeys = (t + 1) * TT
        for c in range(6):
            o = self.psacc.next()
            den = self.psacc.next()
            for h2 in range(2):
                h = 2 * c + h2
                sel = self.tmpf.next()
                self.ts("dve", sel, self.cum, self.oh.ap[0:12, h:h + 1], None, ALU.mult, None,
                        out_ap=sel.ap[0:12, :], in_ap=self.cum.ap[0:12, :], extra_reads=[self.oh])
                pb = self.ps.next()
                self.mm1(pb, pb.ap, self.ones12.ap[0:12, 0:128], sel.ap[0:12, :], True, True, [self.ones12, sel])
                self.copy("act", self.cq[h2], pb)
            for kb0 in range(0, nkeys, 1024):
                wk = min(1024, nkeys - kb0)
                kbuf = self.kst.next()
                vbuf = self.vst.next()
                tiles = range(kb0 // TT, (kb0 + wk) // TT)
                fw.dma("sp", kbuf.ap[:, 0:wk], self.kc_d[c, :, kb0:kb0 + wk],
                       reads=[self.kc_bufs[(c, tt_)] for tt_ in tiles], writes=[kbuf])
                fw.dma("sp", vbuf.ap[:, 0:wk // 128, :], self.vc_d[c, :, kb0 // 128:(kb0 + wk) // 128, :],
                       reads=[self.vc_bufs[(c, tt_)] for tt_ in tiles], writes=[vbuf])
                for h2 in range(2):
                    h = 2 * c + h2
                    b0 = 64 * h2
                    for k8 in range(wk // 128):
                        kcg = kb0 // 128 + k8
                        diag = kcg - 4 * t
                        q0 = 128 * diag if diag >= 0 else 0
                        s = self.ps.next()
                        self.mm1(s, s.ap[:, q0:TT], kbuf.ap[b0:b0 + 64, k8 * 128:(k8 + 1) * 128],
                                 self.qT[c].ap[b0:b0 + 64, q0:TT], True, True, [kbuf, self.qT[c]])
                        tm = self.tmpf.next()
                        self.stt(tm, s, 0.125, self.cq[h2], ALU.mult, ALU.add, out_ap=tm.ap[:, q0:TT],
                                 in0_ap=s.ap[:, q0:TT], in1_ap=self.cq[h2].ap[:, q0:TT])
                        if diag >= 0:
                            self.tt("pool", tm, tm, self.tri, ALU.add, out_ap=tm.ap[:, q0:q0 + 128],
                                    a_ap=tm.ap[:, q0:q0 + 128])
                        p = self.tmpb.next()
                        self.act_(AF.Exp, p, tm, out_ap=p.ap[:, q0:TT], in_ap=tm.ap[:, q0:TT],
                                  bias=self.negcumT.ap[:, kcg, h:h + 1], extra_reads=[self.negcumT])
                        first = kcg == 0
                        last = kcg == 4 * t + 3
                        self.mm1(o, o.ap[b0:b0 + 64, q0:TT], vbuf.ap[:, k8, b0:b0 + 64], p.ap[:, q0:TT], first, last,
                                 [vbuf, p])
                        self.mm1(den, den.ap[b0:b0 + 64, q0:TT], self.onesblk.ap[:, 0:64], p.ap[:, q0:TT], first, last,
                                 [self.onesblk, p])
            rec = self.tmpf.next()
            self.act_(AF.Ln, rec, den)
            self.act_(AF.Exp, rec, rec, scale=-1.0)
            self.tt("dve", self.cat[c], o, rec, ALU.mult)
        self.cross_attn(1)
        self.out_proj(1)

    Prog.mix_fox = mix_fox


_add_fox()
```

```python
import contextlib
import numpy as np
import concourse.bass as bass
import concourse.mybir as mybir
from concourse.bass_utils import run_bass_kernel_spmd

F32 = mybir.dt.float32
BF16 = mybir.dt.bfloat16
AF = mybir.ActivationFunctionType
ALU = mybir.AluOpType

D = 1024
S = 4096
DFF = 2816
NMEM = 256
HD = 64
MIXW = 768
KC = D // 128
FC = DFF // 128
EPS = 1e-6
SYNC_SAME = True


class Buf:
    __slots__ = ("ap", "lw", "rd", "name")

    def __init__(self, ap, name=""):
        self.ap = ap if isinstance(ap, bass.AP) else ap[:]
        self.lw = None
        self.rd = {}
        self.name = name


class Q:
    def __init__(self, name, sem, is_pe=False):
        self.name = name
        self.sem = sem
        self.is_pe = is_pe
        self.count = 0
        self.prog = []
        self.waited = {}
        self.dma_sems = []
        self.dma_vals = []
        self.dma_next = 0


class FW:
    def __init__(self, nc, stack, n_dma_sems=8):
        self.nc = nc
        self.q = {}
        for name, is_pe in (("pe", True), ("act", False), ("dve", False), ("pool", False), ("sp", False)):
            sem = stack.enter_context(nc.semaphore(f"prog_{name}"))
            self.q[name] = Q(name, sem, is_pe)
        for name in ("sp", "act", "pool"):
            q = self.q[name]
            for i in range(n_dma_sems):
                q.dma_sems.append(stack.enter_context(nc.semaphore(f"dma_{name}_{i}")))
                q.dma_vals.append(0)

    def _deps(self, q, reads, writes, extra=()):
        deps = {}

        def need(tok):
            if tok is None:
                return
            k = id(tok[0])
            if k not in deps or deps[k][1] < tok[1]:
                deps[k] = tok

        for b in reads:
            need(b.lw)
        for b in writes:
            need(b.lw)
            for tok in b.rd.values():
                need(tok)
        for tok in extra:
            need(tok)
        for k, (sem, val, owner) in deps.items():
            if owner is q and (q.is_pe or not SYNC_SAME):
                continue
            if q.waited.get(k, 0) >= val:
                continue
            q.waited[k] = val
            q.prog.append(("wait", sem, val))

    def op(self, qname, fn, reads=(), writes=(), signal=True):
        q = self.q[qname]
        self._deps(q, reads, writes)
        if signal:
            q.count += 1
            tok = (q.sem, q.count, q)
        else:
            tok = (q.sem, q.count + 1, q)
        q.prog.append(("op", fn, signal))
        for b in writes:
            b.lw = tok
            b.rd = {}
        for b in reads:
            b.rd[q.name] = tok

    def dma(self, qname, out_ap, in_ap, reads=(), writes=()):
        q = self.q[qname]
        i = q.dma_next
        q.dma_next = (i + 1) % len(q.dma_sems)
        sem = q.dma_sems[i]
        cur = q.dma_vals[i]
        extra = [(sem, cur, None)] if cur > 0 else []
        self._deps(q, reads, writes, extra)
        q.dma_vals[i] = cur + 16
        tok = (sem, cur + 16, None)
        q.prog.append(("dma", out_ap, in_ap, sem))
        for b in writes:
            b.lw = tok
            b.rd = {}
        for b in reads:
            b.rd[("dma", id(sem))] = tok

    def finish(self):
        for name in ("sp", "act", "pool"):
            q = self.q[name]
            for sem, val in zip(q.dma_sems, q.dma_vals):
                if val > 0 and q.waited.get(id(sem), 0) < val:
                    q.prog.append(("wait", sem, val))

    def replay(self, qname, eng):
        q = self.q[qname]
        for item in q.prog:
            if item[0] == "wait":
                eng.wait_ge(item[1], item[2])
            elif item[0] == "op":
                inst = item[1](eng)
                if item[2]:
                    inst.then_inc(q.sem, 1)
            else:
                eng.dma_start(out=item[1], in_=item[2]).then_inc(item[3], 16)


class Ring:
    def __init__(self, bufs):
        self.bufs = bufs
        self.i = 0

    def next(self):
        b = self.bufs[self.i]
        self.i = (self.i + 1) % len(self.bufs)
        return b


WT_ELEMS = 2048


def _tiles_of(W, col_ranges, nk_split):
    out = []
    Kin = W.shape[0]
    kcs = Kin // 128
    Wr = W.reshape(kcs, 128, W.shape[1])
    for (c0, nc_) in col_ranges:
        k0 = 0
        for nk in nk_split:
            t = Wr[k0:k0 + nk, :, c0:c0 + nc_]
            t = np.transpose(t, (1, 0, 2)).reshape(128, nk * nc_)
            out.append(t)
            k0 += nk
        assert k0 == kcs
    return out


class WeightPlan:
    def __init__(self):
        self.tiles = []
        self.arrays = []

    def add(self, arrs):
        idx0 = len(self.arrays)
        self.arrays.extend(arrs)
        return list(range(idx0, idx0 + len(arrs)))

    def pack(self):
        out = np.zeros((len(self.arrays), 128, WT_ELEMS), np.float32)
        for i, a in enumerate(self.arrays):
            out[i, :, :a.shape[1]] = a
        return out


def weight_specs():
    sp = []
    for L in range(2):
        for nm, src_in, src_out in (("f1", "ffn1_w_in", "ffn1_w_out"), ("f2", "ffn2_w_in", "ffn2_w_out")):
            for g in range(DFF // 256):
                sp.append(((nm + "i", L, "g", g), src_in, L, g * 256, 256, 0, 8))
                sp.append(((nm + "i", L, "u", g), src_in, L, DFF + g * 256, 256, 0, 8))
            for oc in range(8):
                sp.append(((nm + "o", L, oc, 0), src_out, L, oc * 128, 128, 0, 11))
                sp.append(((nm + "o", L, oc, 1), src_out, L, oc * 128, 128, 11, 11))
        for g in range(4):
            sp.append((("mo", L, g), "mix_w_out", L, g * 256, 256, 0, 8))
    for g in range(7):
        sp.append((("mi", 0, g), "lru_w_in", 0, g * 256, 256, 0, 8))
    for g in range(6):
        sp.append((("mi", 1, g), "fox_w_in", 0, g * 256, 256, 0, 8))
    for g in range(3):
        sp.append((("mv", g), "fox_w_in", 0, 1536 + g * 256, 256, 0, 8))
    sp.append((("mf",), "fox_w_in", 0, 2304, 12, 0, 8))
    sp.append((("mq",), "fox_w_in", 0, 2316, 256, 0, 8))
    for g in range(2):
        sp.append((("kv", g), "mem_w_kv", None, g * 256, 256, 0, 8))
    for c in range(6):
        sp.append((("rg", c), "lru_w_rg", 0, c, 0, 0, 1))
        sp.append((("ig", c), "lru_w_ig", 0, c, 0, 0, 1))
    return sp


def pack_weights(inp):
    sp = weight_specs()
    out = np.zeros((len(sp), 128, WT_ELEMS), np.float32)
    index = {}
    for i, (key, src, L, c0, ncols, k0, nk) in enumerate(sp):
        index[key] = i
        W = inp[src]
        if key[0] in ("rg", "ig"):
            blk = W[0]
            c = c0
            out[i, 0:64, 0:64] = blk[2 * c]
            out[i, 64:128, 64:128] = blk[2 * c + 1]
            continue
        if L is not None:
            W = W[L]
        Wr = W.reshape(W.shape[0] // 128, 128, W.shape[1])
        t = Wr[k0:k0 + nk, :, c0:c0 + ncols]
        out[i, :, :nk * ncols] = np.transpose(t, (1, 0, 2)).reshape(128, nk * ncols)
    return out, index


def vec_cols():
    cols = {}
    n = 0
    for L in range(2):
        for nm in ("ffn1_norm_g", "mix_norm_g", "ffn2_norm_g"):
            cols[(nm, L)] = n
            n += 8
    cols["mem_norm_g"] = n; n += 8
    cols["mem_k_norm_g"] = n; n += 1
    cols[("memq_norm_g", 0)] = n; n += 1
    cols[("memq_norm_g", 1)] = n; n += 1
    cols["conv_w"] = n; n += 24
    cols["conv_b"] = n; n += 6
    cols["b_rg"] = n; n += 6
    cols["b_ig"] = n; n += 6
    cols["lam"] = n; n += 6
    cols["b_f"] = n; n += 1
    cols["fox_q_g"] = n; n += 1
    cols["fox_k_g"] = n; n += 1
    cols["_n"] = n
    return cols


def pack_vecs(inp):
    cols = vec_cols()
    v = np.zeros((128, cols["_n"]), np.float32)

    def fm(a):
        return a.reshape(-1, 128).T

    for L in range(2):
        for nm in ("ffn1_norm_g", "mix_norm_g", "ffn2_norm_g"):
            v[:, cols[(nm, L)]:cols[(nm, L)] + 8] = fm(inp[nm][L])
    v[:, cols["mem_norm_g"]:cols["mem_norm_g"] + 8] = fm(inp["mem_norm_g"])
    rep = lambda a: np.concatenate([a, a])
    v[:, cols["mem_k_norm_g"]] = rep(inp["mem_k_norm_g"])
    for L in range(2):
        v[:, cols[("memq_norm_g", L)]] = rep(inp["memq_norm_g"][L])
    for tap in range(4):
        v[:, cols["conv_w"] + tap * 6: cols["conv_w"] + tap * 6 + 6] = fm(inp["lru_conv_w"][0, tap])
    v[:, cols["conv_b"]:cols["conv_b"] + 6] = fm(inp["lru_conv_b"][0])
    v[:, cols["b_rg"]:cols["b_rg"] + 6] = fm(inp["lru_b_rg"][0])
    v[:, cols["b_ig"]:cols["b_ig"] + 6] = fm(inp["lru_b_ig"][0])
    v[:, cols["lam"]:cols["lam"] + 6] = fm(inp["lru_lambda"][0])
    v[0:12, cols["b_f"]] = inp["fox_b_f"][0]
    v[:, cols["fox_q_g"]] = rep(inp["fox_q_norm_g"][0])
    v[:, cols["fox_k_g"]] = rep(inp["fox_k_norm_g"][0])
    return v


NEG = -30000.0


def make_consts():
    c = {}
    c["ident"] = np.eye(128, dtype=np.float32)
    ones = np.ones((128, 128), np.float32)
    blk = np.zeros((128, 128), np.float32)
    blk[0:64, 0:64] = 1.0
    blk[64:128, 64:128] = 1.0
    c["ones_blk"] = np.concatenate([ones, blk], axis=1)
    md = np.zeros((128, 4, 512), np.float32)
    for kk in range(4):
        key = kk * 128 + np.arange(128)[:, None]
        qry = np.arange(512)[None, :]
        md[:, kk, :] = np.where(key <= qry, 0.0, NEG)
    c["maskd"] = md.reshape(128, 2048)
    sel = np.zeros((128, 12, 128), np.float32)
    for h in range(12):
        sel[h, h, :] = 1.0
    c["sel"] = sel.reshape(128, 12 * 128)
    return c


TT = 512
NCONST_VEC = None


class Prog:
    def __init__(self, n_tiles=S // TT, stages=("ffn1a", "mix0", "ffn2a", "ffn1b", "mix1", "ffn2b"), dbg=False):
        self.n_tiles = n_tiles
        self.stages = stages
        self.dbg = dbg
        self.stack = contextlib.ExitStack()
        nc = self.nc = bass.Bass("TRN2", target_bir_lowering=False)
        self.fw = FW(nc, self.stack)
        self.vc = vec_cols()
        self.widx = {k[0]: i for i, k in enumerate(weight_specs())}
        self.wspec = {k[0]: k for k in weight_specs()}
        nW = len(self.widx)
        dt = nc.dram_tensor
        self.x_d = dt("x", [S, D], F32, kind="ExternalInput").ap()
        self.mem_d = dt("mem", [NMEM, D], F32, kind="ExternalInput").ap()
        self.wts_d = dt("wts", [nW, 128, WT_ELEMS], F32, kind="ExternalInput").ap()
        self.vecs_d = dt("vecs", [128, self.vc["_n"]], F32, kind="ExternalInput").ap()
        self.ident_d = dt("ident", [128, 128], F32, kind="ExternalInput").ap()
        self.onesblk_d = dt("ones_blk", [128, 256], F32, kind="ExternalInput").ap()
        self.tri_d = dt("tri", [128, 128], F32, kind="ExternalInput").ap()
        self.oh_d = dt("onehot", [128, 16], F32, kind="ExternalInput").ap()
        self.out_d = dt("out", [S, D], F32, kind="ExternalOutput").ap()
        self.kc_d = dt("kcache", [6, 128, S], BF16, kind="Internal").ap()
        self.vc_d = dt("vcache", [6, 128, S // 128, 128], BF16, kind="Internal").ap()
        self.wbf_d = dt("wbf16", [nW, 128, WT_ELEMS], BF16, kind="Internal").ap()
        self.wconv = {}
        self.ncast = 0
        if dbg:
            self.dbg_d = dt("dbg", [128, 8 * 512], F32, kind="ExternalOutput").ap()
        self.kc_bufs = {}
        self.vc_bufs = {}
        self._alloc()
        self._prologue()
        for t in range(n_tiles):
            self._tile(t)
        self.fw.finish()
        self._emit()

    def sb(self, name, shape, dtype):
        return self.stack.enter_context(self.nc.sbuf_tensor("sb_" + name, shape, dtype))

    def _alloc(self):
        nc = self.nc
        sb = self.sb
        B = Buf
        self.ident = B(sb("ident", [128, 128], F32))
        self.onesblk = B(sb("onesblk", [128, 256], BF16))
        self.tri = B(sb("tri", [128, 128], F32))
        self.oh = B(sb("oh", [128, 16], F32))
        self.ones12 = B(sb("ones12", [128, 512], F32))
        self.vecs = B(sb("vecs", [128, self.vc["_n"]], F32))
        self.cL = B(sb("cL", [128, 12], F32))
        self.negbf = B(sb("negbf", [128, 1], F32))
        self.epsb = B(sb("epsb", [128, 1], F32))
        hT = sb("hT", [128, 8, TT], F32)
        self.hT = [B(hT[:, k, :], f"hT{k}") for k in range(8)]
        xn = sb("xn", [128, 8, TT], BF16)
        self.xn = [B(xn[:, k, :], f"xn{k}") for k in range(8)]
        act = sb("act", [128, FC, TT], BF16)
        self.act = [B(act[:, k, :], f"act{k}") for k in range(FC)]
        cat = sb("cat", [128, 8, TT], BF16)
        self.cat = [B(cat[:, k, :], f"cat{k}") for k in range(8)]
        self.wstage = Ring([B(sb(f"wst{i}", [128, WT_ELEMS], F32), f"wst{i}") for i in range(2)])
        self.wbf = Ring([B(sb(f"wbf{i}", [128, WT_ELEMS], BF16), f"wbf{i}") for i in range(6)])
        self.xin = B(sb("xin", [128, 4, D], F32), "xin")
        self.yout = B(sb("yout", [128, 4, D], F32), "yout") if False else self.xin
        self.mkT = B(sb("mkT", [128, 2, NMEM], BF16), "mkT")
        self.mv = B(sb("mv", [128, 2, 256], BF16), "mv")
        self.tmpf = Ring([B(sb(f"tmpf{i}", [128, TT], F32), f"tmpf{i}") for i in range(6)])
        self.tmpb = Ring([B(sb(f"tmpb{i}", [128, TT], BF16), f"tmpb{i}") for i in range(4)])
        xbr = sb("xbr", [128, 6, TT + 4], F32)
        self.xbr_t = xbr
        self.xbr = [B(xbr[:, c, :], f"xbr{c}") for c in range(6)]
        self.hstate = [B(sb(f"hstate{c}", [128, 1], F32), f"hstate{c}") for c in range(6)]
        qT = sb("qT", [128, 6, TT], BF16)
        self.qT = [B(qT[:, c, :], f"qT{c}") for c in range(6)]
        self.kst = Ring([B(sb(f"kst{i}", [128, 1024], BF16), f"kst{i}") for i in range(3)])
        self.vst = Ring([B(sb(f"vst{i}", [128, 8, 128], BF16), f"vst{i}") for i in range(3)])
        self.cum = B(sb("cum", [128, TT], F32), "cum")
        self.cumstate = B(sb("cumstate", [128, 1], F32), "cumstate")
        self.negcumT = B(sb("negcumT", [128, S // 128, 12], F32), "negcumT")
        self.vtok = B(sb("vtok", [128, 4, MIXW], BF16), "vtok")
        self.ps = Ring([B(self.stack.enter_context(nc.psum_tensor(f"ps{i}", [128, 512], F32)), f"ps{i}")
                        for i in range(4)])
        self.psacc = Ring([B(self.stack.enter_context(nc.psum_tensor(f"pa{i}", [128, 512], F32)), f"pa{i}")
                           for i in range(4)])
        qmn = sb("qmn", [128, 2, TT], BF16)
        self.qmn = [B(qmn[:, c, :], f"qmn{c}") for c in range(2)]
        self.cq = [B(sb(f"cq{i}", [128, TT], F32), f"cq{i}") for i in range(2)]

    def vcol(self, key, off=0, n=1, p0=0, p1=128):
        c = self.vc[key] + off
        return self.vecs.ap[p0:p1, c:c + n]

    def wtile(self, key):
        fw = self.fw
        _, src, L, c0, ncols, k0, nk = self.wspec[key]
        if key[0] in ("rg", "ig"):
            nk, ncols = 1, 128
        n = nk * ncols
        idx = self.widx[key]
        wb = self.wbf.next()
        if key in self.wconv:
            db = self.wconv[key]
            fw.dma("sp", wb.ap[:, 0:n], db.ap, reads=[db], writes=[wb])
        else:
            st = self.wstage.next()
            fw.dma("sp", st.ap[:, 0:n], self.wts_d[idx, :, 0:n], writes=[st])
            eng = "act"
            self.ncast += 1
            self.copy(eng, wb, st, out_ap=wb.ap[:, 0:n], in_ap=st.ap[:, 0:n])
            if key[0] != "kv" and self.n_tiles > 1:
                db = Buf(self.wbf_d[idx, :, 0:n], "wd")
                self.wconv[key] = db
                fw.dma(eng, db.ap, wb.ap[:, 0:n], reads=[wb], writes=[db])
        return wb, wb.ap[:, 0:n].rearrange("p (k n) -> p k n", k=nk)

    def mm(self, ps, out_ap, lhsT, rhs, start, stop, reads, signal=None):
        self.fw.op("pe", lambda e: e.matmul(out_ap, lhsT, rhs, start=start, stop=stop),
                   reads=reads, writes=[ps], signal=(stop if signal is None else signal))

    def mm1(self, ps, out_ap, lhsT, rhs, start, stop, reads):
        self.fw.op("pe", lambda e: e.matmul(out_ap, lhsT, rhs, start=start, stop=stop),
                   reads=reads, writes=[ps], signal=True)

    def act_(self, func, out_b, in_b, out_ap=None, in_ap=None, extra_reads=(), **kw):
        o = out_b.ap if out_ap is None else out_ap
        i = in_b.ap if in_ap is None else in_ap
        self.fw.op("act", lambda e: e.activation(o, i, func, **kw), reads=[in_b] + list(extra_reads), writes=[out_b])

    def tt(self, eng, out_b, a_b, b_b, op, out_ap=None, a_ap=None, b_ap=None):
        o = out_b.ap if out_ap is None else out_ap
        a = a_b.ap if a_ap is None else a_ap
        b = b_b.ap if b_ap is None else b_ap
        self.fw.op(eng, lambda e: e.tensor_tensor(o, a, b, op), reads=[a_b, b_b], writes=[out_b])

    def ts(self, eng, out_b, in_b, s1, s2, op0, op1, out_ap=None, in_ap=None, extra_reads=()):
        o = out_b.ap if out_ap is None else out_ap
        i = in_b.ap if in_ap is None else in_ap
        if op1 is None:
            fn = lambda e: e.tensor_scalar(o, i, s1, None, op0)
        else:
            fn = lambda e: e.tensor_scalar(o, i, s1, s2, op0, op1)
        self.fw.op(eng, fn, reads=[in_b] + list(extra_reads), writes=[out_b])

    def stt(self, out_b, in0_b, scalar, in1_b, op0, op1, out_ap=None, in0_ap=None, in1_ap=None, extra_reads=()):
        o = out_b.ap if out_ap is None else out_ap
        a = in0_b.ap if in0_ap is None else in0_ap
        b = in1_b.ap if in1_ap is None else in1_ap
        self.fw.op("dve", lambda e: e.scalar_tensor_tensor(o, a, scalar, b, op0, op1),
                   reads=[in0_b, in1_b] + list(extra_reads), writes=[out_b])

    def copy(self, eng, out_b, in_b, out_ap=None, in_ap=None):
        o = out_b.ap if out_ap is None else out_ap
        i = in_b.ap if in_ap is None else in_ap
        if eng == "act":
            fn = lambda e: e.copy(o, i)
        else:
            fn = lambda e: e.tensor_copy(o, i)
        self.fw.op(eng, fn, reads=[in_b], writes=[out_b])

    def rmsnorm_fm(self, src, gkey, dst, n=TT):
        ps = self.ps.next()
        for k in range(8):
            sq = self.tmpb.next()
            self.act_(AF.Square, sq, src[k], out_ap=sq.ap[:, 0:n], in_ap=src[k].ap[:, 0:n])
            self.mm(ps, ps.ap[:, 0:n], self.onesblk.ap[:, 0:128], sq.ap[:, 0:n], k == 0, k == 7, [sq, self.onesblk],
                    signal=True)
        rstd = self.tmpf.next()
        self.act_(AF.Ln, rstd, ps, out_ap=rstd.ap[:, 0:n], in_ap=ps.ap[:, 0:n], bias=self.epsb.ap[:, 0:1],
                  scale=1.0 / D, extra_reads=[self.epsb])
        self.act_(AF.Exp, rstd, rstd, out_ap=rstd.ap[:, 0:n], in_ap=rstd.ap[:, 0:n], scale=-0.5)
        for k in range(8):
            self.stt(dst[k], src[k], self.vcol(gkey, k), rstd, ALU.mult, ALU.mult,
                     out_ap=dst[k].ap[:, 0:n], in0_ap=src[k].ap[:, 0:n], in1_ap=rstd.ap[:, 0:n],
                     extra_reads=[self.vecs])

    def headnorm(self, ps, gkey, out_b, out_ap, n=TT):
        sq = self.tmpb.next()
        self.act_(AF.Square, sq, ps, out_ap=sq.ap[:, 0:n], in_ap=ps.ap[:, 0:n])
        pn = self.ps.next()
        self.mm(pn, pn.ap[:, 0:n], self.onesblk.ap[:, 128:256], sq.ap[:, 0:n], True, True, [sq, self.onesblk])
        rstd = self.tmpf.next()
        self.act_(AF.Ln, rstd, pn, out_ap=rstd.ap[:, 0:n], in_ap=pn.ap[:, 0:n], bias=self.epsb.ap[:, 0:1],
                  scale=1.0 / HD, extra_reads=[self.epsb])
        self.act_(AF.Exp, rstd, rstd, out_ap=rstd.ap[:, 0:n], in_ap=rstd.ap[:, 0:n], scale=-0.5)
        self.stt(out_b, ps, self.vcol(gkey), rstd, ALU.mult, ALU.mult,
                 out_ap=out_ap, in0_ap=ps.ap[:, 0:n], in1_ap=rstd.ap[:, 0:n], extra_reads=[self.vecs])

    def load_x(self, t):
        fw = self.fw
        fw.dma("act", self.xin.ap, self.x_d[t * TT:(t + 1) * TT, :].rearrange("(c p) d -> p c d", p=128),
               writes=[self.xin])
        for k in range(8):
            ps = self.ps.next()
            for tc in range(4):
                o = ps.ap[:, tc * 128:(tc + 1) * 128]
                i = self.xin.ap[:, tc, k * 128:(k + 1) * 128]
                fw.op("pe", lambda e, o=o, i=i: e.transpose(o, i, self.ident.ap),
                      reads=[self.xin, self.ident], writes=[ps], signal=(tc == 3))
            self.copy("act" if k % 2 else "dve", self.hT[k], ps)

    def store_out(self, t):
        fw = self.fw
        for tc in range(4):
            for half in range(2):
                ps = self.ps.next()
                for kk in range(4):
                    k = half * 4 + kk
                    o = ps.ap[:, kk * 128:(kk + 1) * 128]
                    i = self.hT[k].ap[:, tc * 128:(tc + 1) * 128]
                    fw.op("pe", lambda e, o=o, i=i: e.transpose(o, i, self.ident.ap),
                          reads=[self.hT[k], self.ident], writes=[ps], signal=(kk == 3))
                self.copy("act" if half else "dve", self.xin, ps,
                          out_ap=self.xin.ap[:, tc, half * 512:(half + 1) * 512])
        fw.dma("act", self.out_d[t * TT:(t + 1) * TT, :].rearrange("(c p) d -> p c d", p=128), self.xin.ap,
               reads=[self.xin])

    def ffn(self, L, nm, gname):
        fw = self.fw
        self.rmsnorm_fm(self.hT, (gname, L), self.xn)
        for g in range(DFF // 256):
            wg_b, wg = self.wtile((nm + "i", L, "g", g))
            wu_b, wu = self.wtile((nm + "i", L, "u", g))
            for fc in range(2):
                f = g * 2 + fc
                pg = self.ps.next()
                pu = self.ps.next()
                for k in range(8):
                    self.mm(pg, pg.ap, wg[:, k, fc * 128:(fc + 1) * 128], self.xn[k].ap, k == 0, k == 7,
                            [wg_b, self.xn[k]])
                for k in range(8):
                    self.mm(pu, pu.ap, wu[:, k, fc * 128:(fc + 1) * 128], self.xn[k].ap, k == 0, k == 7,
                            [wu_b, self.xn[k]])
                sg = self.tmpf.next()
                self.act_(AF.Silu, sg, pg)
                self.tt("dve", self.act[f], sg, pu, ALU.mult)
        for oc in range(8):
            w0b, w0 = self.wtile((nm + "o", L, oc, 0))
            w1b, w1 = self.wtile((nm + "o", L, oc, 1))
            py = self.ps.next()
            for k in range(FC):
                wb, w = (w0b, w0) if k < 11 else (w1b, w1)
                self.mm(py, py.ap, w[:, k % 11, :], self.act[k].ap, k == 0, k == FC - 1, [wb, self.act[k]])
            self.stt(self.hT[oc], py, 0.5, self.hT[oc], ALU.mult, ALU.add)

    def _prologue(self):
        fw = self.fw
        fw.dma("act", self.ident.ap, self.ident_d, writes=[self.ident])
        fw.dma("act", self.tri.ap, self.tri_d, writes=[self.tri])
        fw.dma("act", self.oh.ap, self.oh_d, writes=[self.oh])
        fw.dma("act", self.vecs.ap, self.vecs_d, writes=[self.vecs])
        st = self.tmpf.next()
        fw.dma("act", st.ap[:, 0:256], self.onesblk_d, writes=[st])
        self.copy("dve", self.onesblk, st, in_ap=st.ap[:, 0:256])
        fw.op("dve", lambda e: e.memset(self.epsb.ap, EPS), writes=[self.epsb])
        fw.op("dve", lambda e: e.memset(self.ones12.ap, 1.0), writes=[self.ones12])
        if "mix0" in self.stages or "mix1" in self.stages:
            self._prologue_mix()

    def _tile(self, t):
        st = self.stages
        self.load_x(t)
        if "ffn1a" in st:
            self.ffn(0, "f1", "ffn1_norm_g")
        if "mix0" in st:
            self.mix_lru(t)
        if "ffn2a" in st:
            self.ffn(0, "f2", "ffn2_norm_g")
        if "ffn1b" in st:
            self.ffn(1, "f1", "ffn1_norm_g")
        if "mix1" in st:
            self.mix_fox(t)
        if "ffn2b" in st:
            self.ffn(1, "f2", "ffn2_norm_g")
        self.store_out(t)

    def _emit(self):
        nc = self.nc
        fw = self.fw
        with nc.Block() as block:
            @block.tensor
            def _(e):
                fw.replay("pe", e)

            @block.scalar
            def _(e):
                fw.replay("act", e)

            @block.vector
            def _(e):
                fw.replay("dve", e)

            @block.gpsimd
            def _(e):
                fw.replay("pool", e)

            @block.sync
            def _(e):
                fw.replay("sp", e)
        self.stack.close()


_CACHE = {}


def host_inputs(inp):
    wts, _ = pack_weights(inp)
    vecs = pack_vecs(inp)
    c = make_consts()
    tri = np.where(np.arange(128)[:, None] <= np.arange(128)[None, :], 0.0, NEG).astype(np.float32)
    oh = np.zeros((128, 16), np.float32)
    for h in range(12):
        oh[h, h] = 1.0
    shared = {"wts": wts, "vecs": vecs, "ident": c["ident"], "ones_blk": c["ones_blk"], "tri": tri, "onehot": oh}
    return shared


def kernel(**inputs):
    inp = {k: np.asarray(v) for k, v in inputs.items()}
    shared = host_inputs(inp)
    if "prog" not in _CACHE:
        _CACHE["prog"] = Prog()
    prog = _CACHE["prog"]
    x = np.ascontiguousarray(inp["x"], dtype=np.float32)
    mem = np.ascontiguousarray(inp["mem"], dtype=np.float32)
    in_maps = []
    for b in range(8):
        m = dict(shared)
        m["x"] = x[b]
        m["mem"] = mem[b]
        in_maps.append(m)
    res = run_bass_kernel_spmd(prog.nc, in_maps, core_ids=list(range(8)))
    out = np.stack([np.asarray(r["out"], dtype=np.float32).reshape(S, D) for r in res.results], axis=0)
    return out


def _add_mixers():
    def _prologue_mix(self):
        fw = self.fw
        t = self.tmpf.next()
        lam = self.vecs.ap[:, self.vc["lam"]:self.vc["lam"] + 6]
        one = self.ones12.ap[:, 0:1]
        fw.op("act", lambda e: e.activation(t.ap[:, 0:6], lam, AF.Exp, scale=-1.0), reads=[self.vecs], writes=[t])
        fw.op("act", lambda e: e.activation(t.ap[:, 0:6], t.ap[:, 0:6], AF.Ln, bias=one), reads=[t, self.ones12], writes=[t])
        self.ts("dve", self.cL, t, -8.0, None, ALU.mult, None, out_ap=self.cL.ap[:, 0:6], in_ap=t.ap[:, 0:6])
        self.ts("dve", self.cL, t, -16.0, None, ALU.mult, None, out_ap=self.cL.ap[:, 6:12], in_ap=t.ap[:, 0:6])
        self.ts("dve", self.negbf, self.vecs, -1.0, None, ALU.mult, None, in_ap=self.vcol("b_f"))
        for c in range(6):
            fw.op("dve", lambda e, a=self.hstate[c].ap: e.memset(a, 0.0), writes=[self.hstate[c]])
            fw.op("dve", lambda e, a=self.xbr[c].ap[:, 0:4]: e.memset(a, 0.0), writes=[self.xbr[c]])
        fw.op("dve", lambda e: e.memset(self.cumstate.ap, 0.0), writes=[self.cumstate])
        fw.dma("act", self.xin.ap[:, 0:2, :], self.mem_d.rearrange("(c p) d -> p c d", p=128), writes=[self.xin])
        for k in range(8):
            ps = self.ps.next()
            for tc in range(2):
                o = ps.ap[:, tc * 128:(tc + 1) * 128]
                i = self.xin.ap[:, tc, k * 128:(k + 1) * 128]
                fw.op("pe", lambda e, o=o, i=i: e.transpose(o, i, self.ident.ap),
                      reads=[self.xin, self.ident], writes=[ps], signal=(tc == 1))
            self.copy("act" if k % 2 else "dve", self.hT[k], ps, out_ap=self.hT[k].ap[:, 0:256], in_ap=ps.ap[:, 0:256])
        self.rmsnorm_fm(self.hT, "mem_norm_g", self.xn, n=NMEM)
        wkb, wk = self.wtile(("kv", 0))
        wvb, wv = self.wtile(("kv", 1))
        for c in range(2):
            ps = self.ps.next()
            for k in range(8):
                self.mm(ps, ps.ap[:, 0:NMEM], wk[:, k, c * 128:(c + 1) * 128], self.xn[k].ap[:, 0:NMEM], k == 0, k == 7,
                        [wkb, self.xn[k]])
            self.headnorm(ps, "mem_k_norm_g", self.mkT, self.mkT.ap[:, c, :], n=NMEM)
        for nch in range(2):
            ps = self.ps.next()
            for k in range(8):
                self.mm(ps, ps.ap[:, 0:256], self.xn[k].ap[:, nch * 128:(nch + 1) * 128], wv[:, k, :], k == 0, k == 7,
                        [wvb, self.xn[k]])
            self.copy("act", self.mv, ps, out_ap=self.mv.ap[:, nch, :], in_ap=ps.ap[:, 0:256])

    def cross_attn(self, L):
        for c in range(2):
            o = self.psacc.next()
            den = self.psacc.next()
            for h2 in range(2):
                h = 2 * c + h2
                b0 = 64 * h2
                for nch in range(2):
                    s = self.ps.next()
                    self.mm1(s, s.ap, self.mkT.ap[b0:b0 + 64, c, nch * 128:(nch + 1) * 128],
                             self.qmn[c].ap[b0:b0 + 64, :], True, True, [self.mkT, self.qmn[c]])
                    e_ = self.tmpb.next()
                    self.act_(AF.Exp, e_, s, scale=0.125)
                    self.mm1(o, o.ap[b0:b0 + 64, :], self.mv.ap[:, nch, h * 64:(h + 1) * 64], e_.ap,
                             nch == 0, nch == 1, [self.mv, e_])
                    self.mm1(den, den.ap[b0:b0 + 64, :], self.onesblk.ap[:, 0:64], e_.ap,
                             nch == 0, nch == 1, [self.onesblk, e_])
            rec = self.tmpf.next()
            self.act_(AF.Ln, rec, den)
            self.act_(AF.Exp, rec, rec, scale=-1.0)
            self.tt("dve", self.cat[6 + c], o, rec, ALU.mult)

    def out_proj(self, L):
        for g in range(4):
            wb, w = self.wtile(("mo", L, g))
            for fc in range(2):
                oc = 2 * g + fc
                ps = self.ps.next()
                for k in range(8):
                    self.mm(ps, ps.ap, w[:, k, fc * 128:(fc + 1) * 128], self.cat[k].ap, k == 0, k == 7,
                            [wb, self.cat[k]])
                self.tt("dve", self.hT[oc], ps, self.hT[oc], ALU.add)

    def mix_lru(self, t):
        fw = self.fw
        self.rmsnorm_fm(self.hT, ("mix_norm_g", 0), self.xn)
        if t > 0:
            for c in range(6):
                b = self.xbr[c]
                fw.op("pool", lambda e, b=b: e.tensor_copy(b.ap[:, 0:3], b.ap[:, TT:TT + 3]), reads=[b], writes=[b])
        for g in range(7):
            wb, w = self.wtile(("mi", 0, g))
            for fc in range(2):
                oc = 2 * g + fc
                ps = self.ps.next()
                for k in range(8):
                    self.mm(ps, ps.ap, w[:, k, fc * 128:(fc + 1) * 128], self.xn[k].ap, k == 0, k == 7, [wb, self.xn[k]])
                if oc < 6:
                    self.copy("act", self.xbr[oc], ps, out_ap=self.xbr[oc].ap[:, 3:TT + 3])
                elif oc < 12:
                    c = oc - 6
                    u = self.tmpf.next()
                    self.act_(AF.Square, u, ps)
                    self.ts("dve", u, u, 0.044715, 1.0, ALU.mult, ALU.add)
                    self.tt("dve", u, u, ps, ALU.mult)
                    self.act_(AF.Sigmoid, u, u, scale=1.5957691216057308)
                    self.tt("dve", self.cat[c], u, ps, ALU.mult)
                else:
                    c = oc - 12
                    self.headnorm(ps, ("memq_norm_g", 0), self.qmn[c], self.qmn[c].ap)
        one = self.ones12.ap[:, 0:1]
        for c in range(6):
            xb = self.xbr[c]
            acc = self.tmpf.next()
            cw = lambda tap: self.vcol("conv_w", tap * 6 + c)
            self.ts("dve", acc, xb, cw(0), self.vcol("conv_b", c), ALU.mult, ALU.add, in_ap=xb.ap[:, 0:TT],
                    extra_reads=[self.vecs])
            for tap in range(1, 4):
                self.stt(acc, xb, cw(tap), acc, ALU.mult, ALU.add, in0_ap=xb.ap[:, tap:tap + TT], extra_reads=[self.vecs])
            xcb = self.tmpb.next()
            self.copy("pool", xcb, acc)
            wrb, wr = self.wtile(("rg", c))
            wib, wi = self.wtile(("ig", c))
            pr = self.ps.next()
            self.mm(pr, pr.ap, wr[:, 0, :], xcb.ap, True, True, [wrb, xcb])
            pi = self.ps.next()
            self.mm(pi, pi.ap, wi[:, 0, :], xcb.ap, True, True, [wib, xcb])
            r = self.tmpf.next()
            self.act_(AF.Sigmoid, r, pr, bias=self.vcol("b_rg", c), extra_reads=[self.vecs])
            gi = self.tmpf.next()
            self.act_(AF.Sigmoid, gi, pi, bias=self.vcol("b_ig", c), extra_reads=[self.vecs])
            a = self.tmpf.next()
            self.act_(AF.Exp, a, r, scale=self.cL.ap[:, c:c + 1], extra_reads=[self.cL])
            m = self.tmpf.next()
            self.act_(AF.Exp, m, r, scale=self.cL.ap[:, 6 + c:7 + c], extra_reads=[self.cL])
            self.act_(AF.Sqrt, m, m, scale=-1.0, bias=one, extra_reads=[self.ones12])
            self.tt("pool", gi, gi, acc, ALU.mult)
            self.tt("dve", gi, gi, m, ALU.mult)
            hs = self.tmpf.next()
            hst = self.hstate[c]
            fw.op("dve", lambda e, hs=hs, a=a, gi=gi, hst=hst: e.tensor_tensor_scan(hs.ap, a.ap, gi.ap, hst.ap, ALU.mult, ALU.add),
                  reads=[a, gi, hst], writes=[hs])
            self.copy("pool", hst, hs, in_ap=hs.ap[:, TT - 1:TT])
            self.tt("dve", self.cat[c], hs, self.cat[c], ALU.mult)
        self.cross_attn(0)
        self.out_proj(0)

    Prog._prologue_mix = _prologue_mix
    Prog.cross_attn = cross_attn
    Prog.out_proj = out_proj
    Prog.mix_lru = mix_lru


_add_mixers()


def _add_fox():
    def mix_fox(self, t):
        fw = self.fw
        self.rmsnorm_fm(self.hT, ("mix_norm_g", 1), self.xn)
        for g in range(6):
            wb, w = self.wtile(("mi", 1, g))
            for fc in range(2):
                oc = 2 * g + fc
                ps = self.ps.next()
                for k in range(8):
                    self.mm(ps, ps.ap, w[:, k, fc * 128:(fc + 1) * 128], self.xn[k].ap, k == 0, k == 7, [wb, self.xn[k]])
                if oc < 6:
                    self.headnorm(ps, "fox_q_g", self.qT[oc], self.qT[oc].ap)
                else:
                    c = oc - 6
                    kn = self.tmpb.next()
                    self.headnorm(ps, "fox_k_g", kn, kn.ap)
                    kb = Buf(self.kc_d[c, :, t * TT:(t + 1) * TT], f"kc{c}_{t}")
                    self.kc_bufs[(c, t)] = kb
                    fw.dma("act", kb.ap, kn.ap, reads=[kn], writes=[kb])
        for g in range(3):
            wb, w = self.wtile(("mv", g))
            for tc in range(4):
                ps = self.ps.next()
                for k in range(8):
                    self.mm(ps, ps.ap[:, 0:256], self.xn[k].ap[:, tc * 128:(tc + 1) * 128], w[:, k, :], k == 0, k == 7,
                            [wb, self.xn[k]])
                self.copy("act" if tc % 2 else "dve", self.vtok, ps,
                          out_ap=self.vtok.ap[:, tc, g * 256:(g + 1) * 256], in_ap=ps.ap[:, 0:256])
        for c in range(6):
            vb = Buf(self.vc_d[c, :, 4 * t:4 * t + 4, :], f"vc{c}_{t}")
            self.vc_bufs[(c, t)] = vb
            fw.dma("act", vb.ap, self.vtok.ap[:, :, c * 128:(c + 1) * 128], reads=[self.vtok], writes=[vb])
        wb, w = self.wtile(("mf",))
        ps = self.ps.next()
        for k in range(8):
            self.mm(ps, ps.ap[0:12, :], w[:, k, 0:12], self.xn[k].ap, k == 0, k == 7, [wb, self.xn[k]])
        lf = self.tmpf.next()
        self.act_(AF.Exp, lf, ps, out_ap=lf.ap[0:12, :], in_ap=ps.ap[0:12, :], scale=-1.0, bias=self.negbf.ap[0:12, :],
                  extra_reads=[self.negbf])
        self.act_(AF.Ln, lf, lf, out_ap=lf.ap[0:12, :], in_ap=lf.ap[0:12, :], bias=self.ones12.ap[0:12, 0:1],
                  extra_reads=[self.ones12])
        fw.op("dve", lambda e, lf=lf: e.tensor_tensor_scan(self.cum.ap[0:12, :], self.ones12.ap[0:12, :], lf.ap[0:12, :],
                                                            self.cumstate.ap[0:12, :], ALU.mult, ALU.subtract),
              reads=[lf, self.ones12, self.cumstate], writes=[self.cum])
        self.copy("pool", self.cumstate, self.cum, out_ap=self.cumstate.ap[0:12, :], in_ap=self.cum.ap[0:12, TT - 1:TT])
        for tc in range(4):
            tp = self.ps.next()
            fw.op("pe", lambda e, tp=tp, tc=tc: e.transpose(tp.ap[:, 0:12], self.cum.ap[0:12, tc * 128:(tc + 1) * 128],
                                                           self.ident.ap[0:12, 0:12]),
                  reads=[self.cum, self.ident], writes=[tp])
            self.ts("dve", self.negcumT, tp, -1.0, None, ALU.mult, None,
                    out_ap=self.negcumT.ap[:, 4 * t + tc, :], in_ap=tp.ap[:, 0:12])
        wb, w = self.wtile(("mq",))
        for c in range(2):
            ps = self.ps.next()
            for k in range(8):
                self.mm(ps, ps.ap, w[:, k, c * 128:(c + 1) * 128], self.xn[k].ap, k == 0, k == 7, [wb, self.xn[k]])
            self.headnorm(ps, ("memq_norm_g", 1), self.qmn[c], self.qmn[c].ap)
        nkeys = (t + 1) * TT
        for c in range(6):
            o = self.psacc.next()
            den = self.psacc.next()
            for h2 in range(2):
                h = 2 * c + h2
                sel = self.tmpf.next()
                self.ts("dve", sel, self.cum, self.oh.ap[0:12, h:h + 1], None, ALU.mult, None,
                        out_ap=sel.ap[0:12, :], in_ap=self.cum.ap[0:12, :], extra_reads=[self.oh])
                pb = self.ps.next()
                self.mm1(pb, pb.ap, self.ones12.ap[0:12, 0:128], sel.ap[0:12, :], True, True, [self.ones12, sel])
                self.copy("act", self.cq[h2], pb)
            blocks = []
            kvb = {}
            for kb0 in range(0, nkeys, 1024):
                wk = min(1024, nkeys - kb0)
                for h2 in range(2):
                    for k8 in range(wk // 128):
                        blocks.append((kb0, wk, h2, k8, kb0 // 128 + k8))

            def kv_load(kb0, wk):
                if kb0 not in kvb:
                    kbuf = self.kst.next()
                    vbuf = self.vst.next()
                    tiles = range(kb0 // TT, (kb0 + wk) // TT)
                    fw.dma("sp", kbuf.ap[:, 0:wk], self.kc_d[c, :, kb0:kb0 + wk],
                           reads=[self.kc_bufs[(c, tt_)] for tt_ in tiles], writes=[kbuf])
                    fw.dma("sp", vbuf.ap[:, 0:wk // 128, :], self.vc_d[c, :, kb0 // 128:(kb0 + wk) // 128, :],
                           reads=[self.vc_bufs[(c, tt_)] for tt_ in tiles], writes=[vbuf])
                    kvb[kb0] = (kbuf, vbuf)
                return kvb[kb0]

            def emit_qk(blk):
                kb0, wk, h2, k8, kcg = blk
                kbuf, vbuf = kv_load(kb0, wk)
                h = 2 * c + h2
                b0 = 64 * h2
                diag = kcg - 4 * t
                q0 = 128 * diag if diag >= 0 else 0
                s = self.ps.next()
                self.mm1(s, s.ap[:, q0:TT], kbuf.ap[b0:b0 + 64, k8 * 128:(k8 + 1) * 128],
                         self.qT[c].ap[b0:b0 + 64, q0:TT], True, True, [kbuf, self.qT[c]])
                tm = self.tmpf.next()
                self.stt(tm, s, 0.125, self.cq[h2], ALU.mult, ALU.add, out_ap=tm.ap[:, q0:TT],
                         in0_ap=s.ap[:, q0:TT], in1_ap=self.cq[h2].ap[:, q0:TT])
                if diag >= 0:
                    self.tt("pool", tm, tm, self.tri, ALU.add, out_ap=tm.ap[:, q0:q0 + 128],
                            a_ap=tm.ap[:, q0:q0 + 128])
                p = self.tmpb.next()
                self.act_(AF.Exp, p, tm, out_ap=p.ap[:, q0:TT], in_ap=tm.ap[:, q0:TT],
                          bias=self.negcumT.ap[:, kcg, h:h + 1], extra_reads=[self.negcumT])
                return p, q0

            def emit_pv(blk, p, q0):
                kb0, wk, h2, k8, kcg = blk
                kbuf, vbuf = kvb[kb0]
                b0 = 64 * h2
                first = kcg == 0
                last = kcg == 4 * t + 3
                self.mm1(o, o.ap[b0:b0 + 64, q0:TT], vbuf.ap[:, k8, b0:b0 + 64], p.ap[:, q0:TT], first, last,
                         [vbuf, p])
                self.mm1(den, den.ap[b0:b0 + 64, q0:TT], self.onesblk.ap[:, 0:64], p.ap[:, q0:TT], first, last,
                         [self.onesblk, p])

            LA = 2
            pend = []
            for i in range(len(blocks) + LA):
                if i < len(blocks):
                    pend.append(emit_qk(blocks[i]))
                if i >= LA:
                    emit_pv(blocks[i - LA], *pend[i - LA])
            rec = self.tmpf.next()
            self.act_(AF.Ln, rec, den)
            self.act_(AF.Exp, rec, rec, scale=-1.0)
            self.tt("dve", self.cat[c], o, rec, ALU.mult)
        self.cross_attn(1)
        self.out_proj(1)

    Prog.mix_fox = mix_fox


_add_fox()
```

```python
import contextlib
import numpy as np
import concourse.bass as bass
import concourse.mybir as mybir
from concourse.bass_utils import run_bass_kernel_spmd

F32 = mybir.dt.float32
BF16 = mybir.dt.bfloat16
AF = mybir.ActivationFunctionType
ALU = mybir.AluOpType

D = 1024
S = 4096
DFF = 2816
NMEM = 256
HD = 64
MIXW = 768
KC = D // 128
FC = DFF // 128
EPS = 1e-6
SYNC_SAME = True


class Buf:
    __slots__ = ("ap", "lw", "rd", "name")

    def __init__(self, ap, name=""):
        self.ap = ap if isinstance(ap, bass.AP) else ap[:]
        self.lw = None
        self.rd = {}
        self.name = name


class Q:
    def __init__(self, name, sem, is_pe=False):
        self.name = name
        self.sem = sem
        self.is_pe = is_pe
        self.count = 0
        self.prog = []
        self.waited = {}
        self.dma_sems = []
        self.dma_vals = []
        self.dma_next = 0


class FW:
    def __init__(self, nc, stack, n_dma_sems=8):
        self.nc = nc
        self.q = {}
        for name, is_pe in (("pe", True), ("act", False), ("dve", False), ("pool", False), ("sp", False)):
            sem = stack.enter_context(nc.semaphore(f"prog_{name}"))
            self.q[name] = Q(name, sem, is_pe)
        for name in ("sp", "act", "pool"):
            q = self.q[name]
            for i in range(n_dma_sems):
                q.dma_sems.append(stack.enter_context(nc.semaphore(f"dma_{name}_{i}")))
                q.dma_vals.append(0)

    def _deps(self, q, reads, writes, extra=()):
        deps = {}

        def need(tok):
            if tok is None:
                return
            k = id(tok[0])
            if k not in deps or deps[k][1] < tok[1]:
                deps[k] = tok

        for b in reads:
            need(b.lw)
        for b in writes:
            need(b.lw)
            for tok in b.rd.values():
                need(tok)
        for tok in extra:
            need(tok)
        for k, (sem, val, owner) in deps.items():
            if owner is q and (q.is_pe or not SYNC_SAME):
                continue
            if q.waited.get(k, 0) >= val:
                continue
            q.waited[k] = val
            q.prog.append(("wait", sem, val))

    def op(self, qname, fn, reads=(), writes=(), signal=True):
        q = self.q[qname]
        self._deps(q, reads, writes)
        if signal:
            q.count += 1
            tok = (q.sem, q.count, q)
        else:
            tok = (q.sem, q.count + 1, q)
        q.prog.append(("op", fn, signal))
        for b in writes:
            b.lw = tok
            b.rd = {}
        for b in reads:
            b.rd[q.name] = tok

    def dma(self, qname, out_ap, in_ap, reads=(), writes=()):
        q = self.q[qname]
        i = q.dma_next
        q.dma_next = (i + 1) % len(q.dma_sems)
        sem = q.dma_sems[i]
        cur = q.dma_vals[i]
        extra = [(sem, cur, None)] if cur > 0 else []
        self._deps(q, reads, writes, extra)
        q.dma_vals[i] = cur + 16
        tok = (sem, cur + 16, None)
        q.prog.append(("dma", out_ap, in_ap, sem))
        for b in writes:
            b.lw = tok
            b.rd = {}
        for b in reads:
            b.rd[("dma", id(sem))] = tok

    def finish(self):
        for name in ("sp", "act", "pool"):
            q = self.q[name]
            for sem, val in zip(q.dma_sems, q.dma_vals):
                if val > 0 and q.waited.get(id(sem), 0) < val:
                    q.prog.append(("wait", sem, val))

    def replay(self, qname, eng):
        q = self.q[qname]
        for item in q.prog:
            if item[0] == "wait":
                eng.wait_ge(item[1], item[2])
            elif item[0] == "op":
                inst = item[1](eng)
                if item[2]:
                    inst.then_inc(q.sem, 1)
            else:
                eng.dma_start(out=item[1], in_=item[2]).then_inc(item[3], 16)


class Ring:
    def __init__(self, bufs):
        self.bufs = bufs
        self.i = 0

    def next(self):
        b = self.bufs[self.i]
        self.i = (self.i + 1) % len(self.bufs)
        return b


WT_ELEMS = 2048


def _tiles_of(W, col_ranges, nk_split):
    out = []
    Kin = W.shape[0]
    kcs = Kin // 128
    Wr = W.reshape(kcs, 128, W.shape[1])
    for (c0, nc_) in col_ranges:
        k0 = 0
        for nk in nk_split:
            t = Wr[k0:k0 + nk, :, c0:c0 + nc_]
            t = np.transpose(t, (1, 0, 2)).reshape(128, nk * nc_)
            out.append(t)
            k0 += nk
        assert k0 == kcs
    return out


class WeightPlan:
    def __init__(self):
        self.tiles = []
        self.arrays = []

    def add(self, arrs):
        idx0 = len(self.arrays)
        self.arrays.extend(arrs)
        return list(range(idx0, idx0 + len(arrs)))

    def pack(self):
        out = np.zeros((len(self.arrays), 128, WT_ELEMS), np.float32)
        for i, a in enumerate(self.arrays):
            out[i, :, :a.shape[1]] = a
        return out


def weight_specs():
    sp = []
    for L in range(2):
        for nm, src_in, src_out in (("f1", "ffn1_w_in", "ffn1_w_out"), ("f2", "ffn2_w_in", "ffn2_w_out")):
            for g in range(DFF // 256):
                sp.append(((nm + "i", L, "g", g), src_in, L, g * 256, 256, 0, 8))
                sp.append(((nm + "i", L, "u", g), src_in, L, DFF + g * 256, 256, 0, 8))
            for oc in range(8):
                sp.append(((nm + "o", L, oc, 0), src_out, L, oc * 128, 128, 0, 11))
                sp.append(((nm + "o", L, oc, 1), src_out, L, oc * 128, 128, 11, 11))
        for g in range(4):
            sp.append((("mo", L, g), "mix_w_out", L, g * 256, 256, 0, 8))
    for g in range(7):
        sp.append((("mi", 0, g), "lru_w_in", 0, g * 256, 256, 0, 8))
    for g in range(6):
        sp.append((("mi", 1, g), "fox_w_in", 0, g * 256, 256, 0, 8))
    for g in range(3):
        sp.append((("mv", g), "fox_w_in", 0, 1536 + g * 256, 256, 0, 8))
    sp.append((("mf",), "fox_w_in", 0, 2304, 12, 0, 8))
    sp.append((("mq",), "fox_w_in", 0, 2316, 256, 0, 8))
    for g in range(2):
        sp.append((("kv", g), "mem_w_kv", None, g * 256, 256, 0, 8))
    for c in range(6):
        sp.append((("rg", c), "lru_w_rg", 0, c, 0, 0, 1))
        sp.append((("ig", c), "lru_w_ig", 0, c, 0, 0, 1))
    return sp


def pack_weights(inp):
    sp = weight_specs()
    out = np.zeros((len(sp), 128, WT_ELEMS), np.float32)
    index = {}
    for i, (key, src, L, c0, ncols, k0, nk) in enumerate(sp):
        index[key] = i
        W = inp[src]
        if key[0] in ("rg", "ig"):
            blk = W[0]
            c = c0
            out[i, 0:64, 0:64] = blk[2 * c]
            out[i, 64:128, 64:128] = blk[2 * c + 1]
            continue
        if L is not None:
            W = W[L]
        Wr = W.reshape(W.shape[0] // 128, 128, W.shape[1])
        t = Wr[k0:k0 + nk, :, c0:c0 + ncols]
        out[i, :, :nk * ncols] = np.transpose(t, (1, 0, 2)).reshape(128, nk * ncols)
    return out, index


def vec_cols():
    cols = {}
    n = 0
    for L in range(2):
        for nm in ("ffn1_norm_g", "mix_norm_g", "ffn2_norm_g"):
            cols[(nm, L)] = n
            n += 8
    cols["mem_norm_g"] = n; n += 8
    cols["mem_k_norm_g"] = n; n += 1
    cols[("memq_norm_g", 0)] = n; n += 1
    cols[("memq_norm_g", 1)] = n; n += 1
    cols["conv_w"] = n; n += 24
    cols["conv_b"] = n; n += 6
    cols["b_rg"] = n; n += 6
    cols["b_ig"] = n; n += 6
    cols["lam"] = n; n += 6
    cols["b_f"] = n; n += 1
    cols["fox_q_g"] = n; n += 1
    cols["fox_k_g"] = n; n += 1
    cols["_n"] = n
    return cols


def pack_vecs(inp):
    cols = vec_cols()
    v = np.zeros((128, cols["_n"]), np.float32)

    def fm(a):
        return a.reshape(-1, 128).T

    for L in range(2):
        for nm in ("ffn1_norm_g", "mix_norm_g", "ffn2_norm_g"):
            v[:, cols[(nm, L)]:cols[(nm, L)] + 8] = fm(inp[nm][L])
    v[:, cols["mem_norm_g"]:cols["mem_norm_g"] + 8] = fm(inp["mem_norm_g"])
    rep = lambda a: np.concatenate([a, a])
    v[:, cols["mem_k_norm_g"]] = rep(inp["mem_k_norm_g"])
    for L in range(2):
        v[:, cols[("memq_norm_g", L)]] = rep(inp["memq_norm_g"][L])
    for tap in range(4):
        v[:, cols["conv_w"] + tap * 6: cols["conv_w"] + tap * 6 + 6] = fm(inp["lru_conv_w"][0, tap])
    v[:, cols["conv_b"]:cols["conv_b"] + 6] = fm(inp["lru_conv_b"][0])
    v[:, cols["b_rg"]:cols["b_rg"] + 6] = fm(inp["lru_b_rg"][0])
    v[:, cols["b_ig"]:cols["b_ig"] + 6] = fm(inp["lru_b_ig"][0])
    v[:, cols["lam"]:cols["lam"] + 6] = fm(inp["lru_lambda"][0])
    v[0:12, cols["b_f"]] = inp["fox_b_f"][0]
    v[:, cols["fox_q_g"]] = rep(inp["fox_q_norm_g"][0])
    v[:, cols["fox_k_g"]] = rep(inp["fox_k_norm_g"][0])
    return v


NEG = -30000.0


def make_consts():
    c = {}
    c["ident"] = np.eye(128, dtype=np.float32)
    ones = np.ones((128, 128), np.float32)
    blk = np.zeros((128, 128), np.float32)
    blk[0:64, 0:64] = 1.0
    blk[64:128, 64:128] = 1.0
    c["ones_blk"] = np.concatenate([ones, blk], axis=1)
    md = np.zeros((128, 4, 512), np.float32)
    for kk in range(4):
        key = kk * 128 + np.arange(128)[:, None]
        qry = np.arange(512)[None, :]
        md[:, kk, :] = np.where(key <= qry, 0.0, NEG)
    c["maskd"] = md.reshape(128, 2048)
    sel = np.zeros((128, 12, 128), np.float32)
    for h in range(12):
        sel[h, h, :] = 1.0
    c["sel"] = sel.reshape(128, 12 * 128)
    return c


TT = 512
NCONST_VEC = None


class Prog:
    def __init__(self, n_tiles=S // TT, stages=("ffn1a", "mix0", "ffn2a", "ffn1b", "mix1", "ffn2b"), dbg=False):
        self.n_tiles = n_tiles
        self.stages = stages
        self.dbg = dbg
        self.stack = contextlib.ExitStack()
        nc = self.nc = bass.Bass("TRN2", target_bir_lowering=False)
        self.fw = FW(nc, self.stack)
        self.vc = vec_cols()
        self.widx = {k[0]: i for i, k in enumerate(weight_specs())}
        self.wspec = {k[0]: k for k in weight_specs()}
        nW = len(self.widx)
        dt = nc.dram_tensor
        self.x_d = dt("x", [S, D], F32, kind="ExternalInput").ap()
        self.mem_d = dt("mem", [NMEM, D], F32, kind="ExternalInput").ap()
        self.wts_d = dt("wts", [nW, 128, WT_ELEMS], F32, kind="ExternalInput").ap()
        self.vecs_d = dt("vecs", [128, self.vc["_n"]], F32, kind="ExternalInput").ap()
        self.ident_d = dt("ident", [128, 128], F32, kind="ExternalInput").ap()
        self.onesblk_d = dt("ones_blk", [128, 256], F32, kind="ExternalInput").ap()
        self.tri_d = dt("tri", [128, 128], F32, kind="ExternalInput").ap()
        self.oh_d = dt("onehot", [128, 16], F32, kind="ExternalInput").ap()
        self.swap_d = dt("swapm", [128, 128], F32, kind="ExternalInput").ap()
        self.out_d = dt("out", [S, D], F32, kind="ExternalOutput").ap()
        self.kc_d = dt("kcache", [6, 128, S], BF16, kind="Internal").ap()
        self.vc_d = dt("vcache", [6, 128, S // 128, 256], BF16, kind="Internal").ap()
        self.wbf_d = dt("wbf16", [nW, 128, WT_ELEMS], BF16, kind="Internal").ap()
        self.wconv = {}
        self.ncast = 0
        if dbg:
            self.dbg_d = dt("dbg", [128, 8 * 512], F32, kind="ExternalOutput").ap()
        self.kc_bufs = {}
        self.vc_bufs = {}
        self._alloc()
        self._prologue()
        for t in range(n_tiles):
            self._tile(t)
        self.fw.finish()
        self._emit()

    def sb(self, name, shape, dtype):
        return self.stack.enter_context(self.nc.sbuf_tensor("sb_" + name, shape, dtype))

    def _alloc(self):
        nc = self.nc
        sb = self.sb
        B = Buf
        self.ident = B(sb("ident", [128, 128], F32))
        self.onesblk = B(sb("onesblk", [128, 256], BF16))
        self.tri = B(sb("tri", [128, 128], F32))
        self.oh = B(sb("oh", [128, 16], F32))
        self.swapm = B(sb("swapm", [128, 128], F32))
        self.ones12 = B(sb("ones12", [128, 512], F32))
        self.vecs = B(sb("vecs", [128, self.vc["_n"]], F32))
        self.cL = B(sb("cL", [128, 12], F32))
        self.negbf = B(sb("negbf", [128, 1], F32))
        self.epsb = B(sb("epsb", [128, 1], F32))
        hT = sb("hT", [128, 8, TT], F32)
        self.hT = [B(hT[:, k, :], f"hT{k}") for k in range(8)]
        xn = sb("xn", [128, 8, TT], BF16)
        self.xn = [B(xn[:, k, :], f"xn{k}") for k in range(8)]
        act = sb("act", [128, FC, TT], BF16)
        self.act = [B(act[:, k, :], f"act{k}") for k in range(FC)]
        cat = sb("cat", [128, 8, TT], BF16)
        self.cat = [B(cat[:, k, :], f"cat{k}") for k in range(8)]
        self.wstage = Ring([B(sb(f"wst{i}", [128, WT_ELEMS], F32), f"wst{i}") for i in range(2)])
        self.wbf = Ring([B(sb(f"wbf{i}", [128, WT_ELEMS], BF16), f"wbf{i}") for i in range(6)])
        self.xin = B(sb("xin", [128, 4, D], F32), "xin")
        self.yout = B(sb("yout", [128, 4, D], F32), "yout") if False else self.xin
        self.mkT = B(sb("mkT", [128, 2, NMEM], BF16), "mkT")
        self.mv = B(sb("mv", [128, 2, 256], BF16), "mv")
        self.tmpf = Ring([B(sb(f"tmpf{i}", [128, TT], F32), f"tmpf{i}") for i in range(6)])
        self.tmpb = Ring([B(sb(f"tmpb{i}", [128, TT], BF16), f"tmpb{i}") for i in range(4)])
        xbr = sb("xbr", [128, 6, TT + 4], F32)
        self.xbr_t = xbr
        self.xbr = [B(xbr[:, c, :], f"xbr{c}") for c in range(6)]
        self.hstate = [B(sb(f"hstate{c}", [128, 1], F32), f"hstate{c}") for c in range(6)]
        qT = sb("qT", [128, 6, TT], BF16)
        self.qT = [B(qT[:, c, :], f"qT{c}") for c in range(6)]
        self.kst = Ring([B(sb(f"kst{i}", [128, 1024], BF16), f"kst{i}") for i in range(3)])
        self.vst = Ring([B(sb(f"vst{i}", [128, 8, 256], BF16), f"vst{i}") for i in range(3)])
        self.cum = B(sb("cum", [128, TT], F32), "cum")
        self.cumstate = B(sb("cumstate", [128, 1], F32), "cumstate")
        self.negcumT = B(sb("negcumT", [128, S // 128, 12], F32), "negcumT")
        self.vtok = B(sb("vtok", [128, 4, 6, 256], BF16), "vtok")
        self.ps = Ring([B(self.stack.enter_context(nc.psum_tensor(f"ps{i}", [128, 512], F32)), f"ps{i}")
                        for i in range(4)])
        self.psacc = Ring([B(self.stack.enter_context(nc.psum_tensor(f"pa{i}", [128, 512], F32)), f"pa{i}")
                           for i in range(4)])
        qmn = sb("qmn", [128, 2, TT], BF16)
        self.qmn = [B(qmn[:, c, :], f"qmn{c}") for c in range(2)]
        self.cq = [B(sb(f"cq{i}", [128, TT], F32), f"cq{i}") for i in range(2)]

    def vcol(self, key, off=0, n=1, p0=0, p1=128):
        c = self.vc[key] + off
        return self.vecs.ap[p0:p1, c:c + n]

    def wtile(self, key):
        fw = self.fw
        _, src, L, c0, ncols, k0, nk = self.wspec[key]
        if key[0] in ("rg", "ig"):
            nk, ncols = 1, 128
        n = nk * ncols
        idx = self.widx[key]
        wb = self.wbf.next()
        if key in self.wconv:
            db = self.wconv[key]
            fw.dma("sp", wb.ap[:, 0:n], db.ap, reads=[db], writes=[wb])
        else:
            st = self.wstage.next()
            fw.dma("sp", st.ap[:, 0:n], self.wts_d[idx, :, 0:n], writes=[st])
            eng = "act"
            self.ncast += 1
            self.copy(eng, wb, st, out_ap=wb.ap[:, 0:n], in_ap=st.ap[:, 0:n])
            if key[0] != "kv" and self.n_tiles > 1:
                db = Buf(self.wbf_d[idx, :, 0:n], "wd")
                self.wconv[key] = db
                fw.dma(eng, db.ap, wb.ap[:, 0:n], reads=[wb], writes=[db])
        return wb, wb.ap[:, 0:n].rearrange("p (k n) -> p k n", k=nk)

    def mm(self, ps, out_ap, lhsT, rhs, start, stop, reads, signal=None):
        self.fw.op("pe", lambda e: e.matmul(out_ap, lhsT, rhs, start=start, stop=stop),
                   reads=reads, writes=[ps], signal=(stop if signal is None else signal))

    def mm1(self, ps, out_ap, lhsT, rhs, start, stop, reads):
        self.fw.op("pe", lambda e: e.matmul(out_ap, lhsT, rhs, start=start, stop=stop),
                   reads=reads, writes=[ps], signal=True)

    def act_(self, func, out_b, in_b, out_ap=None, in_ap=None, extra_reads=(), **kw):
        o = out_b.ap if out_ap is None else out_ap
        i = in_b.ap if in_ap is None else in_ap
        self.fw.op("act", lambda e: e.activation(o, i, func, **kw), reads=[in_b] + list(extra_reads), writes=[out_b])

    def tt(self, eng, out_b, a_b, b_b, op, out_ap=None, a_ap=None, b_ap=None):
        o = out_b.ap if out_ap is None else out_ap
        a = a_b.ap if a_ap is None else a_ap
        b = b_b.ap if b_ap is None else b_ap
        self.fw.op(eng, lambda e: e.tensor_tensor(o, a, b, op), reads=[a_b, b_b], writes=[out_b])

    def ts(self, eng, out_b, in_b, s1, s2, op0, op1, out_ap=None, in_ap=None, extra_reads=()):
        o = out_b.ap if out_ap is None else out_ap
        i = in_b.ap if in_ap is None else in_ap
        if op1 is None:
            fn = lambda e: e.tensor_scalar(o, i, s1, None, op0)
        else:
            fn = lambda e: e.tensor_scalar(o, i, s1, s2, op0, op1)
        self.fw.op(eng, fn, reads=[in_b] + list(extra_reads), writes=[out_b])

    def stt(self, out_b, in0_b, scalar, in1_b, op0, op1, out_ap=None, in0_ap=None, in1_ap=None, extra_reads=()):
        o = out_b.ap if out_ap is None else out_ap
        a = in0_b.ap if in0_ap is None else in0_ap
        b = in1_b.ap if in1_ap is None else in1_ap
        self.fw.op("dve", lambda e: e.scalar_tensor_tensor(o, a, scalar, b, op0, op1),
                   reads=[in0_b, in1_b] + list(extra_reads), writes=[out_b])

    def copy(self, eng, out_b, in_b, out_ap=None, in_ap=None):
        o = out_b.ap if out_ap is None else out_ap
        i = in_b.ap if in_ap is None else in_ap
        if eng == "act":
            fn = lambda e: e.copy(o, i)
        else:
            fn = lambda e: e.tensor_copy(o, i)
        self.fw.op(eng, fn, reads=[in_b], writes=[out_b])

    def rmsnorm_fm(self, src, gkey, dst, n=TT):
        ps = self.ps.next()
        for k in range(8):
            sq = self.tmpb.next()
            self.act_(AF.Square, sq, src[k], out_ap=sq.ap[:, 0:n], in_ap=src[k].ap[:, 0:n])
            self.mm(ps, ps.ap[:, 0:n], self.onesblk.ap[:, 0:128], sq.ap[:, 0:n], k == 0, k == 7, [sq, self.onesblk],
                    signal=True)
        rstd = self.tmpf.next()
        self.act_(AF.Ln, rstd, ps, out_ap=rstd.ap[:, 0:n], in_ap=ps.ap[:, 0:n], bias=self.epsb.ap[:, 0:1],
                  scale=1.0 / D, extra_reads=[self.epsb])
        self.act_(AF.Exp, rstd, rstd, out_ap=rstd.ap[:, 0:n], in_ap=rstd.ap[:, 0:n], scale=-0.5)
        for k in range(8):
            self.stt(dst[k], src[k], self.vcol(gkey, k), rstd, ALU.mult, ALU.mult,
                     out_ap=dst[k].ap[:, 0:n], in0_ap=src[k].ap[:, 0:n], in1_ap=rstd.ap[:, 0:n],
                     extra_reads=[self.vecs])

    def headnorm(self, ps, gkey, out_b, out_ap, n=TT):
        sq = self.tmpb.next()
        self.act_(AF.Square, sq, ps, out_ap=sq.ap[:, 0:n], in_ap=ps.ap[:, 0:n])
        pn = self.ps.next()
        self.mm(pn, pn.ap[:, 0:n], self.onesblk.ap[:, 128:256], sq.ap[:, 0:n], True, True, [sq, self.onesblk])
        rstd = self.tmpf.next()
        self.act_(AF.Ln, rstd, pn, out_ap=rstd.ap[:, 0:n], in_ap=pn.ap[:, 0:n], bias=self.epsb.ap[:, 0:1],
                  scale=1.0 / HD, extra_reads=[self.epsb])
        self.act_(AF.Exp, rstd, rstd, out_ap=rstd.ap[:, 0:n], in_ap=rstd.ap[:, 0:n], scale=-0.5)
        self.stt(out_b, ps, self.vcol(gkey), rstd, ALU.mult, ALU.mult,
                 out_ap=out_ap, in0_ap=ps.ap[:, 0:n], in1_ap=rstd.ap[:, 0:n], extra_reads=[self.vecs])

    def load_x(self, t):
        fw = self.fw
        fw.dma("act", self.xin.ap, self.x_d[t * TT:(t + 1) * TT, :].rearrange("(c p) d -> p c d", p=128),
               writes=[self.xin])
        for k in range(8):
            ps = self.ps.next()
            for tc in range(4):
                o = ps.ap[:, tc * 128:(tc + 1) * 128]
                i = self.xin.ap[:, tc, k * 128:(k + 1) * 128]
                fw.op("pe", lambda e, o=o, i=i: e.transpose(o, i, self.ident.ap),
                      reads=[self.xin, self.ident], writes=[ps], signal=(tc == 3))
            self.copy("act" if k % 2 else "dve", self.hT[k], ps)

    def store_out(self, t):
        fw = self.fw
        for tc in range(4):
            for half in range(2):
                ps = self.ps.next()
                for kk in range(4):
                    k = half * 4 + kk
                    o = ps.ap[:, kk * 128:(kk + 1) * 128]
                    i = self.hT[k].ap[:, tc * 128:(tc + 1) * 128]
                    fw.op("pe", lambda e, o=o, i=i: e.transpose(o, i, self.ident.ap),
                          reads=[self.hT[k], self.ident], writes=[ps], signal=(kk == 3))
                self.copy("act" if half else "dve", self.xin, ps,
                          out_ap=self.xin.ap[:, tc, half * 512:(half + 1) * 512])
        fw.dma("act", self.out_d[t * TT:(t + 1) * TT, :].rearrange("(c p) d -> p c d", p=128), self.xin.ap,
               reads=[self.xin])

    def ffn(self, L, nm, gname):
        fw = self.fw
        self.rmsnorm_fm(self.hT, (gname, L), self.xn)
        for g in range(DFF // 256):
            wg_b, wg = self.wtile((nm + "i", L, "g", g))
            wu_b, wu = self.wtile((nm + "i", L, "u", g))
            for fc in range(2):
                f = g * 2 + fc
                pg = self.ps.next()
                pu = self.ps.next()
                for k in range(8):
                    self.mm(pg, pg.ap, wg[:, k, fc * 128:(fc + 1) * 128], self.xn[k].ap, k == 0, k == 7,
                            [wg_b, self.xn[k]])
                for k in range(8):
                    self.mm(pu, pu.ap, wu[:, k, fc * 128:(fc + 1) * 128], self.xn[k].ap, k == 0, k == 7,
                            [wu_b, self.xn[k]])
                sg = self.tmpf.next()
                self.act_(AF.Silu, sg, pg)
                self.tt("dve", self.act[f], sg, pu, ALU.mult)
        for oc in range(8):
            w0b, w0 = self.wtile((nm + "o", L, oc, 0))
            w1b, w1 = self.wtile((nm + "o", L, oc, 1))
            py = self.ps.next()
            for k in range(FC):
                wb, w = (w0b, w0) if k < 11 else (w1b, w1)
                self.mm(py, py.ap, w[:, k % 11, :], self.act[k].ap, k == 0, k == FC - 1, [wb, self.act[k]])
            self.stt(self.hT[oc], py, 0.5, self.hT[oc], ALU.mult, ALU.add)

    def _prologue(self):
        fw = self.fw
        fw.dma("act", self.ident.ap, self.ident_d, writes=[self.ident])
        fw.dma("act", self.tri.ap, self.tri_d, writes=[self.tri])
        fw.dma("act", self.oh.ap, self.oh_d, writes=[self.oh])
        fw.dma("act", self.swapm.ap, self.swap_d, writes=[self.swapm])
        fw.dma("act", self.vecs.ap, self.vecs_d, writes=[self.vecs])
        st = self.tmpf.next()
        fw.dma("act", st.ap[:, 0:256], self.onesblk_d, writes=[st])
        self.copy("dve", self.onesblk, st, in_ap=st.ap[:, 0:256])
        fw.op("dve", lambda e: e.memset(self.epsb.ap, EPS), writes=[self.epsb])
        fw.op("dve", lambda e: e.memset(self.ones12.ap, 1.0), writes=[self.ones12])
        if "mix0" in self.stages or "mix1" in self.stages:
            self._prologue_mix()

    def _tile(self, t):
        st = self.stages
        self.load_x(t)
        if "ffn1a" in st:
            self.ffn(0, "f1", "ffn1_norm_g")
        if "mix0" in st:
            self.mix_lru(t)
        if "ffn2a" in st:
            self.ffn(0, "f2", "ffn2_norm_g")
        if "ffn1b" in st:
            self.ffn(1, "f1", "ffn1_norm_g")
        if "mix1" in st:
            self.mix_fox(t)
        if "ffn2b" in st:
            self.ffn(1, "f2", "ffn2_norm_g")
        self.store_out(t)

    def _emit(self):
        nc = self.nc
        fw = self.fw
        with nc.Block() as block:
            @block.tensor
            def _(e):
                fw.replay("pe", e)

            @block.scalar
            def _(e):
                fw.replay("act", e)

            @block.vector
            def _(e):
                fw.replay("dve", e)

            @block.gpsimd
            def _(e):
                fw.replay("pool", e)

            @block.sync
            def _(e):
                fw.replay("sp", e)
        self.stack.close()


_CACHE = {}


def host_inputs(inp):
    wts, _ = pack_weights(inp)
    vecs = pack_vecs(inp)
    c = make_consts()
    tri = np.where(np.arange(128)[:, None] <= np.arange(128)[None, :], 0.0, NEG).astype(np.float32)
    oh = np.zeros((128, 16), np.float32)
    for h in range(12):
        oh[h, h] = 1.0
    sw = np.zeros((128, 128), np.float32)
    for k in range(128):
        sw[k, (k + 64) % 128] = 1.0
    shared = {"wts": wts, "vecs": vecs, "ident": c["ident"], "ones_blk": c["ones_blk"], "tri": tri, "onehot": oh,
              "swapm": sw}
    return shared


def kernel(**inputs):
    inp = {k: np.asarray(v) for k, v in inputs.items()}
    shared = host_inputs(inp)
    if "prog" not in _CACHE:
        _CACHE["prog"] = Prog()
    prog = _CACHE["prog"]
    x = np.ascontiguousarray(inp["x"], dtype=np.float32)
    mem = np.ascontiguousarray(inp["mem"], dtype=np.float32)
    in_maps = []
    for b in range(8):
        m = dict(shared)
        m["x"] = x[b]
        m["mem"] = mem[b]
        in_maps.append(m)
    res = run_bass_kernel_spmd(prog.nc, in_maps, core_ids=list(range(8)))
    out = np.stack([np.asarray(r["out"], dtype=np.float32).reshape(S, D) for r in res.results], axis=0)
    return out


def _add_mixers():
    def _prologue_mix(self):
        fw = self.fw
        t = self.tmpf.next()
        lam = self.vecs.ap[:, self.vc["lam"]:self.vc["lam"] + 6]
        one = self.ones12.ap[:, 0:1]
        fw.op("act", lambda e: e.activation(t.ap[:, 0:6], lam, AF.Exp, scale=-1.0), reads=[self.vecs], writes=[t])
        fw.op("act", lambda e: e.activation(t.ap[:, 0:6], t.ap[:, 0:6], AF.Ln, bias=one), reads=[t, self.ones12], writes=[t])
        self.ts("dve", self.cL, t, -8.0, None, ALU.mult, None, out_ap=self.cL.ap[:, 0:6], in_ap=t.ap[:, 0:6])
        self.ts("dve", self.cL, t, -16.0, None, ALU.mult, None, out_ap=self.cL.ap[:, 6:12], in_ap=t.ap[:, 0:6])
        self.ts("dve", self.negbf, self.vecs, -1.0, None, ALU.mult, None, in_ap=self.vcol("b_f"))
        for c in range(6):
            fw.op("dve", lambda e, a=self.hstate[c].ap: e.memset(a, 0.0), writes=[self.hstate[c]])
            fw.op("dve", lambda e, a=self.xbr[c].ap[:, 0:4]: e.memset(a, 0.0), writes=[self.xbr[c]])
        fw.op("dve", lambda e: e.memset(self.cumstate.ap, 0.0), writes=[self.cumstate])
        fw.op("pool", lambda e: e.memset(self.vtok.ap, 1.0), writes=[self.vtok])
        fw.dma("act", self.xin.ap[:, 0:2, :], self.mem_d.rearrange("(c p) d -> p c d", p=128), writes=[self.xin])
        for k in range(8):
            ps = self.ps.next()
            for tc in range(2):
                o = ps.ap[:, tc * 128:(tc + 1) * 128]
                i = self.xin.ap[:, tc, k * 128:(k + 1) * 128]
                fw.op("pe", lambda e, o=o, i=i: e.transpose(o, i, self.ident.ap),
                      reads=[self.xin, self.ident], writes=[ps], signal=(tc == 1))
            self.copy("act" if k % 2 else "dve", self.hT[k], ps, out_ap=self.hT[k].ap[:, 0:256], in_ap=ps.ap[:, 0:256])
        self.rmsnorm_fm(self.hT, "mem_norm_g", self.xn, n=NMEM)
        wkb, wk = self.wtile(("kv", 0))
        wvb, wv = self.wtile(("kv", 1))
        for c in range(2):
            ps = self.ps.next()
            for k in range(8):
                self.mm(ps, ps.ap[:, 0:NMEM], wk[:, k, c * 128:(c + 1) * 128], self.xn[k].ap[:, 0:NMEM], k == 0, k == 7,
                        [wkb, self.xn[k]])
            self.headnorm(ps, "mem_k_norm_g", self.mkT, self.mkT.ap[:, c, :], n=NMEM)
        for nch in range(2):
            ps = self.ps.next()
            for k in range(8):
                self.mm(ps, ps.ap[:, 0:256], self.xn[k].ap[:, nch * 128:(nch + 1) * 128], wv[:, k, :], k == 0, k == 7,
                        [wvb, self.xn[k]])
            self.copy("act", self.mv, ps, out_ap=self.mv.ap[:, nch, :], in_ap=ps.ap[:, 0:256])

    def cross_attn(self, L):
        for c in range(2):
            o = self.psacc.next()
            den = self.psacc.next()
            for h2 in range(2):
                h = 2 * c + h2
                b0 = 64 * h2
                for nch in range(2):
                    s = self.ps.next()
                    self.mm1(s, s.ap, self.mkT.ap[b0:b0 + 64, c, nch * 128:(nch + 1) * 128],
                             self.qmn[c].ap[b0:b0 + 64, :], True, True, [self.mkT, self.qmn[c]])
                    e_ = self.tmpb.next()
                    self.act_(AF.Exp, e_, s, scale=0.125)
                    self.mm1(o, o.ap[b0:b0 + 64, :], self.mv.ap[:, nch, h * 64:(h + 1) * 64], e_.ap,
                             nch == 0, nch == 1, [self.mv, e_])
                    self.mm1(den, den.ap[b0:b0 + 64, :], self.onesblk.ap[:, 0:64], e_.ap,
                             nch == 0, nch == 1, [self.onesblk, e_])
            rec = self.tmpf.next()
            self.act_(AF.Ln, rec, den)
            self.act_(AF.Exp, rec, rec, scale=-1.0)
            self.tt("dve", self.cat[6 + c], o, rec, ALU.mult)

    def out_proj(self, L):
        for g in range(4):
            wb, w = self.wtile(("mo", L, g))
            for fc in range(2):
                oc = 2 * g + fc
                ps = self.ps.next()
                for k in range(8):
                    self.mm(ps, ps.ap, w[:, k, fc * 128:(fc + 1) * 128], self.cat[k].ap, k == 0, k == 7,
                            [wb, self.cat[k]])
                self.tt("dve", self.hT[oc], ps, self.hT[oc], ALU.add)

    def mix_lru(self, t):
        fw = self.fw
        self.rmsnorm_fm(self.hT, ("mix_norm_g", 0), self.xn)
        if t > 0:
            for c in range(6):
                b = self.xbr[c]
                fw.op("pool", lambda e, b=b: e.tensor_copy(b.ap[:, 0:3], b.ap[:, TT:TT + 3]), reads=[b], writes=[b])
        for g in range(7):
            wb, w = self.wtile(("mi", 0, g))
            for fc in range(2):
                oc = 2 * g + fc
                ps = self.ps.next()
                for k in range(8):
                    self.mm(ps, ps.ap, w[:, k, fc * 128:(fc + 1) * 128], self.xn[k].ap, k == 0, k == 7, [wb, self.xn[k]])
                if oc < 6:
                    self.copy("act", self.xbr[oc], ps, out_ap=self.xbr[oc].ap[:, 3:TT + 3])
                elif oc < 12:
                    c = oc - 6
                    u = self.tmpf.next()
                    self.act_(AF.Square, u, ps)
                    self.ts("dve", u, u, 0.044715, 1.0, ALU.mult, ALU.add)
                    self.tt("dve", u, u, ps, ALU.mult)
                    self.act_(AF.Sigmoid, u, u, scale=1.5957691216057308)
                    self.tt("dve", self.cat[c], u, ps, ALU.mult)
                else:
                    c = oc - 12
                    self.headnorm(ps, ("memq_norm_g", 0), self.qmn[c], self.qmn[c].ap)
        one = self.ones12.ap[:, 0:1]
        for c in range(6):
            xb = self.xbr[c]
            acc = self.tmpf.next()
            cw = lambda tap: self.vcol("conv_w", tap * 6 + c)
            self.ts("dve", acc, xb, cw(0), self.vcol("conv_b", c), ALU.mult, ALU.add, in_ap=xb.ap[:, 0:TT],
                    extra_reads=[self.vecs])
            for tap in range(1, 4):
                self.stt(acc, xb, cw(tap), acc, ALU.mult, ALU.add, in0_ap=xb.ap[:, tap:tap + TT], extra_reads=[self.vecs])
            xcb = self.tmpb.next()
            self.copy("pool", xcb, acc)
            wrb, wr = self.wtile(("rg", c))
            wib, wi = self.wtile(("ig", c))
            pr = self.ps.next()
            self.mm(pr, pr.ap, wr[:, 0, :], xcb.ap, True, True, [wrb, xcb])
            pi = self.ps.next()
            self.mm(pi, pi.ap, wi[:, 0, :], xcb.ap, True, True, [wib, xcb])
            r = self.tmpf.next()
            self.act_(AF.Sigmoid, r, pr, bias=self.vcol("b_rg", c), extra_reads=[self.vecs])
            gi = self.tmpf.next()
            self.act_(AF.Sigmoid, gi, pi, bias=self.vcol("b_ig", c), extra_reads=[self.vecs])
            a = self.tmpf.next()
            self.act_(AF.Exp, a, r, scale=self.cL.ap[:, c:c + 1], extra_reads=[self.cL])
            m = self.tmpf.next()
            self.act_(AF.Exp, m, r, scale=self.cL.ap[:, 6 + c:7 + c], extra_reads=[self.cL])
            self.act_(AF.Sqrt, m, m, scale=-1.0, bias=one, extra_reads=[self.ones12])
            self.tt("pool", gi, gi, acc, ALU.mult)
            self.tt("dve", gi, gi, m, ALU.mult)
            hs = self.tmpf.next()
            hst = self.hstate[c]
            fw.op("dve", lambda e, hs=hs, a=a, gi=gi, hst=hst: e.tensor_tensor_scan(hs.ap, a.ap, gi.ap, hst.ap, ALU.mult, ALU.add),
                  reads=[a, gi, hst], writes=[hs])
            self.copy("pool", hst, hs, in_ap=hs.ap[:, TT - 1:TT])
            self.tt("dve", self.cat[c], hs, self.cat[c], ALU.mult)
        self.cross_attn(0)
        self.out_proj(0)

    Prog._prologue_mix = _prologue_mix
    Prog.cross_attn = cross_attn
    Prog.out_proj = out_proj
    Prog.mix_lru = mix_lru


_add_mixers()


def _add_fox():
    def mix_fox(self, t):
        fw = self.fw
        self.rmsnorm_fm(self.hT, ("mix_norm_g", 1), self.xn)
        for g in range(6):
            wb, w = self.wtile(("mi", 1, g))
            for fc in range(2):
                oc = 2 * g + fc
                ps = self.ps.next()
                for k in range(8):
                    self.mm(ps, ps.ap, w[:, k, fc * 128:(fc + 1) * 128], self.xn[k].ap, k == 0, k == 7, [wb, self.xn[k]])
                if oc < 6:
                    self.headnorm(ps, "fox_q_g", self.qT[oc], self.qT[oc].ap)
                else:
                    c = oc - 6
                    kn = self.tmpb.next()
                    self.headnorm(ps, "fox_k_g", kn, kn.ap)
                    kb = Buf(self.kc_d[c, :, t * TT:(t + 1) * TT], f"kc{c}_{t}")
                    self.kc_bufs[(c, t)] = kb
                    fw.dma("act", kb.ap, kn.ap, reads=[kn], writes=[kb])
        for g in range(3):
            wb, w = self.wtile(("mv", g))
            for tc in range(4):
                ps = self.ps.next()
                for k in range(8):
                    self.mm(ps, ps.ap[:, 0:256], self.xn[k].ap[:, tc * 128:(tc + 1) * 128], w[:, k, :], k == 0, k == 7,
                            [wb, self.xn[k]])
                pv4 = ps.ap[:, 0:256].rearrange("p (c h d) -> p c h d", c=2, h=2)
                self.copy("act", self.vtok, ps, out_ap=self.vtok.ap[:, tc, 2 * g:2 * g + 2, 0:64], in_ap=pv4[:, :, 0, :])
                self.copy("dve", self.vtok, ps, out_ap=self.vtok.ap[:, tc, 2 * g:2 * g + 2, 192:256], in_ap=pv4[:, :, 1, :])
        for c in range(6):
            vb = Buf(self.vc_d[c, :, 4 * t:4 * t + 4, :], f"vc{c}_{t}")
            self.vc_bufs[(c, t)] = vb
            fw.dma("act", vb.ap, self.vtok.ap[:, :, c, :], reads=[self.vtok], writes=[vb])
        wb, w = self.wtile(("mf",))
        ps = self.ps.next()
        for k in range(8):
            self.mm(ps, ps.ap[0:12, :], w[:, k, 0:12], self.xn[k].ap, k == 0, k == 7, [wb, self.xn[k]])
        lf = self.tmpf.next()
        self.act_(AF.Exp, lf, ps, out_ap=lf.ap[0:12, :], in_ap=ps.ap[0:12, :], scale=-1.0, bias=self.negbf.ap[0:12, :],
                  extra_reads=[self.negbf])
        self.act_(AF.Ln, lf, lf, out_ap=lf.ap[0:12, :], in_ap=lf.ap[0:12, :], bias=self.ones12.ap[0:12, 0:1],
                  extra_reads=[self.ones12])
        fw.op("dve", lambda e, lf=lf: e.tensor_tensor_scan(self.cum.ap[0:12, :], self.ones12.ap[0:12, :], lf.ap[0:12, :],
                                                            self.cumstate.ap[0:12, :], ALU.mult, ALU.subtract),
              reads=[lf, self.ones12, self.cumstate], writes=[self.cum])
        self.copy("pool", self.cumstate, self.cum, out_ap=self.cumstate.ap[0:12, :], in_ap=self.cum.ap[0:12, TT - 1:TT])
        for tc in range(4):
            tp = self.ps.next()
            fw.op("pe", lambda e, tp=tp, tc=tc: e.transpose(tp.ap[:, 0:12], self.cum.ap[0:12, tc * 128:(tc + 1) * 128],
                                                           self.ident.ap[0:12, 0:12]),
                  reads=[self.cum, self.ident], writes=[tp])
            self.ts("dve", self.negcumT, tp, -1.0, None, ALU.mult, None,
                    out_ap=self.negcumT.ap[:, 4 * t + tc, :], in_ap=tp.ap[:, 0:12])
        wb, w = self.wtile(("mq",))
        for c in range(2):
            ps = self.ps.next()
            for k in range(8):
                self.mm(ps, ps.ap, w[:, k, c * 128:(c + 1) * 128], self.xn[k].ap, k == 0, k == 7, [wb, self.xn[k]])
            self.headnorm(ps, ("memq_norm_g", 1), self.qmn[c], self.qmn[c].ap)
        nkeys = (t + 1) * TT
        for c in range(6):
            o = self.psacc.next()
            den = self.psacc.next()
            for h2 in range(2):
                h = 2 * c + h2
                sel = self.tmpf.next()
                self.ts("dve", sel, self.cum, self.oh.ap[0:12, h:h + 1], None, ALU.mult, None,
                        out_ap=sel.ap[0:12, :], in_ap=self.cum.ap[0:12, :], extra_reads=[self.oh])
                pb = self.ps.next()
                self.mm1(pb, pb.ap, self.ones12.ap[0:12, 0:128], sel.ap[0:12, :], True, True, [self.ones12, sel])
                self.copy("act", self.cq[h2], pb)
            blocks = []
            kvb = {}
            for kb0 in range(0, nkeys, 1024):
                wk = min(1024, nkeys - kb0)
                for k8 in range(wk // 128):
                    for h2 in range(2):
                        blocks.append((kb0, wk, h2, k8, kb0 // 128 + k8))

            def kv_load(kb0, wk):
                if kb0 not in kvb:
                    kbuf = self.kst.next()
                    vbuf = self.vst.next()
                    tiles = range(kb0 // TT, (kb0 + wk) // TT)
                    fw.dma("sp", kbuf.ap[:, 0:wk], self.kc_d[c, :, kb0:kb0 + wk],
                           reads=[self.kc_bufs[(c, tt_)] for tt_ in tiles], writes=[kbuf])
                    fw.dma("sp", vbuf.ap[:, 0:wk // 128, :], self.vc_d[c, :, kb0 // 128:(kb0 + wk) // 128, :],
                           reads=[self.vc_bufs[(c, tt_)] for tt_ in tiles], writes=[vbuf])
                    kvb[kb0] = (kbuf, vbuf)
                return kvb[kb0]

            def emit_qk(blk):
                kb0, wk, h2, k8, kcg = blk
                kbuf, vbuf = kv_load(kb0, wk)
                h = 2 * c + h2
                b0 = 64 * h2
                diag = kcg - 4 * t
                q0 = 128 * diag if diag >= 0 else 0
                s = self.ps.next()
                self.mm1(s, s.ap[:, q0:TT], kbuf.ap[b0:b0 + 64, k8 * 128:(k8 + 1) * 128],
                         self.qT[c].ap[b0:b0 + 64, q0:TT], True, True, [kbuf, self.qT[c]])
                tm = self.tmpf.next()
                self.stt(tm, s, 0.125, self.cq[h2], ALU.mult, ALU.add, out_ap=tm.ap[:, q0:TT],
                         in0_ap=s.ap[:, q0:TT], in1_ap=self.cq[h2].ap[:, q0:TT])
                if diag >= 0:
                    self.tt("pool", tm, tm, self.tri, ALU.add, out_ap=tm.ap[:, q0:q0 + 128],
                            a_ap=tm.ap[:, q0:q0 + 128])
                p = self.tmpb.next()
                self.act_(AF.Exp, p, tm, out_ap=p.ap[:, q0:TT], in_ap=tm.ap[:, q0:TT],
                          bias=self.negcumT.ap[:, kcg, h:h + 1], extra_reads=[self.negcumT])
                return p, q0

            def emit_pv(blk, p, q0):
                kb0, wk, h2, k8, kcg = blk
                kbuf, vbuf = kvb[kb0]
                b0 = 64 * h2
                first = kcg == 0
                last = kcg == 4 * t + 3
                acc = o if h2 == 0 else den
                self.mm1(acc, acc.ap[:, q0:TT], vbuf.ap[:, k8, h2 * 128:(h2 + 1) * 128], p.ap[:, q0:TT], first, last,
                         [vbuf, p])

            pend = []
            npair = len(blocks) // 2
            for j in range(npair + 1):
                if j < npair:
                    pend.append(emit_qk(blocks[2 * j]))
                    pend.append(emit_qk(blocks[2 * j + 1]))
                if j >= 1:
                    emit_pv(blocks[2 * j - 2], *pend[2 * j - 2])
                    emit_pv(blocks[2 * j - 1], *pend[2 * j - 1])
            rec = self.tmpf.next()
            self.act_(AF.Ln, rec, den, out_ap=rec.ap[0:64, :], in_ap=den.ap[0:64, :])
            self.act_(AF.Ln, rec, o, out_ap=rec.ap[64:128, :], in_ap=o.ap[64:128, :])
            self.act_(AF.Exp, rec, rec, scale=-1.0)
            psw = self.ps.next()
            self.mm1(psw, psw.ap, self.swapm.ap, rec.ap, True, True, [self.swapm, rec])
            rsw = self.tmpf.next()
            self.copy("act", rsw, psw)
            self.tt("dve", self.cat[c], o, rsw, ALU.mult, out_ap=self.cat[c].ap[0:64, :], a_ap=o.ap[0:64, :],
                    b_ap=rsw.ap[0:64, :])
            self.tt("dve", self.cat[c], den, rsw, ALU.mult, out_ap=self.cat[c].ap[64:128, :], a_ap=den.ap[64:128, :],
                    b_ap=rsw.ap[64:128, :])
        self.cross_attn(1)
        self.out_proj(1)

    Prog.mix_fox = mix_fox


_add_fox()
```

```python
import contextlib
import numpy as np
import concourse.bass as bass
import concourse.mybir as mybir
from concourse.bass_utils import run_bass_kernel_spmd

F32 = mybir.dt.float32
BF16 = mybir.dt.bfloat16
AF = mybir.ActivationFunctionType
ALU = mybir.AluOpType

D = 1024
S = 4096
DFF = 2816
NMEM = 256
HD = 64
MIXW = 768
KC = D // 128
FC = DFF // 128
EPS = 1e-6
SYNC_SAME = True


class Buf:
    __slots__ = ("ap", "lw", "rd", "name")

    def __init__(self, ap, name=""):
        self.ap = ap if isinstance(ap, bass.AP) else ap[:]
        self.lw = None
        self.rd = {}
        self.name = name


class Q:
    def __init__(self, name, sem, is_pe=False):
        self.name = name
        self.sem = sem
        self.is_pe = is_pe
        self.count = 0
        self.prog = []
        self.waited = {}
        self.dma_sems = []
        self.dma_vals = []
        self.dma_next = 0


class FW:
    def __init__(self, nc, stack, n_dma_sems=8):
        self.nc = nc
        self.q = {}
        for name, is_pe in (("pe", True), ("act", False), ("dve", False), ("pool", False), ("sp", False)):
            sem = stack.enter_context(nc.semaphore(f"prog_{name}"))
            self.q[name] = Q(name, sem, is_pe)
        for name in ("sp", "act", "pool"):
            q = self.q[name]
            for i in range(n_dma_sems):
                q.dma_sems.append(stack.enter_context(nc.semaphore(f"dma_{name}_{i}")))
                q.dma_vals.append(0)

    def _deps(self, q, reads, writes, extra=()):
        deps = {}

        def need(tok):
            if tok is None:
                return
            k = id(tok[0])
            if k not in deps or deps[k][1] < tok[1]:
                deps[k] = tok

        for b in reads:
            need(b.lw)
        for b in writes:
            need(b.lw)
            for tok in b.rd.values():
                need(tok)
        for tok in extra:
            need(tok)
        for k, (sem, val, owner) in deps.items():
            if owner is q and (q.is_pe or not SYNC_SAME):
                continue
            if q.waited.get(k, 0) >= val:
                continue
            q.waited[k] = val
            q.prog.append(("wait", sem, val))

    def op(self, qname, fn, reads=(), writes=(), signal=True):
        q = self.q[qname]
        self._deps(q, reads, writes)
        if signal:
            q.count += 1
            tok = (q.sem, q.count, q)
        else:
            tok = (q.sem, q.count + 1, q)
        q.prog.append(("op", fn, signal))
        for b in writes:
            b.lw = tok
            b.rd = {}
        for b in reads:
            b.rd[q.name] = tok

    def dma(self, qname, out_ap, in_ap, reads=(), writes=()):
        q = self.q[qname]
        i = q.dma_next
        q.dma_next = (i + 1) % len(q.dma_sems)
        sem = q.dma_sems[i]
        cur = q.dma_vals[i]
        extra = [(sem, cur, None)] if cur > 0 else []
        self._deps(q, reads, writes, extra)
        q.dma_vals[i] = cur + 16
        tok = (sem, cur + 16, None)
        q.prog.append(("dma", out_ap, in_ap, sem))
        for b in writes:
            b.lw = tok
            b.rd = {}
        for b in reads:
            b.rd[("dma", id(sem))] = tok

    def finish(self):
        for name in ("sp", "act", "pool"):
            q = self.q[name]
            for sem, val in zip(q.dma_sems, q.dma_vals):
                if val > 0 and q.waited.get(id(sem), 0) < val:
                    q.prog.append(("wait", sem, val))

    def replay(self, qname, eng):
        q = self.q[qname]
        for item in q.prog:
            if item[0] == "wait":
                eng.wait_ge(item[1], item[2])
            elif item[0] == "op":
                inst = item[1](eng)
                if item[2]:
                    inst.then_inc(q.sem, 1)
            else:
                eng.dma_start(out=item[1], in_=item[2]).then_inc(item[3], 16)


class Ring:
    def __init__(self, bufs):
        self.bufs = bufs
        self.i = 0

    def next(self):
        b = self.bufs[self.i]
        self.i = (self.i + 1) % len(self.bufs)
        return b


WT_ELEMS = 2048


def _tiles_of(W, col_ranges, nk_split):
    out = []
    Kin = W.shape[0]
    kcs = Kin // 128
    Wr = W.reshape(kcs, 128, W.shape[1])
    for (c0, nc_) in col_ranges:
        k0 = 0
        for nk in nk_split:
            t = Wr[k0:k0 + nk, :, c0:c0 + nc_]
            t = np.transpose(t, (1, 0, 2)).reshape(128, nk * nc_)
            out.append(t)
            k0 += nk
        assert k0 == kcs
    return out


class WeightPlan:
    def __init__(self):
        self.tiles = []
        self.arrays = []

    def add(self, arrs):
        idx0 = len(self.arrays)
        self.arrays.extend(arrs)
        return list(range(idx0, idx0 + len(arrs)))

    def pack(self):
        out = np.zeros((len(self.arrays), 128, WT_ELEMS), np.float32)
        for i, a in enumerate(self.arrays):
            out[i, :, :a.shape[1]] = a
        return out


def weight_specs():
    sp = []
    for L in range(2):
        for nm, src_in, src_out in (("f1", "ffn1_w_in", "ffn1_w_out"), ("f2", "ffn2_w_in", "ffn2_w_out")):
            for g in range(DFF // 256):
                sp.append(((nm + "i", L, "g", g), src_in, L, g * 256, 256, 0, 8))
                sp.append(((nm + "i", L, "u", g), src_in, L, DFF + g * 256, 256, 0, 8))
            for oc in range(8):
                sp.append(((nm + "o", L, oc, 0), src_out, L, oc * 128, 128, 0, 11))
                sp.append(((nm + "o", L, oc, 1), src_out, L, oc * 128, 128, 11, 11))
        for g in range(4):
            sp.append((("mo", L, g), "mix_w_out", L, g * 256, 256, 0, 8))
    for g in range(7):
        sp.append((("mi", 0, g), "lru_w_in", 0, g * 256, 256, 0, 8))
    for g in range(6):
        sp.append((("mi", 1, g), "fox_w_in", 0, g * 256, 256, 0, 8))
    for g in range(3):
        sp.append((("mv", g), "fox_w_in", 0, 1536 + g * 256, 256, 0, 8))
    sp.append((("mf",), "fox_w_in", 0, 2304, 12, 0, 8))
    sp.append((("mq",), "fox_w_in", 0, 2316, 256, 0, 8))
    for g in range(2):
        sp.append((("kv", g), "mem_w_kv", None, g * 256, 256, 0, 8))
    for c in range(6):
        sp.append((("rg", c), "lru_w_rg", 0, c, 0, 0, 1))
        sp.append((("ig", c), "lru_w_ig", 0, c, 0, 0, 1))
    return sp


def pack_weights(inp):
    sp = weight_specs()
    out = np.zeros((len(sp), 128, WT_ELEMS), np.float32)
    index = {}
    for i, (key, src, L, c0, ncols, k0, nk) in enumerate(sp):
        index[key] = i
        W = inp[src]
        if key[0] in ("rg", "ig"):
            blk = W[0]
            c = c0
            out[i, 0:64, 0:64] = blk[2 * c]
            out[i, 64:128, 64:128] = blk[2 * c + 1]
            continue
        if L is not None:
            W = W[L]
        Wr = W.reshape(W.shape[0] // 128, 128, W.shape[1])
        t = Wr[k0:k0 + nk, :, c0:c0 + ncols]
        out[i, :, :nk * ncols] = np.transpose(t, (1, 0, 2)).reshape(128, nk * ncols)
    return out, index


def vec_cols():
    cols = {}
    n = 0
    for L in range(2):
        for nm in ("ffn1_norm_g", "mix_norm_g", "ffn2_norm_g"):
            cols[(nm, L)] = n
            n += 8
    cols["mem_norm_g"] = n; n += 8
    cols["mem_k_norm_g"] = n; n += 1
    cols[("memq_norm_g", 0)] = n; n += 1
    cols[("memq_norm_g", 1)] = n; n += 1
    cols["conv_w"] = n; n += 24
    cols["conv_b"] = n; n += 6
    cols["b_rg"] = n; n += 6
    cols["b_ig"] = n; n += 6
    cols["lam"] = n; n += 6
    cols["b_f"] = n; n += 1
    cols["fox_q_g"] = n; n += 1
    cols["fox_k_g"] = n; n += 1
    cols["_n"] = n
    return cols


def pack_vecs(inp):
    cols = vec_cols()
    v = np.zeros((128, cols["_n"]), np.float32)

    def fm(a):
        return a.reshape(-1, 128).T

    for L in range(2):
        for nm in ("ffn1_norm_g", "mix_norm_g", "ffn2_norm_g"):
            v[:, cols[(nm, L)]:cols[(nm, L)] + 8] = fm(inp[nm][L])
    v[:, cols["mem_norm_g"]:cols["mem_norm_g"] + 8] = fm(inp["mem_norm_g"])
    rep = lambda a: np.concatenate([a, a])
    v[:, cols["mem_k_norm_g"]] = rep(inp["mem_k_norm_g"])
    for L in range(2):
        v[:, cols[("memq_norm_g", L)]] = rep(inp["memq_norm_g"][L])
    for tap in range(4):
        v[:, cols["conv_w"] + tap * 6: cols["conv_w"] + tap * 6 + 6] = fm(inp["lru_conv_w"][0, tap])
    v[:, cols["conv_b"]:cols["conv_b"] + 6] = fm(inp["lru_conv_b"][0])
    v[:, cols["b_rg"]:cols["b_rg"] + 6] = fm(inp["lru_b_rg"][0])
    v[:, cols["b_ig"]:cols["b_ig"] + 6] = fm(inp["lru_b_ig"][0])
    v[:, cols["lam"]:cols["lam"] + 6] = fm(inp["lru_lambda"][0])
    v[0:12, cols["b_f"]] = inp["fox_b_f"][0]
    v[:, cols["fox_q_g"]] = rep(inp["fox_q_norm_g"][0])
    v[:, cols["fox_k_g"]] = rep(inp["fox_k_norm_g"][0])
    return v


NEG = -30000.0


def make_consts():
    c = {}
    c["ident"] = np.eye(128, dtype=np.float32)
    ones = np.ones((128, 128), np.float32)
    blk = np.zeros((128, 128), np.float32)
    blk[0:64, 0:64] = 1.0
    blk[64:128, 64:128] = 1.0
    c["ones_blk"] = np.concatenate([ones, blk], axis=1)
    md = np.zeros((128, 4, 512), np.float32)
    for kk in range(4):
        key = kk * 128 + np.arange(128)[:, None]
        qry = np.arange(512)[None, :]
        md[:, kk, :] = np.where(key <= qry, 0.0, NEG)
    c["maskd"] = md.reshape(128, 2048)
    sel = np.zeros((128, 12, 128), np.float32)
    for h in range(12):
        sel[h, h, :] = 1.0
    c["sel"] = sel.reshape(128, 12 * 128)
    return c


TT = 512
NCONST_VEC = None


class Prog:
    def __init__(self, n_tiles=S // TT, stages=("ffn1a", "mix0", "ffn2a", "ffn1b", "mix1", "ffn2b"), dbg=False):
        self.n_tiles = n_tiles
        self.stages = stages
        self.dbg = dbg
        self.stack = contextlib.ExitStack()
        nc = self.nc = bass.Bass("TRN2", target_bir_lowering=False)
        self.fw = FW(nc, self.stack)
        self.vc = vec_cols()
        self.widx = {k[0]: i for i, k in enumerate(weight_specs())}
        self.wspec = {k[0]: k for k in weight_specs()}
        nW = len(self.widx)
        dt = nc.dram_tensor
        self.x_d = dt("x", [S, D], F32, kind="ExternalInput").ap()
        self.mem_d = dt("mem", [NMEM, D], F32, kind="ExternalInput").ap()
        self.wts_d = dt("wts", [nW, 128, WT_ELEMS], F32, kind="ExternalInput").ap()
        self.vecs_d = dt("vecs", [128, self.vc["_n"]], F32, kind="ExternalInput").ap()
        self.ident_d = dt("ident", [128, 128], F32, kind="ExternalInput").ap()
        self.onesblk_d = dt("ones_blk", [128, 256], F32, kind="ExternalInput").ap()
        self.tri_d = dt("tri", [128, 128], F32, kind="ExternalInput").ap()
        self.oh_d = dt("onehot", [128, 16], F32, kind="ExternalInput").ap()
        self.swap_d = dt("swapm", [128, 128], F32, kind="ExternalInput").ap()
        self.out_d = dt("out", [S, D], F32, kind="ExternalOutput").ap()
        self.kc_d = dt("kcache", [6, 128, S], BF16, kind="Internal").ap()
        self.vc_d = dt("vcache", [6, 128, S // 128, 256], BF16, kind="Internal").ap()
        self.wbf_d = dt("wbf16", [nW, 128, WT_ELEMS], BF16, kind="Internal").ap()
        self.wconv = {}
        self.ncast = 0
        if dbg:
            self.dbg_d = dt("dbg", [128, 8 * 512], F32, kind="ExternalOutput").ap()
        self.kc_bufs = {}
        self.vc_bufs = {}
        self._alloc()
        self._prologue()
        for t in range(n_tiles):
            self._tile(t)
        self.fw.finish()
        self._emit()

    def sb(self, name, shape, dtype):
        return self.stack.enter_context(self.nc.sbuf_tensor("sb_" + name, shape, dtype))

    def _alloc(self):
        nc = self.nc
        sb = self.sb
        B = Buf
        self.ident = B(sb("ident", [128, 128], F32))
        self.onesblk = B(sb("onesblk", [128, 256], BF16))
        self.tri = B(sb("tri", [128, 128], F32))
        self.oh = B(sb("oh", [128, 16], F32))
        self.swapm = B(sb("swapm", [128, 128], F32))
        self.ones12 = B(sb("ones12", [128, 512], F32))
        self.vecs = B(sb("vecs", [128, self.vc["_n"]], F32))
        self.cL = B(sb("cL", [128, 12], F32))
        self.negbf = B(sb("negbf", [128, 1], F32))
        self.epsb = B(sb("epsb", [128, 1], F32))
        hT = sb("hT", [128, 8, TT], F32)
        self.hT = [B(hT[:, k, :], f"hT{k}") for k in range(8)]
        xn = sb("xn", [128, 8, TT], BF16)
        self.xn = [B(xn[:, k, :], f"xn{k}") for k in range(8)]
        act = sb("act", [128, FC, TT], BF16)
        self.act = [B(act[:, k, :], f"act{k}") for k in range(FC)]
        cat = sb("cat", [128, 8, TT], BF16)
        self.cat = [B(cat[:, k, :], f"cat{k}") for k in range(8)]
        self.wstage = Ring([B(sb(f"wst{i}", [128, WT_ELEMS], F32), f"wst{i}") for i in range(2)])
        self.wbf = Ring([B(sb(f"wbf{i}", [128, WT_ELEMS], BF16), f"wbf{i}") for i in range(6)])
        self.xin = B(sb("xin", [128, 4, D], F32), "xin")
        self.yout = B(sb("yout", [128, 4, D], F32), "yout") if False else self.xin
        self.mkT = B(sb("mkT", [128, 2, NMEM], BF16), "mkT")
        self.mv = B(sb("mv", [128, 2, 256], BF16), "mv")
        self.tmpf = Ring([B(sb(f"tmpf{i}", [128, TT], F32), f"tmpf{i}") for i in range(6)])
        self.tmpb = Ring([B(sb(f"tmpb{i}", [128, TT], BF16), f"tmpb{i}") for i in range(4)])
        xbr = sb("xbr", [128, 6, TT + 4], F32)
        self.xbr_t = xbr
        self.xbr = [B(xbr[:, c, :], f"xbr{c}") for c in range(6)]
        self.hstate = [B(sb(f"hstate{c}", [128, 1], F32), f"hstate{c}") for c in range(6)]
        qT = sb("qT", [128, 6, TT], BF16)
        self.qT = [B(qT[:, c, :], f"qT{c}") for c in range(6)]
        self.kst = Ring([B(sb(f"kst{i}", [128, 1024], BF16), f"kst{i}") for i in range(3)])
        self.vst = Ring([B(sb(f"vst{i}", [128, 8, 256], BF16), f"vst{i}") for i in range(3)])
        self.cum = B(sb("cum", [128, TT], F32), "cum")
        self.cumstate = B(sb("cumstate", [128, 1], F32), "cumstate")
        self.negcumT = B(sb("negcumT", [128, S // 128, 12], F32), "negcumT")
        self.vtok = B(sb("vtok", [128, 4, 6, 256], BF16), "vtok")
        self.ps = Ring([B(self.stack.enter_context(nc.psum_tensor(f"ps{i}", [128, 512], F32)), f"ps{i}")
                        for i in range(4)])
        self.psacc = Ring([B(self.stack.enter_context(nc.psum_tensor(f"pa{i}", [128, 512], F32)), f"pa{i}")
                           for i in range(4)])
        qmn = sb("qmn", [128, 2, TT], BF16)
        self.qmn = [B(qmn[:, c, :], f"qmn{c}") for c in range(2)]
        self.cq = [B(sb(f"cq{i}", [128, TT], F32), f"cq{i}") for i in range(4)]

    def vcol(self, key, off=0, n=1, p0=0, p1=128):
        c = self.vc[key] + off
        return self.vecs.ap[p0:p1, c:c + n]

    def wtile(self, key):
        fw = self.fw
        _, src, L, c0, ncols, k0, nk = self.wspec[key]
        if key[0] in ("rg", "ig"):
            nk, ncols = 1, 128
        n = nk * ncols
        idx = self.widx[key]
        wb = self.wbf.next()
        if key in self.wconv:
            db = self.wconv[key]
            fw.dma("sp", wb.ap[:, 0:n], db.ap, reads=[db], writes=[wb])
        else:
            st = self.wstage.next()
            fw.dma("sp", st.ap[:, 0:n], self.wts_d[idx, :, 0:n], writes=[st])
            eng = "act"
            self.ncast += 1
            self.copy(eng, wb, st, out_ap=wb.ap[:, 0:n], in_ap=st.ap[:, 0:n])
            if key[0] != "kv" and self.n_tiles > 1:
                db = Buf(self.wbf_d[idx, :, 0:n], "wd")
                self.wconv[key] = db
                fw.dma(eng, db.ap, wb.ap[:, 0:n], reads=[wb], writes=[db])
        return wb, wb.ap[:, 0:n].rearrange("p (k n) -> p k n", k=nk)

    def mm(self, ps, out_ap, lhsT, rhs, start, stop, reads, signal=None):
        self.fw.op("pe", lambda e: e.matmul(out_ap, lhsT, rhs, start=start, stop=stop),
                   reads=reads, writes=[ps], signal=(stop if signal is None else signal))

    def mm1(self, ps, out_ap, lhsT, rhs, start, stop, reads):
        self.fw.op("pe", lambda e: e.matmul(out_ap, lhsT, rhs, start=start, stop=stop),
                   reads=reads, writes=[ps], signal=True)

    def act_(self, func, out_b, in_b, out_ap=None, in_ap=None, extra_reads=(), **kw):
        o = out_b.ap if out_ap is None else out_ap
        i = in_b.ap if in_ap is None else in_ap
        self.fw.op("act", lambda e: e.activation(o, i, func, **kw), reads=[in_b] + list(extra_reads), writes=[out_b])

    def tt(self, eng, out_b, a_b, b_b, op, out_ap=None, a_ap=None, b_ap=None):
        o = out_b.ap if out_ap is None else out_ap
        a = a_b.ap if a_ap is None else a_ap
        b = b_b.ap if b_ap is None else b_ap
        self.fw.op(eng, lambda e: e.tensor_tensor(o, a, b, op), reads=[a_b, b_b], writes=[out_b])

    def ts(self, eng, out_b, in_b, s1, s2, op0, op1, out_ap=None, in_ap=None, extra_reads=()):
        o = out_b.ap if out_ap is None else out_ap
        i = in_b.ap if in_ap is None else in_ap
        if op1 is None:
            fn = lambda e: e.tensor_scalar(o, i, s1, None, op0)
        else:
            fn = lambda e: e.tensor_scalar(o, i, s1, s2, op0, op1)
        self.fw.op(eng, fn, reads=[in_b] + list(extra_reads), writes=[out_b])

    def stt(self, out_b, in0_b, scalar, in1_b, op0, op1, out_ap=None, in0_ap=None, in1_ap=None, extra_reads=()):
        o = out_b.ap if out_ap is None else out_ap
        a = in0_b.ap if in0_ap is None else in0_ap
        b = in1_b.ap if in1_ap is None else in1_ap
        self.fw.op("dve", lambda e: e.scalar_tensor_tensor(o, a, scalar, b, op0, op1),
                   reads=[in0_b, in1_b] + list(extra_reads), writes=[out_b])

    def copy(self, eng, out_b, in_b, out_ap=None, in_ap=None):
        o = out_b.ap if out_ap is None else out_ap
        i = in_b.ap if in_ap is None else in_ap
        if eng == "act":
            fn = lambda e: e.copy(o, i)
        else:
            fn = lambda e: e.tensor_copy(o, i)
        self.fw.op(eng, fn, reads=[in_b], writes=[out_b])

    def rmsnorm_fm(self, src, gkey, dst, n=TT):
        ps = self.ps.next()
        for k in range(8):
            sq = self.tmpb.next()
            self.act_(AF.Square, sq, src[k], out_ap=sq.ap[:, 0:n], in_ap=src[k].ap[:, 0:n])
            self.mm(ps, ps.ap[:, 0:n], self.onesblk.ap[:, 0:128], sq.ap[:, 0:n], k == 0, k == 7, [sq, self.onesblk],
                    signal=True)
        rstd = self.tmpf.next()
        self.act_(AF.Ln, rstd, ps, out_ap=rstd.ap[:, 0:n], in_ap=ps.ap[:, 0:n], bias=self.epsb.ap[:, 0:1],
                  scale=1.0 / D, extra_reads=[self.epsb])
        self.act_(AF.Exp, rstd, rstd, out_ap=rstd.ap[:, 0:n], in_ap=rstd.ap[:, 0:n], scale=-0.5)
        for k in range(8):
            self.stt(dst[k], src[k], self.vcol(gkey, k), rstd, ALU.mult, ALU.mult,
                     out_ap=dst[k].ap[:, 0:n], in0_ap=src[k].ap[:, 0:n], in1_ap=rstd.ap[:, 0:n],
                     extra_reads=[self.vecs])

    def headnorm(self, ps, gkey, out_b, out_ap, n=TT):
        sq = self.tmpb.next()
        self.act_(AF.Square, sq, ps, out_ap=sq.ap[:, 0:n], in_ap=ps.ap[:, 0:n])
        pn = self.ps.next()
        self.mm(pn, pn.ap[:, 0:n], self.onesblk.ap[:, 128:256], sq.ap[:, 0:n], True, True, [sq, self.onesblk])
        rstd = self.tmpf.next()
        self.act_(AF.Ln, rstd, pn, out_ap=rstd.ap[:, 0:n], in_ap=pn.ap[:, 0:n], bias=self.epsb.ap[:, 0:1],
                  scale=1.0 / HD, extra_reads=[self.epsb])
        self.act_(AF.Exp, rstd, rstd, out_ap=rstd.ap[:, 0:n], in_ap=rstd.ap[:, 0:n], scale=-0.5)
        self.stt(out_b, ps, self.vcol(gkey), rstd, ALU.mult, ALU.mult,
                 out_ap=out_ap, in0_ap=ps.ap[:, 0:n], in1_ap=rstd.ap[:, 0:n], extra_reads=[self.vecs])

    def load_x(self, t):
        fw = self.fw
        fw.dma("act", self.xin.ap, self.x_d[t * TT:(t + 1) * TT, :].rearrange("(c p) d -> p c d", p=128),
               writes=[self.xin])
        for k in range(8):
            ps = self.ps.next()
            for tc in range(4):
                o = ps.ap[:, tc * 128:(tc + 1) * 128]
                i = self.xin.ap[:, tc, k * 128:(k + 1) * 128]
                fw.op("pe", lambda e, o=o, i=i: e.transpose(o, i, self.ident.ap),
                      reads=[self.xin, self.ident], writes=[ps], signal=(tc == 3))
            self.copy("act" if k % 2 else "dve", self.hT[k], ps)

    def store_out(self, t):
        fw = self.fw
        for tc in range(4):
            for half in range(2):
                ps = self.ps.next()
                for kk in range(4):
                    k = half * 4 + kk
                    o = ps.ap[:, kk * 128:(kk + 1) * 128]
                    i = self.hT[k].ap[:, tc * 128:(tc + 1) * 128]
                    fw.op("pe", lambda e, o=o, i=i: e.transpose(o, i, self.ident.ap),
                          reads=[self.hT[k], self.ident], writes=[ps], signal=(kk == 3))
                self.copy("act" if half else "dve", self.xin, ps,
                          out_ap=self.xin.ap[:, tc, half * 512:(half + 1) * 512])
        fw.dma("act", self.out_d[t * TT:(t + 1) * TT, :].rearrange("(c p) d -> p c d", p=128), self.xin.ap,
               reads=[self.xin])

    def ffn(self, L, nm, gname):
        fw = self.fw
        self.rmsnorm_fm(self.hT, (gname, L), self.xn)
        for g in range(DFF // 256):
            wg_b, wg = self.wtile((nm + "i", L, "g", g))
            wu_b, wu = self.wtile((nm + "i", L, "u", g))
            for fc in range(2):
                f = g * 2 + fc
                pg = self.ps.next()
                pu = self.ps.next()
                for k in range(8):
                    self.mm(pg, pg.ap, wg[:, k, fc * 128:(fc + 1) * 128], self.xn[k].ap, k == 0, k == 7,
                            [wg_b, self.xn[k]])
                for k in range(8):
                    self.mm(pu, pu.ap, wu[:, k, fc * 128:(fc + 1) * 128], self.xn[k].ap, k == 0, k == 7,
                            [wu_b, self.xn[k]])
                sg = self.tmpf.next()
                self.act_(AF.Silu, sg, pg)
                self.tt("dve", self.act[f], sg, pu, ALU.mult)
        for oc in range(8):
            w0b, w0 = self.wtile((nm + "o", L, oc, 0))
            w1b, w1 = self.wtile((nm + "o", L, oc, 1))
            py = self.ps.next()
            for k in range(FC):
                wb, w = (w0b, w0) if k < 11 else (w1b, w1)
                self.mm(py, py.ap, w[:, k % 11, :], self.act[k].ap, k == 0, k == FC - 1, [wb, self.act[k]])
            self.stt(self.hT[oc], py, 0.5, self.hT[oc], ALU.mult, ALU.add)

    def _prologue(self):
        fw = self.fw
        fw.dma("act", self.ident.ap, self.ident_d, writes=[self.ident])
        fw.dma("act", self.tri.ap, self.tri_d, writes=[self.tri])
        fw.dma("act", self.oh.ap, self.oh_d, writes=[self.oh])
        fw.dma("act", self.swapm.ap, self.swap_d, writes=[self.swapm])
        fw.dma("act", self.vecs.ap, self.vecs_d, writes=[self.vecs])
        st = self.tmpf.next()
        fw.dma("act", st.ap[:, 0:256], self.onesblk_d, writes=[st])
        self.copy("dve", self.onesblk, st, in_ap=st.ap[:, 0:256])
        fw.op("dve", lambda e: e.memset(self.epsb.ap, EPS), writes=[self.epsb])
        fw.op("dve", lambda e: e.memset(self.ones12.ap, 1.0), writes=[self.ones12])
        if "mix0" in self.stages or "mix1" in self.stages:
            self._prologue_mix()

    def _tile(self, t):
        st = self.stages
        self.load_x(t)
        if "ffn1a" in st:
            self.ffn(0, "f1", "ffn1_norm_g")
        if "mix0" in st:
            self.mix_lru(t)
        if "ffn2a" in st:
            self.ffn(0, "f2", "ffn2_norm_g")
        if "ffn1b" in st:
            self.ffn(1, "f1", "ffn1_norm_g")
        if "mix1" in st:
            self.mix_fox(t)
        if "ffn2b" in st:
            self.ffn(1, "f2", "ffn2_norm_g")
        self.store_out(t)

    def _emit(self):
        nc = self.nc
        fw = self.fw
        with nc.Block() as block:
            @block.tensor
            def _(e):
                fw.replay("pe", e)

            @block.scalar
            def _(e):
                fw.replay("act", e)

            @block.vector
            def _(e):
                fw.replay("dve", e)

            @block.gpsimd
            def _(e):
                fw.replay("pool", e)

            @block.sync
            def _(e):
                fw.replay("sp", e)
        self.stack.close()


_CACHE = {}


def host_inputs(inp):
    wts, _ = pack_weights(inp)
    vecs = pack_vecs(inp)
    c = make_consts()
    tri = np.where(np.arange(128)[:, None] <= np.arange(128)[None, :], 0.0, NEG).astype(np.float32)
    oh = np.zeros((128, 16), np.float32)
    for h in range(12):
        oh[h, h] = 1.0
    sw = np.zeros((128, 128), np.float32)
    for k in range(128):
        sw[k, (k + 64) % 128] = 1.0
    shared = {"wts": wts, "vecs": vecs, "ident": c["ident"], "ones_blk": c["ones_blk"], "tri": tri, "onehot": oh,
              "swapm": sw}
    return shared


def kernel(**inputs):
    inp = {k: np.asarray(v) for k, v in inputs.items()}
    shared = host_inputs(inp)
    if "prog" not in _CACHE:
        _CACHE["prog"] = Prog()
    prog = _CACHE["prog"]
    x = np.ascontiguousarray(inp["x"], dtype=np.float32)
    mem = np.ascontiguousarray(inp["mem"], dtype=np.float32)
    in_maps = []
    for b in range(8):
        m = dict(shared)
        m["x"] = x[b]
        m["mem"] = mem[b]
        in_maps.append(m)
    res = run_bass_kernel_spmd(prog.nc, in_maps, core_ids=list(range(8)))
    out = np.stack([np.asarray(r["out"], dtype=np.float32).reshape(S, D) for r in res.results], axis=0)
    return out


def _add_mixers():
    def _prologue_mix(self):
        fw = self.fw
        t = self.tmpf.next()
        lam = self.vecs.ap[:, self.vc["lam"]:self.vc["lam"] + 6]
        one = self.ones12.ap[:, 0:1]
        fw.op("act", lambda e: e.activation(t.ap[:, 0:6], lam, AF.Exp, scale=-1.0), reads=[self.vecs], writes=[t])
        fw.op("act", lambda e: e.activation(t.ap[:, 0:6], t.ap[:, 0:6], AF.Ln, bias=one), reads=[t, self.ones12], writes=[t])
        self.ts("dve", self.cL, t, -8.0, None, ALU.mult, None, out_ap=self.cL.ap[:, 0:6], in_ap=t.ap[:, 0:6])
        self.ts("dve", self.cL, t, -16.0, None, ALU.mult, None, out_ap=self.cL.ap[:, 6:12], in_ap=t.ap[:, 0:6])
        self.ts("dve", self.negbf, self.vecs, -1.0, None, ALU.mult, None, in_ap=self.vcol("b_f"))
        for c in range(6):
            fw.op("dve", lambda e, a=self.hstate[c].ap: e.memset(a, 0.0), writes=[self.hstate[c]])
            fw.op("dve", lambda e, a=self.xbr[c].ap[:, 0:4]: e.memset(a, 0.0), writes=[self.xbr[c]])
        fw.op("dve", lambda e: e.memset(self.cumstate.ap, 0.0), writes=[self.cumstate])
        fw.op("pool", lambda e: e.memset(self.vtok.ap, 1.0), writes=[self.vtok])
        fw.dma("act", self.xin.ap[:, 0:2, :], self.mem_d.rearrange("(c p) d -> p c d", p=128), writes=[self.xin])
        for k in range(8):
            ps = self.ps.next()
            for tc in range(2):
                o = ps.ap[:, tc * 128:(tc + 1) * 128]
                i = self.xin.ap[:, tc, k * 128:(k + 1) * 128]
                fw.op("pe", lambda e, o=o, i=i: e.transpose(o, i, self.ident.ap),
                      reads=[self.xin, self.ident], writes=[ps], signal=(tc == 1))
            self.copy("act" if k % 2 else "dve", self.hT[k], ps, out_ap=self.hT[k].ap[:, 0:256], in_ap=ps.ap[:, 0:256])
        self.rmsnorm_fm(self.hT, "mem_norm_g", self.xn, n=NMEM)
        wkb, wk = self.wtile(("kv", 0))
        wvb, wv = self.wtile(("kv", 1))
        for c in range(2):
            ps = self.ps.next()
            for k in range(8):
                self.mm(ps, ps.ap[:, 0:NMEM], wk[:, k, c * 128:(c + 1) * 128], self.xn[k].ap[:, 0:NMEM], k == 0, k == 7,
                        [wkb, self.xn[k]])
            self.headnorm(ps, "mem_k_norm_g", self.mkT, self.mkT.ap[:, c, :], n=NMEM)
        for nch in range(2):
            ps = self.ps.next()
            for k in range(8):
                self.mm(ps, ps.ap[:, 0:256], self.xn[k].ap[:, nch * 128:(nch + 1) * 128], wv[:, k, :], k == 0, k == 7,
                        [wvb, self.xn[k]])
            self.copy("act", self.mv, ps, out_ap=self.mv.ap[:, nch, :], in_ap=ps.ap[:, 0:256])

    def cross_attn(self, L):
        for c in range(2):
            o = self.psacc.next()
            den = self.psacc.next()
            for h2 in range(2):
                h = 2 * c + h2
                b0 = 64 * h2
                for nch in range(2):
                    s = self.ps.next()
                    self.mm1(s, s.ap, self.mkT.ap[b0:b0 + 64, c, nch * 128:(nch + 1) * 128],
                             self.qmn[c].ap[b0:b0 + 64, :], True, True, [self.mkT, self.qmn[c]])
                    e_ = self.tmpb.next()
                    self.act_(AF.Exp, e_, s, scale=0.125)
                    self.mm1(o, o.ap[b0:b0 + 64, :], self.mv.ap[:, nch, h * 64:(h + 1) * 64], e_.ap,
                             nch == 0, nch == 1, [self.mv, e_])
                    self.mm1(den, den.ap[b0:b0 + 64, :], self.onesblk.ap[:, 0:64], e_.ap,
                             nch == 0, nch == 1, [self.onesblk, e_])
            rec = self.tmpf.next()
            self.act_(AF.Ln, rec, den)
            self.act_(AF.Exp, rec, rec, scale=-1.0)
            self.tt("dve", self.cat[6 + c], o, rec, ALU.mult)

    def out_proj(self, L):
        for g in range(4):
            wb, w = self.wtile(("mo", L, g))
            for fc in range(2):
                oc = 2 * g + fc
                ps = self.ps.next()
                for k in range(8):
                    self.mm(ps, ps.ap, w[:, k, fc * 128:(fc + 1) * 128], self.cat[k].ap, k == 0, k == 7,
                            [wb, self.cat[k]])
                self.tt("dve", self.hT[oc], ps, self.hT[oc], ALU.add)

    def mix_lru(self, t):
        fw = self.fw
        self.rmsnorm_fm(self.hT, ("mix_norm_g", 0), self.xn)
        if t > 0:
            for c in range(6):
                b = self.xbr[c]
                fw.op("pool", lambda e, b=b: e.tensor_copy(b.ap[:, 0:3], b.ap[:, TT:TT + 3]), reads=[b], writes=[b])
        for g in range(7):
            wb, w = self.wtile(("mi", 0, g))
            for fc in range(2):
                oc = 2 * g + fc
                ps = self.ps.next()
                for k in range(8):
                    self.mm(ps, ps.ap, w[:, k, fc * 128:(fc + 1) * 128], self.xn[k].ap, k == 0, k == 7, [wb, self.xn[k]])
                if oc < 6:
                    self.copy("act", self.xbr[oc], ps, out_ap=self.xbr[oc].ap[:, 3:TT + 3])
                elif oc < 12:
                    c = oc - 6
                    u = self.tmpf.next()
                    self.act_(AF.Square, u, ps)
                    self.ts("dve", u, u, 0.044715, 1.0, ALU.mult, ALU.add)
                    self.tt("dve", u, u, ps, ALU.mult)
                    self.act_(AF.Sigmoid, u, u, scale=1.5957691216057308)
                    self.tt("dve", self.cat[c], u, ps, ALU.mult)
                else:
                    c = oc - 12
                    self.headnorm(ps, ("memq_norm_g", 0), self.qmn[c], self.qmn[c].ap)
        one = self.ones12.ap[:, 0:1]
        for c in range(6):
            xb = self.xbr[c]
            acc = self.tmpf.next()
            cw = lambda tap: self.vcol("conv_w", tap * 6 + c)
            self.ts("dve", acc, xb, cw(0), self.vcol("conv_b", c), ALU.mult, ALU.add, in_ap=xb.ap[:, 0:TT],
                    extra_reads=[self.vecs])
            for tap in range(1, 4):
                self.stt(acc, xb, cw(tap), acc, ALU.mult, ALU.add, in0_ap=xb.ap[:, tap:tap + TT], extra_reads=[self.vecs])
            xcb = self.tmpb.next()
            self.copy("pool", xcb, acc)
            wrb, wr = self.wtile(("rg", c))
            wib, wi = self.wtile(("ig", c))
            pr = self.ps.next()
            self.mm(pr, pr.ap, wr[:, 0, :], xcb.ap, True, True, [wrb, xcb])
            pi = self.ps.next()
            self.mm(pi, pi.ap, wi[:, 0, :], xcb.ap, True, True, [wib, xcb])
            r = self.tmpf.next()
            self.act_(AF.Sigmoid, r, pr, bias=self.vcol("b_rg", c), extra_reads=[self.vecs])
            gi = self.tmpf.next()
            self.act_(AF.Sigmoid, gi, pi, bias=self.vcol("b_ig", c), extra_reads=[self.vecs])
            a = self.tmpf.next()
            self.act_(AF.Exp, a, r, scale=self.cL.ap[:, c:c + 1], extra_reads=[self.cL])
            m = self.tmpf.next()
            self.act_(AF.Exp, m, r, scale=self.cL.ap[:, 6 + c:7 + c], extra_reads=[self.cL])
            self.act_(AF.Sqrt, m, m, scale=-1.0, bias=one, extra_reads=[self.ones12])
            self.tt("pool", gi, gi, acc, ALU.mult)
            self.tt("dve", gi, gi, m, ALU.mult)
            hs = self.tmpf.next()
            hst = self.hstate[c]
            fw.op("dve", lambda e, hs=hs, a=a, gi=gi, hst=hst: e.tensor_tensor_scan(hs.ap, a.ap, gi.ap, hst.ap, ALU.mult, ALU.add),
                  reads=[a, gi, hst], writes=[hs])
            self.copy("pool", hst, hs, in_ap=hs.ap[:, TT - 1:TT])
            self.tt("dve", self.cat[c], hs, self.cat[c], ALU.mult)
        self.cross_attn(0)
        self.out_proj(0)

    Prog._prologue_mix = _prologue_mix
    Prog.cross_attn = cross_attn
    Prog.out_proj = out_proj
    Prog.mix_lru = mix_lru


_add_mixers()


def _add_fox():
    def mix_fox(self, t):
        fw = self.fw
        self.rmsnorm_fm(self.hT, ("mix_norm_g", 1), self.xn)
        for g in range(6):
            wb, w = self.wtile(("mi", 1, g))
            for fc in range(2):
                oc = 2 * g + fc
                ps = self.ps.next()
                for k in range(8):
                    self.mm(ps, ps.ap, w[:, k, fc * 128:(fc + 1) * 128], self.xn[k].ap, k == 0, k == 7, [wb, self.xn[k]])
                if oc < 6:
                    self.headnorm(ps, "fox_q_g", self.qT[oc], self.qT[oc].ap)
                else:
                    c = oc - 6
                    kn = self.tmpb.next()
                    self.headnorm(ps, "fox_k_g", kn, kn.ap)
                    kb = Buf(self.kc_d[c, :, t * TT:(t + 1) * TT], f"kc{c}_{t}")
                    self.kc_bufs[(c, t)] = kb
                    fw.dma("act", kb.ap, kn.ap, reads=[kn], writes=[kb])
        for g in range(3):
            wb, w = self.wtile(("mv", g))
            for tc in range(4):
                ps = self.ps.next()
                for k in range(8):
                    self.mm(ps, ps.ap[:, 0:256], self.xn[k].ap[:, tc * 128:(tc + 1) * 128], w[:, k, :], k == 0, k == 7,
                            [wb, self.xn[k]])
                pv4 = ps.ap[:, 0:256].rearrange("p (c h d) -> p c h d", c=2, h=2)
                self.copy("act", self.vtok, ps, out_ap=self.vtok.ap[:, tc, 2 * g:2 * g + 2, 0:64], in_ap=pv4[:, :, 0, :])
                self.copy("dve", self.vtok, ps, out_ap=self.vtok.ap[:, tc, 2 * g:2 * g + 2, 192:256], in_ap=pv4[:, :, 1, :])
        for c in range(6):
            vb = Buf(self.vc_d[c, :, 4 * t:4 * t + 4, :], f"vc{c}_{t}")
            self.vc_bufs[(c, t)] = vb
            fw.dma("act", vb.ap, self.vtok.ap[:, :, c, :], reads=[self.vtok], writes=[vb])
        wb, w = self.wtile(("mf",))
        ps = self.ps.next()
        for k in range(8):
            self.mm(ps, ps.ap[0:12, :], w[:, k, 0:12], self.xn[k].ap, k == 0, k == 7, [wb, self.xn[k]])
        lf = self.tmpf.next()
        self.act_(AF.Exp, lf, ps, out_ap=lf.ap[0:12, :], in_ap=ps.ap[0:12, :], scale=-1.0, bias=self.negbf.ap[0:12, :],
                  extra_reads=[self.negbf])
        self.act_(AF.Ln, lf, lf, out_ap=lf.ap[0:12, :], in_ap=lf.ap[0:12, :], bias=self.ones12.ap[0:12, 0:1],
                  extra_reads=[self.ones12])
        fw.op("dve", lambda e, lf=lf: e.tensor_tensor_scan(self.cum.ap[0:12, :], self.ones12.ap[0:12, :], lf.ap[0:12, :],
                                                            self.cumstate.ap[0:12, :], ALU.mult, ALU.subtract),
              reads=[lf, self.ones12, self.cumstate], writes=[self.cum])
        self.copy("pool", self.cumstate, self.cum, out_ap=self.cumstate.ap[0:12, :], in_ap=self.cum.ap[0:12, TT - 1:TT])
        for tc in range(4):
            tp = self.ps.next()
            fw.op("pe", lambda e, tp=tp, tc=tc: e.transpose(tp.ap[:, 0:12], self.cum.ap[0:12, tc * 128:(tc + 1) * 128],
                                                           self.ident.ap[0:12, 0:12]),
                  reads=[self.cum, self.ident], writes=[tp])
            self.ts("dve", self.negcumT, tp, -1.0, None, ALU.mult, None,
                    out_ap=self.negcumT.ap[:, 4 * t + tc, :], in_ap=tp.ap[:, 0:12])
        wb, w = self.wtile(("mq",))
        for c in range(2):
            ps = self.ps.next()
            for k in range(8):
                self.mm(ps, ps.ap, w[:, k, c * 128:(c + 1) * 128], self.xn[k].ap, k == 0, k == 7, [wb, self.xn[k]])
            self.headnorm(ps, ("memq_norm_g", 1), self.qmn[c], self.qmn[c].ap)
        nkeys = (t + 1) * TT

        def prep(c):
            for h2 in range(2):
                h = 2 * c + h2
                sel = self.tmpf.next()
                self.ts("dve", sel, self.cum, self.oh.ap[0:12, h:h + 1], None, ALU.mult, None,
                        out_ap=sel.ap[0:12, :], in_ap=self.cum.ap[0:12, :], extra_reads=[self.oh])
                pb = self.ps.next()
                self.mm1(pb, pb.ap, self.ones12.ap[0:12, 0:128], sel.ap[0:12, :], True, True, [self.ones12, sel])
                self.copy("act", self.cq[2 * (c % 2) + h2], pb)

        def finalize(c, o, den):
            rec = self.tmpf.next()
            self.act_(AF.Ln, rec, o, out_ap=rec.ap[0:64, :], in_ap=o.ap[64:128, :])
            self.act_(AF.Ln, rec, den, out_ap=rec.ap[64:128, :], in_ap=den.ap[0:64, :])
            self.act_(AF.Exp, rec, rec, scale=-1.0)
            self.tt("dve", self.cat[c], o, rec, ALU.mult, out_ap=self.cat[c].ap[0:64, :], a_ap=o.ap[0:64, :],
                    b_ap=rec.ap[0:64, :])
            self.tt("dve", self.cat[c], den, rec, ALU.mult, out_ap=self.cat[c].ap[64:128, :], a_ap=den.ap[64:128, :],
                    b_ap=rec.ap[64:128, :])

        pending_fin = None
        prep(0)
        for c in range(6):
            o = self.psacc.next()
            den = self.psacc.next()
            if c + 1 < 6:
                prep(c + 1)
            blocks = []
            kvb = {}
            for kb0 in range(0, nkeys, 1024):
                wk = min(1024, nkeys - kb0)
                for k8 in range(wk // 128):
                    for h2 in range(2):
                        blocks.append((kb0, wk, h2, k8, kb0 // 128 + k8))

            def kv_load(kb0, wk):
                if kb0 not in kvb:
                    kbuf = self.kst.next()
                    vbuf = self.vst.next()
                    tiles = range(kb0 // TT, (kb0 + wk) // TT)
                    fw.dma("sp", kbuf.ap[:, 0:wk], self.kc_d[c, :, kb0:kb0 + wk],
                           reads=[self.kc_bufs[(c, tt_)] for tt_ in tiles], writes=[kbuf])
                    fw.dma("sp", vbuf.ap[:, 0:wk // 128, :], self.vc_d[c, :, kb0 // 128:(kb0 + wk) // 128, :],
                           reads=[self.vc_bufs[(c, tt_)] for tt_ in tiles], writes=[vbuf])
                    kvb[kb0] = (kbuf, vbuf)
                return kvb[kb0]

            def emit_qk(blk):
                kb0, wk, h2, k8, kcg = blk
                kbuf, vbuf = kv_load(kb0, wk)
                h = 2 * c + h2
                b0 = 64 * h2
                diag = kcg - 4 * t
                q0 = 128 * diag if diag >= 0 else 0
                s = self.ps.next()
                self.mm1(s, s.ap[:, q0:TT], kbuf.ap[b0:b0 + 64, k8 * 128:(k8 + 1) * 128],
                         self.qT[c].ap[b0:b0 + 64, q0:TT], True, True, [kbuf, self.qT[c]])
                tm = self.tmpf.next()
                self.stt(tm, s, 0.125, self.cq[2 * (c % 2) + h2], ALU.mult, ALU.add, out_ap=tm.ap[:, q0:TT],
                         in0_ap=s.ap[:, q0:TT], in1_ap=self.cq[2 * (c % 2) + h2].ap[:, q0:TT])
                if diag >= 0:
                    self.tt("pool", tm, tm, self.tri, ALU.add, out_ap=tm.ap[:, q0:q0 + 128],
                            a_ap=tm.ap[:, q0:q0 + 128])
                p = self.tmpb.next()
                self.act_(AF.Exp, p, tm, out_ap=p.ap[:, q0:TT], in_ap=tm.ap[:, q0:TT],
                          bias=self.negcumT.ap[:, kcg, h:h + 1], extra_reads=[self.negcumT])
                return p, q0

            def emit_pv(blk, p, q0):
                kb0, wk, h2, k8, kcg = blk
                kbuf, vbuf = kvb[kb0]
                b0 = 64 * h2
                first = kcg == 0
                last = kcg == 4 * t + 3
                acc = o if h2 == 0 else den
                self.mm1(acc, acc.ap[:, q0:TT], vbuf.ap[:, k8, h2 * 128:(h2 + 1) * 128], p.ap[:, q0:TT], first, last,
                         [vbuf, p])

            pend = []
            npair = len(blocks) // 2
            for j in range(npair + 1):
                if j < npair:
                    pend.append(emit_qk(blocks[2 * j]))
                    pend.append(emit_qk(blocks[2 * j + 1]))
                if j == 0 and pending_fin is not None:
                    finalize(*pending_fin)
                    pending_fin = None
                if j >= 1:
                    emit_pv(blocks[2 * j - 2], *pend[2 * j - 2])
                    emit_pv(blocks[2 * j - 1], *pend[2 * j - 1])
            pending_fin = (c, o, den)
        finalize(*pending_fin)
        self.cross_attn(1)
        self.out_proj(1)

    Prog.mix_fox = mix_fox


_add_fox()
```

```python
import contextlib
import numpy as np
import concourse.bass as bass
import concourse.mybir as mybir
from concourse.bass_utils import run_bass_kernel_spmd

F32 = mybir.dt.float32
BF16 = mybir.dt.bfloat16
AF = mybir.ActivationFunctionType
ALU = mybir.AluOpType

D = 1024
S = 4096
DFF = 2816
NMEM = 256
HD = 64
MIXW = 768
KC = D // 128
FC = DFF // 128
EPS = 1e-6
SYNC_SAME = True


class Buf:
    __slots__ = ("ap", "lw", "rd", "name")

    def __init__(self, ap, name=""):
        self.ap = ap if isinstance(ap, bass.AP) else ap[:]
        self.lw = None
        self.rd = {}
        self.name = name


class Q:
    def __init__(self, name, sem, is_pe=False):
        self.name = name
        self.sem = sem
        self.is_pe = is_pe
        self.count = 0
        self.prog = []
        self.waited = {}
        self.dma_sems = []
        self.dma_vals = []
        self.dma_next = 0


class FW:
    def __init__(self, nc, stack, n_dma_sems=8):
        self.nc = nc
        self.q = {}
        for name, is_pe in (("pe", True), ("act", False), ("dve", False), ("pool", False), ("sp", False)):
            sem = stack.enter_context(nc.semaphore(f"prog_{name}"))
            self.q[name] = Q(name, sem, is_pe)
        for name in ("sp", "act", "pool"):
            q = self.q[name]
            for i in range(n_dma_sems):
                q.dma_sems.append(stack.enter_context(nc.semaphore(f"dma_{name}_{i}")))
                q.dma_vals.append(0)

    def _deps(self, q, reads, writes, extra=()):
        deps = {}

        def need(tok):
            if tok is None:
                return
            k = id(tok[0])
            if k not in deps or deps[k][1] < tok[1]:
                deps[k] = tok

        for b in reads:
            need(b.lw)
        for b in writes:
            need(b.lw)
            for tok in b.rd.values():
                need(tok)
        for tok in extra:
            need(tok)
        for k, (sem, val, owner) in deps.items():
            if owner is q and (q.is_pe or not SYNC_SAME):
                continue
            if q.waited.get(k, 0) >= val:
                continue
            q.waited[k] = val
            q.prog.append(("wait", sem, val))

    def op(self, qname, fn, reads=(), writes=(), signal=True):
        q = self.q[qname]
        self._deps(q, reads, writes)
        if signal:
            q.count += 1
            tok = (q.sem, q.count, q)
        else:
            tok = (q.sem, q.count + 1, q)
        q.prog.append(("op", fn, signal))
        for b in writes:
            b.lw = tok
            b.rd = {}
        for b in reads:
            b.rd[q.name] = tok

    def dma(self, qname, out_ap, in_ap, reads=(), writes=()):
        q = self.q[qname]
        i = q.dma_next
        q.dma_next = (i + 1) % len(q.dma_sems)
        sem = q.dma_sems[i]
        cur = q.dma_vals[i]
        extra = [(sem, cur, None)] if cur > 0 else []
        self._deps(q, reads, writes, extra)
        q.dma_vals[i] = cur + 16
        tok = (sem, cur + 16, None)
        q.prog.append(("dma", out_ap, in_ap, sem))
        for b in writes:
            b.lw = tok
            b.rd = {}
        for b in reads:
            b.rd[("dma", id(sem))] = tok

    def finish(self):
        for name in ("sp", "act", "pool"):
            q = self.q[name]
            for sem, val in zip(q.dma_sems, q.dma_vals):
                if val > 0 and q.waited.get(id(sem), 0) < val:
                    q.prog.append(("wait", sem, val))

    def replay(self, qname, eng):
        q = self.q[qname]
        for item in q.prog:
            if item[0] == "wait":
                eng.wait_ge(item[1], item[2])
            elif item[0] == "op":
                inst = item[1](eng)
                if item[2]:
                    inst.then_inc(q.sem, 1)
            else:
                eng.dma_start(out=item[1], in_=item[2]).then_inc(item[3], 16)


class Ring:
    def __init__(self, bufs):
        self.bufs = bufs
        self.i = 0

    def next(self):
        b = self.bufs[self.i]
        self.i = (self.i + 1) % len(self.bufs)
        return b


WT_ELEMS = 2048


def _tiles_of(W, col_ranges, nk_split):
    out = []
    Kin = W.shape[0]
    kcs = Kin // 128
    Wr = W.reshape(kcs, 128, W.shape[1])
    for (c0, nc_) in col_ranges:
        k0 = 0
        for nk in nk_split:
            t = Wr[k0:k0 + nk, :, c0:c0 + nc_]
            t = np.transpose(t, (1, 0, 2)).reshape(128, nk * nc_)
            out.append(t)
            k0 += nk
        assert k0 == kcs
    return out


class WeightPlan:
    def __init__(self):
        self.tiles = []
        self.arrays = []

    def add(self, arrs):
        idx0 = len(self.arrays)
        self.arrays.extend(arrs)
        return list(range(idx0, idx0 + len(arrs)))

    def pack(self):
        out = np.zeros((len(self.arrays), 128, WT_ELEMS), np.float32)
        for i, a in enumerate(self.arrays):
            out[i, :, :a.shape[1]] = a
        return out


def weight_specs():
    sp = []
    for L in range(2):
        for nm, src_in, src_out in (("f1", "ffn1_w_in", "ffn1_w_out"), ("f2", "ffn2_w_in", "ffn2_w_out")):
            for g in range(DFF // 256):
                sp.append(((nm + "i", L, "g", g), src_in, L, g * 256, 256, 0, 8))
                sp.append(((nm + "i", L, "u", g), src_in, L, DFF + g * 256, 256, 0, 8))
            for oc in range(8):
                sp.append(((nm + "o", L, oc, 0), src_out, L, oc * 128, 128, 0, 11))
                sp.append(((nm + "o", L, oc, 1), src_out, L, oc * 128, 128, 11, 11))
        for g in range(4):
            sp.append((("mo", L, g), "mix_w_out", L, g * 256, 256, 0, 8))
    for g in range(7):
        sp.append((("mi", 0, g), "lru_w_in", 0, g * 256, 256, 0, 8))
    for g in range(6):
        sp.append((("mi", 1, g), "fox_w_in", 0, g * 256, 256, 0, 8))
    for g in range(3):
        sp.append((("mv", g), "fox_w_in", 0, 1536 + g * 256, 256, 0, 8))
    sp.append((("mf",), "fox_w_in", 0, 2304, 12, 0, 8))
    sp.append((("mq",), "fox_w_in", 0, 2316, 256, 0, 8))
    for g in range(2):
        sp.append((("kv", g), "mem_w_kv", None, g * 256, 256, 0, 8))
    for c in range(6):
        sp.append((("rg", c), "lru_w_rg", 0, c, 0, 0, 1))
        sp.append((("ig", c), "lru_w_ig", 0, c, 0, 0, 1))
    return sp


def pack_weights(inp):
    sp = weight_specs()
    out = np.zeros((len(sp), 128, WT_ELEMS), np.float32)
    index = {}
    for i, (key, src, L, c0, ncols, k0, nk) in enumerate(sp):
        index[key] = i
        W = inp[src]
        if key[0] in ("rg", "ig"):
            blk = W[0]
            c = c0
            out[i, 0:64, 0:64] = blk[2 * c]
            out[i, 64:128, 64:128] = blk[2 * c + 1]
            continue
        if L is not None:
            W = W[L]
        Wr = W.reshape(W.shape[0] // 128, 128, W.shape[1])
        t = Wr[k0:k0 + nk, :, c0:c0 + ncols]
        out[i, :, :nk * ncols] = np.transpose(t, (1, 0, 2)).reshape(128, nk * ncols)
    return out, index


def vec_cols():
    cols = {}
    n = 0
    for L in range(2):
        for nm in ("ffn1_norm_g", "mix_norm_g", "ffn2_norm_g"):
            cols[(nm, L)] = n
            n += 8
    cols["mem_norm_g"] = n; n += 8
    cols["mem_k_norm_g"] = n; n += 1
    cols[("memq_norm_g", 0)] = n; n += 1
    cols[("memq_norm_g", 1)] = n; n += 1
    cols["conv_w"] = n; n += 24
    cols["conv_b"] = n; n += 6
    cols["b_rg"] = n; n += 6
    cols["b_ig"] = n; n += 6
    cols["lam"] = n; n += 6
    cols["b_f"] = n; n += 1
    cols["fox_q_g"] = n; n += 1
    cols["fox_k_g"] = n; n += 1
    cols["_n"] = n
    return cols


def pack_vecs(inp):
    cols = vec_cols()
    v = np.zeros((128, cols["_n"]), np.float32)

    def fm(a):
        return a.reshape(-1, 128).T

    for L in range(2):
        for nm in ("ffn1_norm_g", "mix_norm_g", "ffn2_norm_g"):
            v[:, cols[(nm, L)]:cols[(nm, L)] + 8] = fm(inp[nm][L])
    v[:, cols["mem_norm_g"]:cols["mem_norm_g"] + 8] = fm(inp["mem_norm_g"])
    rep = lambda a: np.concatenate([a, a])
    v[:, cols["mem_k_norm_g"]] = rep(inp["mem_k_norm_g"])
    for L in range(2):
        v[:, cols[("memq_norm_g", L)]] = rep(inp["memq_norm_g"][L])
    for tap in range(4):
        v[:, cols["conv_w"] + tap * 6: cols["conv_w"] + tap * 6 + 6] = fm(inp["lru_conv_w"][0, tap])
    v[:, cols["conv_b"]:cols["conv_b"] + 6] = fm(inp["lru_conv_b"][0])
    v[:, cols["b_rg"]:cols["b_rg"] + 6] = fm(inp["lru_b_rg"][0])
    v[:, cols["b_ig"]:cols["b_ig"] + 6] = fm(inp["lru_b_ig"][0])
    v[:, cols["lam"]:cols["lam"] + 6] = fm(inp["lru_lambda"][0])
    v[0:12, cols["b_f"]] = inp["fox_b_f"][0]
    v[:, cols["fox_q_g"]] = rep(inp["fox_q_norm_g"][0])
    v[:, cols["fox_k_g"]] = rep(inp["fox_k_norm_g"][0])
    return v


NEG = -30000.0


def make_consts():
    c = {}
    c["ident"] = np.eye(128, dtype=np.float32)
    ones = np.ones((128, 128), np.float32)
    blk = np.zeros((128, 128), np.float32)
    blk[0:64, 0:64] = 1.0
    blk[64:128, 64:128] = 1.0
    c["ones_blk"] = np.concatenate([ones, blk], axis=1)
    md = np.zeros((128, 4, 512), np.float32)
    for kk in range(4):
        key = kk * 128 + np.arange(128)[:, None]
        qry = np.arange(512)[None, :]
        md[:, kk, :] = np.where(key <= qry, 0.0, NEG)
    c["maskd"] = md.reshape(128, 2048)
    sel = np.zeros((128, 12, 128), np.float32)
    for h in range(12):
        sel[h, h, :] = 1.0
    c["sel"] = sel.reshape(128, 12 * 128)
    return c


TT = 512
NCONST_VEC = None


class Prog:
    def __init__(self, n_tiles=S // TT, stages=("ffn1a", "mix0", "ffn2a", "ffn1b", "mix1", "ffn2b"), dbg=False):
        self.n_tiles = n_tiles
        self.stages = stages
        self.dbg = dbg
        self.stack = contextlib.ExitStack()
        nc = self.nc = bass.Bass("TRN2", target_bir_lowering=False)
        self.fw = FW(nc, self.stack)
        self.vc = vec_cols()
        self.widx = {k[0]: i for i, k in enumerate(weight_specs())}
        self.wspec = {k[0]: k for k in weight_specs()}
        nW = len(self.widx)
        dt = nc.dram_tensor
        self.x_d = dt("x", [S, D], F32, kind="ExternalInput").ap()
        self.mem_d = dt("mem", [NMEM, D], F32, kind="ExternalInput").ap()
        self.wts_d = dt("wts", [nW, 128, WT_ELEMS], F32, kind="ExternalInput").ap()
        self.vecs_d = dt("vecs", [128, self.vc["_n"]], F32, kind="ExternalInput").ap()
        self.ident_d = dt("ident", [128, 128], F32, kind="ExternalInput").ap()
        self.onesblk_d = dt("ones_blk", [128, 256], F32, kind="ExternalInput").ap()
        self.tri_d = dt("tri", [128, 128], F32, kind="ExternalInput").ap()
        self.oh_d = dt("onehot", [128, 16], F32, kind="ExternalInput").ap()
        self.swap_d = dt("swapm", [128, 128], F32, kind="ExternalInput").ap()
        self.out_d = dt("out", [S, D], F32, kind="ExternalOutput").ap()
        self.kc_d = dt("kcache", [6, 128, S], BF16, kind="Internal").ap()
        self.vc_d = dt("vcache", [6, 128, S // 128, 256], BF16, kind="Internal").ap()
        self.wbf_d = dt("wbf16", [nW, 128, WT_ELEMS], BF16, kind="Internal").ap()
        self.wconv = {}
        self.ncast = 0
        if dbg:
            self.dbg_d = dt("dbg", [128, 8 * 512], F32, kind="ExternalOutput").ap()
        self.kc_bufs = {}
        self.vc_bufs = {}
        self._alloc()
        self._prologue()
        for t in range(n_tiles):
            self._tile(t)
        self.fw.finish()
        self._emit()

    def sb(self, name, shape, dtype):
        return self.stack.enter_context(self.nc.sbuf_tensor("sb_" + name, shape, dtype))

    def _alloc(self):
        nc = self.nc
        sb = self.sb
        B = Buf
        self.ident = B(sb("ident", [128, 128], F32))
        self.onesblk = B(sb("onesblk", [128, 256], BF16))
        self.tri = B(sb("tri", [128, 128], F32))
        self.oh = B(sb("oh", [128, 16], F32))
        self.swapm = B(sb("swapm", [128, 128], F32))
        self.ones12 = B(sb("ones12", [128, 512], F32))
        self.vecs = B(sb("vecs", [128, self.vc["_n"]], F32))
        self.cL = B(sb("cL", [128, 12], F32))
        self.negbf = B(sb("negbf", [128, 1], F32))
        self.epsb = B(sb("epsb", [128, 1], F32))
        hT = sb("hT", [128, 8, TT], F32)
        self.hT = [B(hT[:, k, :], f"hT{k}") for k in range(8)]
        xn = sb("xn", [128, 8, TT], BF16)
        self.xn = [B(xn[:, k, :], f"xn{k}") for k in range(8)]
        act = sb("act", [128, FC, TT], BF16)
        self.act = [B(act[:, k, :], f"act{k}") for k in range(FC)]
        cat = sb("cat", [128, 8, TT], BF16)
        self.cat = [B(cat[:, k, :], f"cat{k}") for k in range(8)]
        self.wstage = Ring([B(sb(f"wst{i}", [128, WT_ELEMS], F32), f"wst{i}") for i in range(2)])
        self.wbf = Ring([B(sb(f"wbf{i}", [128, WT_ELEMS], BF16), f"wbf{i}") for i in range(6)])
        self.xin = B(sb("xin", [128, 4, D], F32), "xin")
        self.yout = B(sb("yout", [128, 4, D], F32), "yout") if False else self.xin
        self.mkT = B(sb("mkT", [128, 2, NMEM], BF16), "mkT")
        self.mv = B(sb("mv", [128, 2, 256], BF16), "mv")
        self.tmpf = Ring([B(sb(f"tmpf{i}", [128, TT], F32), f"tmpf{i}") for i in range(10)])
        self.tmpb = Ring([B(sb(f"tmpb{i}", [128, TT], BF16), f"tmpb{i}") for i in range(4)])
        xbr = sb("xbr", [128, 6, TT + 4], F32)
        self.xbr_t = xbr
        self.xbr = [B(xbr[:, c, :], f"xbr{c}") for c in range(6)]
        self.hstate = [B(sb(f"hstate{c}", [128, 1], F32), f"hstate{c}") for c in range(6)]
        qT = sb("qT", [128, 6, TT], BF16)
        self.qT = [B(qT[:, c, :], f"qT{c}") for c in range(6)]
        self.kst = Ring([B(sb(f"kst{i}", [128, 1024], BF16), f"kst{i}") for i in range(3)])
        self.vst = Ring([B(sb(f"vst{i}", [128, 8, 256], BF16), f"vst{i}") for i in range(3)])
        self.cum = B(sb("cum", [128, TT], F32), "cum")
        self.cumstate = B(sb("cumstate", [128, 1], F32), "cumstate")
        self.negcumT = B(sb("negcumT", [128, S // 128, 12], F32), "negcumT")
        self.vtok = B(sb("vtok", [128, 4, 6, 256], BF16), "vtok")
        self.ps = Ring([B(self.stack.enter_context(nc.psum_tensor(f"ps{i}", [128, 512], F32)), f"ps{i}")
                        for i in range(4)])
        self.psacc = Ring([B(self.stack.enter_context(nc.psum_tensor(f"pa{i}", [128, 512], F32)), f"pa{i}")
                           for i in range(4)])
        qmn = sb("qmn", [128, 2, TT], BF16)
        self.qmn = [B(qmn[:, c, :], f"qmn{c}") for c in range(2)]
        self.cq = [B(sb(f"cq{i}", [128, TT], F32), f"cq{i}") for i in range(4)]

    def vcol(self, key, off=0, n=1, p0=0, p1=128):
        c = self.vc[key] + off
        return self.vecs.ap[p0:p1, c:c + n]

    def wtile(self, key):
        fw = self.fw
        _, src, L, c0, ncols, k0, nk = self.wspec[key]
        if key[0] in ("rg", "ig"):
            nk, ncols = 1, 128
        n = nk * ncols
        idx = self.widx[key]
        wb = self.wbf.next()
        if key in self.wconv:
            db = self.wconv[key]
            fw.dma("sp", wb.ap[:, 0:n], db.ap, reads=[db], writes=[wb])
        else:
            st = self.wstage.next()
            fw.dma("sp", st.ap[:, 0:n], self.wts_d[idx, :, 0:n], writes=[st])
            eng = "act"
            self.ncast += 1
            self.copy(eng, wb, st, out_ap=wb.ap[:, 0:n], in_ap=st.ap[:, 0:n])
            if key[0] != "kv" and self.n_tiles > 1:
                db = Buf(self.wbf_d[idx, :, 0:n], "wd")
                self.wconv[key] = db
                fw.dma(eng, db.ap, wb.ap[:, 0:n], reads=[wb], writes=[db])
        return wb, wb.ap[:, 0:n].rearrange("p (k n) -> p k n", k=nk)

    def mm(self, ps, out_ap, lhsT, rhs, start, stop, reads, signal=None):
        self.fw.op("pe", lambda e: e.matmul(out_ap, lhsT, rhs, start=start, stop=stop),
                   reads=reads, writes=[ps], signal=(stop if signal is None else signal))

    def mm1(self, ps, out_ap, lhsT, rhs, start, stop, reads):
        self.fw.op("pe", lambda e: e.matmul(out_ap, lhsT, rhs, start=start, stop=stop),
                   reads=reads, writes=[ps], signal=True)

    def act_(self, func, out_b, in_b, out_ap=None, in_ap=None, extra_reads=(), **kw):
        o = out_b.ap if out_ap is None else out_ap
        i = in_b.ap if in_ap is None else in_ap
        self.fw.op("act", lambda e: e.activation(o, i, func, **kw), reads=[in_b] + list(extra_reads), writes=[out_b])

    def tt(self, eng, out_b, a_b, b_b, op, out_ap=None, a_ap=None, b_ap=None):
        o = out_b.ap if out_ap is None else out_ap
        a = a_b.ap if a_ap is None else a_ap
        b = b_b.ap if b_ap is None else b_ap
        self.fw.op(eng, lambda e: e.tensor_tensor(o, a, b, op), reads=[a_b, b_b], writes=[out_b])

    def ts(self, eng, out_b, in_b, s1, s2, op0, op1, out_ap=None, in_ap=None, extra_reads=()):
        o = out_b.ap if out_ap is None else out_ap
        i = in_b.ap if in_ap is None else in_ap
        if op1 is None:
            fn = lambda e: e.tensor_scalar(o, i, s1, None, op0)
        else:
            fn = lambda e: e.tensor_scalar(o, i, s1, s2, op0, op1)
        self.fw.op(eng, fn, reads=[in_b] + list(extra_reads), writes=[out_b])

    def stt(self, out_b, in0_b, scalar, in1_b, op0, op1, out_ap=None, in0_ap=None, in1_ap=None, extra_reads=()):
        o = out_b.ap if out_ap is None else out_ap
        a = in0_b.ap if in0_ap is None else in0_ap
        b = in1_b.ap if in1_ap is None else in1_ap
        self.fw.op("dve", lambda e: e.scalar_tensor_tensor(o, a, scalar, b, op0, op1),
                   reads=[in0_b, in1_b] + list(extra_reads), writes=[out_b])

    def copy(self, eng, out_b, in_b, out_ap=None, in_ap=None):
        o = out_b.ap if out_ap is None else out_ap
        i = in_b.ap if in_ap is None else in_ap
        if eng == "act":
            fn = lambda e: e.copy(o, i)
        else:
            fn = lambda e: e.tensor_copy(o, i)
        self.fw.op(eng, fn, reads=[in_b], writes=[out_b])

    def rmsnorm_fm(self, src, gkey, dst, n=TT):
        ps = self.ps.next()
        for k in range(8):
            sq = self.tmpb.next()
            self.act_(AF.Square, sq, src[k], out_ap=sq.ap[:, 0:n], in_ap=src[k].ap[:, 0:n])
            self.mm(ps, ps.ap[:, 0:n], self.onesblk.ap[:, 0:128], sq.ap[:, 0:n], k == 0, k == 7, [sq, self.onesblk],
                    signal=True)
        rstd = self.tmpf.next()
        self.act_(AF.Ln, rstd, ps, out_ap=rstd.ap[:, 0:n], in_ap=ps.ap[:, 0:n], bias=self.epsb.ap[:, 0:1],
                  scale=1.0 / D, extra_reads=[self.epsb])
        self.act_(AF.Exp, rstd, rstd, out_ap=rstd.ap[:, 0:n], in_ap=rstd.ap[:, 0:n], scale=-0.5)
        for k in range(8):
            self.stt(dst[k], src[k], self.vcol(gkey, k), rstd, ALU.mult, ALU.mult,
                     out_ap=dst[k].ap[:, 0:n], in0_ap=src[k].ap[:, 0:n], in1_ap=rstd.ap[:, 0:n],
                     extra_reads=[self.vecs])

    def headnorm(self, ps, gkey, out_b, out_ap, n=TT):
        sq = self.tmpb.next()
        self.act_(AF.Square, sq, ps, out_ap=sq.ap[:, 0:n], in_ap=ps.ap[:, 0:n])
        pn = self.ps.next()
        self.mm(pn, pn.ap[:, 0:n], self.onesblk.ap[:, 128:256], sq.ap[:, 0:n], True, True, [sq, self.onesblk])
        rstd = self.tmpf.next()
        self.act_(AF.Ln, rstd, pn, out_ap=rstd.ap[:, 0:n], in_ap=pn.ap[:, 0:n], bias=self.epsb.ap[:, 0:1],
                  scale=1.0 / HD, extra_reads=[self.epsb])
        self.act_(AF.Exp, rstd, rstd, out_ap=rstd.ap[:, 0:n], in_ap=rstd.ap[:, 0:n], scale=-0.5)
        self.stt(out_b, ps, self.vcol(gkey), rstd, ALU.mult, ALU.mult,
                 out_ap=out_ap, in0_ap=ps.ap[:, 0:n], in1_ap=rstd.ap[:, 0:n], extra_reads=[self.vecs])

    def load_x(self, t):
        fw = self.fw
        fw.dma("act", self.xin.ap, self.x_d[t * TT:(t + 1) * TT, :].rearrange("(c p) d -> p c d", p=128),
               writes=[self.xin])
        for k in range(8):
            ps = self.ps.next()
            for tc in range(4):
                o = ps.ap[:, tc * 128:(tc + 1) * 128]
                i = self.xin.ap[:, tc, k * 128:(k + 1) * 128]
                fw.op("pe", lambda e, o=o, i=i: e.transpose(o, i, self.ident.ap),
                      reads=[self.xin, self.ident], writes=[ps], signal=(tc == 3))
            self.copy("act" if k % 2 else "dve", self.hT[k], ps)

    def store_out(self, t):
        fw = self.fw
        for tc in range(4):
            for half in range(2):
                ps = self.ps.next()
                for kk in range(4):
                    k = half * 4 + kk
                    o = ps.ap[:, kk * 128:(kk + 1) * 128]
                    i = self.hT[k].ap[:, tc * 128:(tc + 1) * 128]
                    fw.op("pe", lambda e, o=o, i=i: e.transpose(o, i, self.ident.ap),
                          reads=[self.hT[k], self.ident], writes=[ps], signal=(kk == 3))
                self.copy("act" if half else "dve", self.xin, ps,
                          out_ap=self.xin.ap[:, tc, half * 512:(half + 1) * 512])
        fw.dma("act", self.out_d[t * TT:(t + 1) * TT, :].rearrange("(c p) d -> p c d", p=128), self.xin.ap,
               reads=[self.xin])

    def ffn(self, L, nm, gname):
        fw = self.fw
        self.rmsnorm_fm(self.hT, (gname, L), self.xn)
        for g in range(DFF // 256):
            wg_b, wg = self.wtile((nm + "i", L, "g", g))
            wu_b, wu = self.wtile((nm + "i", L, "u", g))
            for fc in range(2):
                f = g * 2 + fc
                pg = self.ps.next()
                pu = self.ps.next()
                for k in range(8):
                    self.mm(pg, pg.ap, wg[:, k, fc * 128:(fc + 1) * 128], self.xn[k].ap, k == 0, k == 7,
                            [wg_b, self.xn[k]])
                for k in range(8):
                    self.mm(pu, pu.ap, wu[:, k, fc * 128:(fc + 1) * 128], self.xn[k].ap, k == 0, k == 7,
                            [wu_b, self.xn[k]])
                sg = self.tmpf.next()
                self.act_(AF.Silu, sg, pg)
                self.tt("dve", self.act[f], sg, pu, ALU.mult)
        for oc in range(8):
            w0b, w0 = self.wtile((nm + "o", L, oc, 0))
            w1b, w1 = self.wtile((nm + "o", L, oc, 1))
            py = self.ps.next()
            for k in range(FC):
                wb, w = (w0b, w0) if k < 11 else (w1b, w1)
                self.mm(py, py.ap, w[:, k % 11, :], self.act[k].ap, k == 0, k == FC - 1, [wb, self.act[k]])
            self.stt(self.hT[oc], py, 0.5, self.hT[oc], ALU.mult, ALU.add)

    def _prologue(self):
        fw = self.fw
        fw.dma("act", self.ident.ap, self.ident_d, writes=[self.ident])
        fw.dma("act", self.tri.ap, self.tri_d, writes=[self.tri])
        fw.dma("act", self.oh.ap, self.oh_d, writes=[self.oh])
        fw.dma("act", self.swapm.ap, self.swap_d, writes=[self.swapm])
        fw.dma("act", self.vecs.ap, self.vecs_d, writes=[self.vecs])
        st = self.tmpf.next()
        fw.dma("act", st.ap[:, 0:256], self.onesblk_d, writes=[st])
        self.copy("dve", self.onesblk, st, in_ap=st.ap[:, 0:256])
        fw.op("dve", lambda e: e.memset(self.epsb.ap, EPS), writes=[self.epsb])
        fw.op("dve", lambda e: e.memset(self.ones12.ap, 1.0), writes=[self.ones12])
        if "mix0" in self.stages or "mix1" in self.stages:
            self._prologue_mix()

    def _tile(self, t):
        st = self.stages
        self.load_x(t)
        if "ffn1a" in st:
            self.ffn(0, "f1", "ffn1_norm_g")
        if "mix0" in st:
            self.mix_lru(t)
        if "ffn2a" in st:
            self.ffn(0, "f2", "ffn2_norm_g")
        if "ffn1b" in st:
            self.ffn(1, "f1", "ffn1_norm_g")
        if "mix1" in st:
            self.mix_fox(t)
        if "ffn2b" in st:
            self.ffn(1, "f2", "ffn2_norm_g")
        self.store_out(t)

    def _emit(self):
        nc = self.nc
        fw = self.fw
        with nc.Block() as block:
            @block.tensor
            def _(e):
                fw.replay("pe", e)

            @block.scalar
            def _(e):
                fw.replay("act", e)

            @block.vector
            def _(e):
                fw.replay("dve", e)

            @block.gpsimd
            def _(e):
                fw.replay("pool", e)

            @block.sync
            def _(e):
                fw.replay("sp", e)
        self.stack.close()


_CACHE = {}


def host_inputs(inp):
    wts, _ = pack_weights(inp)
    vecs = pack_vecs(inp)
    c = make_consts()
    tri = np.where(np.arange(128)[:, None] <= np.arange(128)[None, :], 0.0, NEG).astype(np.float32)
    oh = np.zeros((128, 16), np.float32)
    for h in range(12):
        oh[h, h] = 1.0
    sw = np.zeros((128, 128), np.float32)
    for k in range(128):
        sw[k, (k + 64) % 128] = 1.0
    shared = {"wts": wts, "vecs": vecs, "ident": c["ident"], "ones_blk": c["ones_blk"], "tri": tri, "onehot": oh,
              "swapm": sw}
    return shared


def kernel(**inputs):
    inp = {k: np.asarray(v) for k, v in inputs.items()}
    shared = host_inputs(inp)
    if "prog" not in _CACHE:
        _CACHE["prog"] = Prog()
    prog = _CACHE["prog"]
    x = np.ascontiguousarray(inp["x"], dtype=np.float32)
    mem = np.ascontiguousarray(inp["mem"], dtype=np.float32)
    in_maps = []
    for b in range(8):
        m = dict(shared)
        m["x"] = x[b]
        m["mem"] = mem[b]
        in_maps.append(m)
    res = run_bass_kernel_spmd(prog.nc, in_maps, core_ids=list(range(8)))
    out = np.stack([np.asarray(r["out"], dtype=np.float32).reshape(S, D) for r in res.results], axis=0)
    return out


def _add_mixers():
    def _prologue_mix(self):
        fw = self.fw
        t = self.tmpf.next()
        lam = self.vecs.ap[:, self.vc["lam"]:self.vc["lam"] + 6]
        one = self.ones12.ap[:, 0:1]
        fw.op("act", lambda e: e.activation(t.ap[:, 0:6], lam, AF.Exp, scale=-1.0), reads=[self.vecs], writes=[t])
        fw.op("act", lambda e: e.activation(t.ap[:, 0:6], t.ap[:, 0:6], AF.Ln, bias=one), reads=[t, self.ones12], writes=[t])
        self.ts("dve", self.cL, t, -8.0, None, ALU.mult, None, out_ap=self.cL.ap[:, 0:6], in_ap=t.ap[:, 0:6])
        self.ts("dve", self.cL, t, -16.0, None, ALU.mult, None, out_ap=self.cL.ap[:, 6:12], in_ap=t.ap[:, 0:6])
        self.ts("dve", self.negbf, self.vecs, -1.0, None, ALU.mult, None, in_ap=self.vcol("b_f"))
        for c in range(6):
            fw.op("dve", lambda e, a=self.hstate[c].ap: e.memset(a, 0.0), writes=[self.hstate[c]])
            fw.op("dve", lambda e, a=self.xbr[c].ap[:, 0:4]: e.memset(a, 0.0), writes=[self.xbr[c]])
        fw.op("dve", lambda e: e.memset(self.cumstate.ap, 0.0), writes=[self.cumstate])
        fw.op("pool", lambda e: e.memset(self.vtok.ap, 1.0), writes=[self.vtok])
        fw.dma("act", self.xin.ap[:, 0:2, :], self.mem_d.rearrange("(c p) d -> p c d", p=128), writes=[self.xin])
        for k in range(8):
            ps = self.ps.next()
            for tc in range(2):
                o = ps.ap[:, tc * 128:(tc + 1) * 128]
                i = self.xin.ap[:, tc, k * 128:(k + 1) * 128]
                fw.op("pe", lambda e, o=o, i=i: e.transpose(o, i, self.ident.ap),
                      reads=[self.xin, self.ident], writes=[ps], signal=(tc == 1))
            self.copy("act" if k % 2 else "dve", self.hT[k], ps, out_ap=self.hT[k].ap[:, 0:256], in_ap=ps.ap[:, 0:256])
        self.rmsnorm_fm(self.hT, "mem_norm_g", self.xn, n=NMEM)
        wkb, wk = self.wtile(("kv", 0))
        wvb, wv = self.wtile(("kv", 1))
        for c in range(2):
            ps = self.ps.next()
            for k in range(8):
                self.mm(ps, ps.ap[:, 0:NMEM], wk[:, k, c * 128:(c + 1) * 128], self.xn[k].ap[:, 0:NMEM], k == 0, k == 7,
                        [wkb, self.xn[k]])
            self.headnorm(ps, "mem_k_norm_g", self.mkT, self.mkT.ap[:, c, :], n=NMEM)
        for nch in range(2):
            ps = self.ps.next()
            for k in range(8):
                self.mm(ps, ps.ap[:, 0:256], self.xn[k].ap[:, nch * 128:(nch + 1) * 128], wv[:, k, :], k == 0, k == 7,
                        [wvb, self.xn[k]])
            self.copy("act", self.mv, ps, out_ap=self.mv.ap[:, nch, :], in_ap=ps.ap[:, 0:256])

    def cross_attn(self, L):
        for c in range(2):
            o = self.psacc.next()
            den = self.psacc.next()
            for h2 in range(2):
                h = 2 * c + h2
                b0 = 64 * h2
                for nch in range(2):
                    s = self.ps.next()
                    self.mm1(s, s.ap, self.mkT.ap[b0:b0 + 64, c, nch * 128:(nch + 1) * 128],
                             self.qmn[c].ap[b0:b0 + 64, :], True, True, [self.mkT, self.qmn[c]])
                    e_ = self.tmpb.next()
                    self.act_(AF.Exp, e_, s, scale=0.125)
                    self.mm1(o, o.ap[b0:b0 + 64, :], self.mv.ap[:, nch, h * 64:(h + 1) * 64], e_.ap,
                             nch == 0, nch == 1, [self.mv, e_])
                    self.mm1(den, den.ap[b0:b0 + 64, :], self.onesblk.ap[:, 0:64], e_.ap,
                             nch == 0, nch == 1, [self.onesblk, e_])
            rec = self.tmpf.next()
            self.act_(AF.Ln, rec, den)
            self.act_(AF.Exp, rec, rec, scale=-1.0)
            self.tt("dve", self.cat[6 + c], o, rec, ALU.mult)

    def out_proj(self, L):
        for g in range(4):
            wb, w = self.wtile(("mo", L, g))
            for fc in range(2):
                oc = 2 * g + fc
                ps = self.ps.next()
                for k in range(8):
                    self.mm(ps, ps.ap, w[:, k, fc * 128:(fc + 1) * 128], self.cat[k].ap, k == 0, k == 7,
                            [wb, self.cat[k]])
                self.tt("dve", self.hT[oc], ps, self.hT[oc], ALU.add)

    def mix_lru(self, t):
        fw = self.fw
        self.rmsnorm_fm(self.hT, ("mix_norm_g", 0), self.xn)
        if t > 0:
            for c in range(6):
                b = self.xbr[c]
                fw.op("pool", lambda e, b=b: e.tensor_copy(b.ap[:, 0:3], b.ap[:, TT:TT + 3]), reads=[b], writes=[b])
        for g in range(7):
            wb, w = self.wtile(("mi", 0, g))
            for fc in range(2):
                oc = 2 * g + fc
                ps = self.ps.next()
                for k in range(8):
                    self.mm(ps, ps.ap, w[:, k, fc * 128:(fc + 1) * 128], self.xn[k].ap, k == 0, k == 7, [wb, self.xn[k]])
                if oc < 6:
                    self.copy("act", self.xbr[oc], ps, out_ap=self.xbr[oc].ap[:, 3:TT + 3])
                elif oc < 12:
                    c = oc - 6
                    u = self.tmpf.next()
                    self.act_(AF.Square, u, ps)
                    self.ts("dve", u, u, 0.044715, 1.0, ALU.mult, ALU.add)
                    self.tt("dve", u, u, ps, ALU.mult)
                    self.act_(AF.Sigmoid, u, u, scale=1.5957691216057308)
                    self.tt("dve", self.cat[c], u, ps, ALU.mult)
                else:
                    c = oc - 12
                    self.headnorm(ps, ("memq_norm_g", 0), self.qmn[c], self.qmn[c].ap)
        one = self.ones12.ap[:, 0:1]
        for c0 in (0, 2, 4):
            cs = (c0, c0 + 1)
            acc, xcb, pr, pi, r, gi, a, m, hs = {}, {}, {}, {}, {}, {}, {}, {}, {}
            for c in cs:
                xb = self.xbr[c]
                acc[c] = self.tmpf.next()
                self.ts("dve", acc[c], xb, self.vcol("conv_w", c), self.vcol("conv_b", c), ALU.mult, ALU.add,
                        in_ap=xb.ap[:, 0:TT], extra_reads=[self.vecs])
                for tap in range(1, 4):
                    self.stt(acc[c], xb, self.vcol("conv_w", tap * 6 + c), acc[c], ALU.mult, ALU.add,
                             in0_ap=xb.ap[:, tap:tap + TT], extra_reads=[self.vecs])
            for c in cs:
                xcb[c] = self.tmpb.next()
                self.copy("pool", xcb[c], acc[c])
            for c in cs:
                wrb, wr = self.wtile(("rg", c))
                wib, wi = self.wtile(("ig", c))
                pr[c] = self.ps.next()
                self.mm(pr[c], pr[c].ap, wr[:, 0, :], xcb[c].ap, True, True, [wrb, xcb[c]])
                pi[c] = self.ps.next()
                self.mm(pi[c], pi[c].ap, wi[:, 0, :], xcb[c].ap, True, True, [wib, xcb[c]])
            for c in cs:
                r[c] = self.tmpf.next()
                self.act_(AF.Sigmoid, r[c], pr[c], bias=self.vcol("b_rg", c), extra_reads=[self.vecs])
                gi[c] = self.tmpf.next()
                self.act_(AF.Sigmoid, gi[c], pi[c], bias=self.vcol("b_ig", c), extra_reads=[self.vecs])
            for c in cs:
                a[c] = self.tmpf.next()
                self.act_(AF.Exp, a[c], r[c], scale=self.cL.ap[:, c:c + 1], extra_reads=[self.cL])
                m[c] = self.tmpf.next()
                self.act_(AF.Exp, m[c], r[c], scale=self.cL.ap[:, 6 + c:7 + c], extra_reads=[self.cL])
            for c in cs:
                self.tt("pool", gi[c], gi[c], acc[c], ALU.mult)
            for c in cs:
                self.act_(AF.Sqrt, m[c], m[c], scale=-1.0, bias=one, extra_reads=[self.ones12])
            for c in cs:
                self.tt("dve", gi[c], gi[c], m[c], ALU.mult)
            for c in cs:
                hs[c] = self.tmpf.next()
                hst = self.hstate[c]
                fw.op("dve", lambda e, hs=hs[c], a=a[c], gi=gi[c], hst=hst: e.tensor_tensor_scan(
                    hs.ap, a.ap, gi.ap, hst.ap, ALU.mult, ALU.add), reads=[a[c], gi[c], hst], writes=[hs[c]])
            for c in cs:
                self.copy("pool", self.hstate[c], hs[c], in_ap=hs[c].ap[:, TT - 1:TT])
                self.tt("dve", self.cat[c], hs[c], self.cat[c], ALU.mult)
        self.cross_attn(0)
        self.out_proj(0)

    Prog._prologue_mix = _prologue_mix
    Prog.cross_attn = cross_attn
    Prog.out_proj = out_proj
    Prog.mix_lru = mix_lru


_add_mixers()


def _add_fox():
    def mix_fox(self, t):
        fw = self.fw
        self.rmsnorm_fm(self.hT, ("mix_norm_g", 1), self.xn)
        for g in range(6):
            wb, w = self.wtile(("mi", 1, g))
            for fc in range(2):
                oc = 2 * g + fc
                ps = self.ps.next()
                for k in range(8):
                    self.mm(ps, ps.ap, w[:, k, fc * 128:(fc + 1) * 128], self.xn[k].ap, k == 0, k == 7, [wb, self.xn[k]])
                if oc < 6:
                    self.headnorm(ps, "fox_q_g", self.qT[oc], self.qT[oc].ap)
                else:
                    c = oc - 6
                    kn = self.tmpb.next()
                    self.headnorm(ps, "fox_k_g", kn, kn.ap)
                    kb = Buf(self.kc_d[c, :, t * TT:(t + 1) * TT], f"kc{c}_{t}")
                    self.kc_bufs[(c, t)] = kb
                    fw.dma("act", kb.ap, kn.ap, reads=[kn], writes=[kb])
        for g in range(3):
            wb, w = self.wtile(("mv", g))
            for tc in range(4):
                ps = self.ps.next()
                for k in range(8):
                    self.mm(ps, ps.ap[:, 0:256], self.xn[k].ap[:, tc * 128:(tc + 1) * 128], w[:, k, :], k == 0, k == 7,
                            [wb, self.xn[k]])
                pv4 = ps.ap[:, 0:256].rearrange("p (c h d) -> p c h d", c=2, h=2)
                self.copy("act", self.vtok, ps, out_ap=self.vtok.ap[:, tc, 2 * g:2 * g + 2, 0:64], in_ap=pv4[:, :, 0, :])
                self.copy("dve", self.vtok, ps, out_ap=self.vtok.ap[:, tc, 2 * g:2 * g + 2, 192:256], in_ap=pv4[:, :, 1, :])
        for c in range(6):
            vb = Buf(self.vc_d[c, :, 4 * t:4 * t + 4, :], f"vc{c}_{t}")
            self.vc_bufs[(c, t)] = vb
            fw.dma("act", vb.ap, self.vtok.ap[:, :, c, :], reads=[self.vtok], writes=[vb])
        wb, w = self.wtile(("mf",))
        ps = self.ps.next()
        for k in range(8):
            self.mm(ps, ps.ap[0:12, :], w[:, k, 0:12], self.xn[k].ap, k == 0, k == 7, [wb, self.xn[k]])
        lf = self.tmpf.next()
        self.act_(AF.Exp, lf, ps, out_ap=lf.ap[0:12, :], in_ap=ps.ap[0:12, :], scale=-1.0, bias=self.negbf.ap[0:12, :],
                  extra_reads=[self.negbf])
        self.act_(AF.Ln, lf, lf, out_ap=lf.ap[0:12, :], in_ap=lf.ap[0:12, :], bias=self.ones12.ap[0:12, 0:1],
                  extra_reads=[self.ones12])
        fw.op("dve", lambda e, lf=lf: e.tensor_tensor_scan(self.cum.ap[0:12, :], self.ones12.ap[0:12, :], lf.ap[0:12, :],
                                                            self.cumstate.ap[0:12, :], ALU.mult, ALU.subtract),
              reads=[lf, self.ones12, self.cumstate], writes=[self.cum])
        self.copy("pool", self.cumstate, self.cum, out_ap=self.cumstate.ap[0:12, :], in_ap=self.cum.ap[0:12, TT - 1:TT])
        for tc in range(4):
            tp = self.ps.next()
            fw.op("pe", lambda e, tp=tp, tc=tc: e.transpose(tp.ap[:, 0:12], self.cum.ap[0:12, tc * 128:(tc + 1) * 128],
                                                           self.ident.ap[0:12, 0:12]),
                  reads=[self.cum, self.ident], writes=[tp])
            self.ts("dve", self.negcumT, tp, -1.0, None, ALU.mult, None,
                    out_ap=self.negcumT.ap[:, 4 * t + tc, :], in_ap=tp.ap[:, 0:12])
        wb, w = self.wtile(("mq",))
        for c in range(2):
            ps = self.ps.next()
            for k in range(8):
                self.mm(ps, ps.ap, w[:, k, c * 128:(c + 1) * 128], self.xn[k].ap, k == 0, k == 7, [wb, self.xn[k]])
            self.headnorm(ps, ("memq_norm_g", 1), self.qmn[c], self.qmn[c].ap)
        nkeys = (t + 1) * TT

        def prep(c):
            for h2 in range(2):
                h = 2 * c + h2
                sel = self.tmpf.next()
                self.ts("dve", sel, self.cum, self.oh.ap[0:12, h:h + 1], None, ALU.mult, None,
                        out_ap=sel.ap[0:12, :], in_ap=self.cum.ap[0:12, :], extra_reads=[self.oh])
                pb = self.ps.next()
                self.mm1(pb, pb.ap, self.ones12.ap[0:12, 0:128], sel.ap[0:12, :], True, True, [self.ones12, sel])
                self.copy("act", self.cq[2 * (c % 2) + h2], pb)

        def finalize(c, o, den):
            rec = self.tmpf.next()
            self.act_(AF.Ln, rec, o, out_ap=rec.ap[0:64, :], in_ap=o.ap[64:128, :])
            self.act_(AF.Ln, rec, den, out_ap=rec.ap[64:128, :], in_ap=den.ap[0:64, :])
            self.act_(AF.Exp, rec, rec, scale=-1.0)
            self.tt("dve", self.cat[c], o, rec, ALU.mult, out_ap=self.cat[c].ap[0:64, :], a_ap=o.ap[0:64, :],
                    b_ap=rec.ap[0:64, :])
            self.tt("dve", self.cat[c], den, rec, ALU.mult, out_ap=self.cat[c].ap[64:128, :], a_ap=den.ap[64:128, :],
                    b_ap=rec.ap[64:128, :])

        pending_fin = None
        prep(0)
        for c in range(6):
            o = self.psacc.next()
            den = self.psacc.next()
            if c + 1 < 6:
                prep(c + 1)
            blocks = []
            kvb = {}
            for kb0 in range(0, nkeys, 1024):
                wk = min(1024, nkeys - kb0)
                for k8 in range(wk // 128):
                    for h2 in range(2):
                        blocks.append((kb0, wk, h2, k8, kb0 // 128 + k8))

            def kv_load(kb0, wk):
                if kb0 not in kvb:
                    kbuf = self.kst.next()
                    vbuf = self.vst.next()
                    tiles = range(kb0 // TT, (kb0 + wk) // TT)
                    fw.dma("sp", kbuf.ap[:, 0:wk], self.kc_d[c, :, kb0:kb0 + wk],
                           reads=[self.kc_bufs[(c, tt_)] for tt_ in tiles], writes=[kbuf])
                    fw.dma("sp", vbuf.ap[:, 0:wk // 128, :], self.vc_d[c, :, kb0 // 128:(kb0 + wk) // 128, :],
                           reads=[self.vc_bufs[(c, tt_)] for tt_ in tiles], writes=[vbuf])
                    kvb[kb0] = (kbuf, vbuf)
                return kvb[kb0]

            def emit_qk(blk):
                kb0, wk, h2, k8, kcg = blk
                kbuf, vbuf = kv_load(kb0, wk)
                h = 2 * c + h2
                b0 = 64 * h2
                diag = kcg - 4 * t
                q0 = 128 * diag if diag >= 0 else 0
                s = self.ps.next()
                self.mm1(s, s.ap[:, q0:TT], kbuf.ap[b0:b0 + 64, k8 * 128:(k8 + 1) * 128],
                         self.qT[c].ap[b0:b0 + 64, q0:TT], True, True, [kbuf, self.qT[c]])
                tm = self.tmpf.next()
                self.stt(tm, s, 0.125, self.cq[2 * (c % 2) + h2], ALU.mult, ALU.add, out_ap=tm.ap[:, q0:TT],
                         in0_ap=s.ap[:, q0:TT], in1_ap=self.cq[2 * (c % 2) + h2].ap[:, q0:TT])
                if diag >= 0:
                    self.tt("pool", tm, tm, self.tri, ALU.add, out_ap=tm.ap[:, q0:q0 + 128],
                            a_ap=tm.ap[:, q0:q0 + 128])
                p = self.tmpb.next()
                self.act_(AF.Exp, p, tm, out_ap=p.ap[:, q0:TT], in_ap=tm.ap[:, q0:TT],
                          bias=self.negcumT.ap[:, kcg, h:h + 1], extra_reads=[self.negcumT])
                return p, q0

            def emit_pv(blk, p, q0):
                kb0, wk, h2, k8, kcg = blk
                kbuf, vbuf = kvb[kb0]
                b0 = 64 * h2
                first = kcg == 0
                last = kcg == 4 * t + 3
                acc = o if h2 == 0 else den
                self.mm1(acc, acc.ap[:, q0:TT], vbuf.ap[:, k8, h2 * 128:(h2 + 1) * 128], p.ap[:, q0:TT], first, last,
                         [vbuf, p])

            pend = []
            npair = len(blocks) // 2
            for j in range(npair + 1):
                if j < npair:
                    pend.append(emit_qk(blocks[2 * j]))
                    pend.append(emit_qk(blocks[2 * j + 1]))
                if j == 0 and pending_fin is not None:
                    finalize(*pending_fin)
                    pending_fin = None
                if j >= 1:
                    emit_pv(blocks[2 * j - 2], *pend[2 * j - 2])
                    emit_pv(blocks[2 * j - 1], *pend[2 * j - 1])
            pending_fin = (c, o, den)
        finalize(*pending_fin)
        self.cross_attn(1)
        self.out_proj(1)

    Prog.mix_fox = mix_fox


_add_fox()
```

```python
import contextlib
import numpy as np
import concourse.bass as bass
import concourse.mybir as mybir
from concourse.bass_utils import run_bass_kernel_spmd

F32 = mybir.dt.float32
BF16 = mybir.dt.bfloat16
AF = mybir.ActivationFunctionType
ALU = mybir.AluOpType

D = 1024
S = 4096
DFF = 2816
NMEM = 256
HD = 64
MIXW = 768
KC = D // 128
FC = DFF // 128
EPS = 1e-6
SYNC_SAME = True


class Buf:
    __slots__ = ("ap", "lw", "rd", "name")

    def __init__(self, ap, name=""):
        self.ap = ap if isinstance(ap, bass.AP) else ap[:]
        self.lw = None
        self.rd = {}
        self.name = name


class Q:
    def __init__(self, name, sem, is_pe=False):
        self.name = name
        self.sem = sem
        self.is_pe = is_pe
        self.count = 0
        self.prog = []
        self.waited = {}
        self.dma_sems = []
        self.dma_vals = []
        self.dma_next = 0


class FW:
    def __init__(self, nc, stack, n_dma_sems=8):
        self.nc = nc
        self.q = {}
        for name, is_pe in (("pe", True), ("act", False), ("dve", False), ("pool", False), ("sp", False)):
            sem = stack.enter_context(nc.semaphore(f"prog_{name}"))
            self.q[name] = Q(name, sem, is_pe)
        for name in ("sp", "act", "pool"):
            q = self.q[name]
            for i in range(n_dma_sems):
                q.dma_sems.append(stack.enter_context(nc.semaphore(f"dma_{name}_{i}")))
                q.dma_vals.append(0)

    def _deps(self, q, reads, writes, extra=()):
        deps = {}

        def need(tok):
            if tok is None:
                return
            k = id(tok[0])
            if k not in deps or deps[k][1] < tok[1]:
                deps[k] = tok

        for b in reads:
            need(b.lw)
        for b in writes:
            need(b.lw)
            for tok in b.rd.values():
                need(tok)
        for tok in extra:
            need(tok)
        for k, (sem, val, owner) in deps.items():
            if owner is q and (q.is_pe or not SYNC_SAME):
                continue
            if q.waited.get(k, 0) >= val:
                continue
            q.waited[k] = val
            q.prog.append(("wait", sem, val))

    def op(self, qname, fn, reads=(), writes=(), signal=True):
        q = self.q[qname]
        self._deps(q, reads, writes)
        if signal:
            q.count += 1
            tok = (q.sem, q.count, q)
        else:
            tok = (q.sem, q.count + 1, q)
        q.prog.append(("op", fn, signal))
        for b in writes:
            b.lw = tok
            b.rd = {}
        for b in reads:
            b.rd[q.name] = tok

    def dma(self, qname, out_ap, in_ap, reads=(), writes=()):
        q = self.q[qname]
        i = q.dma_next
        q.dma_next = (i + 1) % len(q.dma_sems)
        sem = q.dma_sems[i]
        cur = q.dma_vals[i]
        extra = [(sem, cur, None)] if cur > 0 else []
        self._deps(q, reads, writes, extra)
        q.dma_vals[i] = cur + 16
        tok = (sem, cur + 16, None)
        q.prog.append(("dma", out_ap, in_ap, sem))
        for b in writes:
            b.lw = tok
            b.rd = {}
        for b in reads:
            b.rd[("dma", id(sem))] = tok

    def finish(self):
        for name in ("sp", "act", "pool"):
            q = self.q[name]
            for sem, val in zip(q.dma_sems, q.dma_vals):
                if val > 0 and q.waited.get(id(sem), 0) < val:
                    q.prog.append(("wait", sem, val))

    def replay(self, qname, eng):
        q = self.q[qname]
        for item in q.prog:
            if item[0] == "wait":
                eng.wait_ge(item[1], item[2])
            elif item[0] == "op":
                inst = item[1](eng)
                if item[2]:
                    inst.then_inc(q.sem, 1)
            else:
                eng.dma_start(out=item[1], in_=item[2]).then_inc(item[3], 16)


class Ring:
    def __init__(self, bufs):
        self.bufs = bufs
        self.i = 0

    def next(self):
        b = self.bufs[self.i]
        self.i = (self.i + 1) % len(self.bufs)
        return b


WT_ELEMS = 2048


def _tiles_of(W, col_ranges, nk_split):
    out = []
    Kin = W.shape[0]
    kcs = Kin // 128
    Wr = W.reshape(kcs, 128, W.shape[1])
    for (c0, nc_) in col_ranges:
        k0 = 0
        for nk in nk_split:
            t = Wr[k0:k0 + nk, :, c0:c0 + nc_]
            t = np.transpose(t, (1, 0, 2)).reshape(128, nk * nc_)
            out.append(t)
            k0 += nk
        assert k0 == kcs
    return out


class WeightPlan:
    def __init__(self):
        self.tiles = []
        self.arrays = []

    def add(self, arrs):
        idx0 = len(self.arrays)
        self.arrays.extend(arrs)
        return list(range(idx0, idx0 + len(arrs)))

    def pack(self):
        out = np.zeros((len(self.arrays), 128, WT_ELEMS), np.float32)
        for i, a in enumerate(self.arrays):
            out[i, :, :a.shape[1]] = a
        return out


def weight_specs():
    sp = []
    for L in range(2):
        for nm, src_in, src_out in (("f1", "ffn1_w_in", "ffn1_w_out"), ("f2", "ffn2_w_in", "ffn2_w_out")):
            for g in range(DFF // 256):
                sp.append(((nm + "i", L, "g", g), src_in, L, g * 256, 256, 0, 8))
                sp.append(((nm + "i", L, "u", g), src_in, L, DFF + g * 256, 256, 0, 8))
            for oc in range(8):
                sp.append(((nm + "o", L, oc, 0), src_out, L, oc * 128, 128, 0, 11))
                sp.append(((nm + "o", L, oc, 1), src_out, L, oc * 128, 128, 11, 11))
        for g in range(4):
            sp.append((("mo", L, g), "mix_w_out", L, g * 256, 256, 0, 8))
    for g in range(7):
        sp.append((("mi", 0, g), "lru_w_in", 0, g * 256, 256, 0, 8))
    for g in range(6):
        sp.append((("mi", 1, g), "fox_w_in", 0, g * 256, 256, 0, 8))
    for g in range(3):
        sp.append((("mv", g), "fox_w_in", 0, 1536 + g * 256, 256, 0, 8))
    sp.append((("mf",), "fox_w_in", 0, 2304, 12, 0, 8))
    sp.append((("mq",), "fox_w_in", 0, 2316, 256, 0, 8))
    for g in range(2):
        sp.append((("kv", g), "mem_w_kv", None, g * 256, 256, 0, 8))
    for c in range(6):
        sp.append((("rg", c), "lru_w_rg", 0, c, 0, 0, 1))
        sp.append((("ig", c), "lru_w_ig", 0, c, 0, 0, 1))
    return sp


def pack_weights(inp):
    sp = weight_specs()
    out = np.zeros((len(sp), 128, WT_ELEMS), np.float32)
    index = {}
    for i, (key, src, L, c0, ncols, k0, nk) in enumerate(sp):
        index[key] = i
        W = inp[src]
        if key[0] in ("rg", "ig"):
            blk = W[0]
            c = c0
            out[i, 0:64, 0:64] = blk[2 * c]
            out[i, 64:128, 64:128] = blk[2 * c + 1]
            continue
        if L is not None:
            W = W[L]
        Wr = W.reshape(W.shape[0] // 128, 128, W.shape[1])
        t = Wr[k0:k0 + nk, :, c0:c0 + ncols]
        out[i, :, :nk * ncols] = np.transpose(t, (1, 0, 2)).reshape(128, nk * ncols)
    return out, index


def vec_cols():
    cols = {}
    n = 0
    for L in range(2):
        for nm in ("ffn1_norm_g", "mix_norm_g", "ffn2_norm_g"):
            cols[(nm, L)] = n
            n += 8
    cols["mem_norm_g"] = n; n += 8
    cols["mem_k_norm_g"] = n; n += 1
    cols[("memq_norm_g", 0)] = n; n += 1
    cols[("memq_norm_g", 1)] = n; n += 1
    cols["conv_w"] = n; n += 24
    cols["conv_b"] = n; n += 6
    cols["b_rg"] = n; n += 6
    cols["b_ig"] = n; n += 6
    cols["lam"] = n; n += 6
    cols["b_f"] = n; n += 1
    cols["fox_q_g"] = n; n += 1
    cols["fox_k_g"] = n; n += 1
    cols["_n"] = n
    return cols


def pack_vecs(inp):
    cols = vec_cols()
    v = np.zeros((128, cols["_n"]), np.float32)

    def fm(a):
        return a.reshape(-1, 128).T

    for L in range(2):
        for nm in ("ffn1_norm_g", "mix_norm_g", "ffn2_norm_g"):
            v[:, cols[(nm, L)]:cols[(nm, L)] + 8] = fm(inp[nm][L])
    v[:, cols["mem_norm_g"]:cols["mem_norm_g"] + 8] = fm(inp["mem_norm_g"])
    rep = lambda a: np.concatenate([a, a])
    v[:, cols["mem_k_norm_g"]] = rep(inp["mem_k_norm_g"])
    for L in range(2):
        v[:, cols[("memq_norm_g", L)]] = rep(inp["memq_norm_g"][L])
    for tap in range(4):
        v[:, cols["conv_w"] + tap * 6: cols["conv_w"] + tap * 6 + 6] = fm(inp["lru_conv_w"][0, tap])
    v[:, cols["conv_b"]:cols["conv_b"] + 6] = fm(inp["lru_conv_b"][0])
    v[:, cols["b_rg"]:cols["b_rg"] + 6] = fm(inp["lru_b_rg"][0])
    v[:, cols["b_ig"]:cols["b_ig"] + 6] = fm(inp["lru_b_ig"][0])
    v[:, cols["lam"]:cols["lam"] + 6] = fm(inp["lru_lambda"][0])
    v[0:12, cols["b_f"]] = inp["fox_b_f"][0]
    v[:, cols["fox_q_g"]] = rep(inp["fox_q_norm_g"][0])
    v[:, cols["fox_k_g"]] = rep(inp["fox_k_norm_g"][0])
    return v


NEG = -30000.0


def make_consts():
    c = {}
    c["ident"] = np.eye(128, dtype=np.float32)
    ones = np.ones((128, 128), np.float32)
    blk = np.zeros((128, 128), np.float32)
    blk[0:64, 0:64] = 1.0
    blk[64:128, 64:128] = 1.0
    c["ones_blk"] = np.concatenate([ones, blk], axis=1)
    md = np.zeros((128, 4, 512), np.float32)
    for kk in range(4):
        key = kk * 128 + np.arange(128)[:, None]
        qry = np.arange(512)[None, :]
        md[:, kk, :] = np.where(key <= qry, 0.0, NEG)
    c["maskd"] = md.reshape(128, 2048)
    sel = np.zeros((128, 12, 128), np.float32)
    for h in range(12):
        sel[h, h, :] = 1.0
    c["sel"] = sel.reshape(128, 12 * 128)
    return c


TT = 512
NCONST_VEC = None


class Prog:
    def __init__(self, n_tiles=S // TT, stages=("ffn1a", "mix0", "ffn2a", "ffn1b", "mix1", "ffn2b"), dbg=False):
        self.n_tiles = n_tiles
        self.stages = stages
        self.dbg = dbg
        self.stack = contextlib.ExitStack()
        nc = self.nc = bass.Bass("TRN2", target_bir_lowering=False)
        self.fw = FW(nc, self.stack)
        self.vc = vec_cols()
        self.widx = {k[0]: i for i, k in enumerate(weight_specs())}
        self.wspec = {k[0]: k for k in weight_specs()}
        nW = len(self.widx)
        dt = nc.dram_tensor
        self.x_d = dt("x", [S, D], F32, kind="ExternalInput").ap()
        self.mem_d = dt("mem", [NMEM, D], F32, kind="ExternalInput").ap()
        self.wts_d = dt("wts", [nW, 128, WT_ELEMS], F32, kind="ExternalInput").ap()
        self.vecs_d = dt("vecs", [128, self.vc["_n"]], F32, kind="ExternalInput").ap()
        self.ident_d = dt("ident", [128, 128], F32, kind="ExternalInput").ap()
        self.onesblk_d = dt("ones_blk", [128, 256], F32, kind="ExternalInput").ap()
        self.tri_d = dt("tri", [128, 128], F32, kind="ExternalInput").ap()
        self.oh_d = dt("onehot", [128, 16], F32, kind="ExternalInput").ap()
        self.swap_d = dt("swapm", [128, 128], F32, kind="ExternalInput").ap()
        self.out_d = dt("out", [S, D], F32, kind="ExternalOutput").ap()
        self.kc_d = dt("kcache", [6, 128, S], BF16, kind="Internal").ap()
        self.vc_d = dt("vcache", [6, 128, S // 128, 256], BF16, kind="Internal").ap()
        self.wbf_d = dt("wbf16", [nW, 128, WT_ELEMS], BF16, kind="Internal").ap()
        self.wconv = {}
        self.ncast = 0
        if dbg:
            self.dbg_d = dt("dbg", [128, 8 * 512], F32, kind="ExternalOutput").ap()
        self.kc_bufs = {}
        self.vc_bufs = {}
        self._alloc()
        self._prologue()
        for t in range(n_tiles):
            self._tile(t)
        self.fw.finish()
        self._emit()

    def sb(self, name, shape, dtype):
        return self.stack.enter_context(self.nc.sbuf_tensor("sb_" + name, shape, dtype))

    def _alloc(self):
        nc = self.nc
        sb = self.sb
        B = Buf
        self.ident = B(sb("ident", [128, 128], F32))
        self.onesblk = B(sb("onesblk", [128, 256], BF16))
        self.tri = B(sb("tri", [128, 128], F32))
        self.oh = B(sb("oh", [128, 16], F32))
        self.swapm = B(sb("swapm", [128, 128], F32))
        self.ones12 = B(sb("ones12", [128, 512], F32))
        self.vecs = B(sb("vecs", [128, self.vc["_n"]], F32))
        self.cL = B(sb("cL", [128, 12], F32))
        self.negbf = B(sb("negbf", [128, 1], F32))
        self.epsb = B(sb("epsb", [128, 1], F32))
        hT = sb("hT", [128, 8, TT], F32)
        self.hT = [B(hT[:, k, :], f"hT{k}") for k in range(8)]
        xn = sb("xn", [128, 8, TT], BF16)
        self.xn = [B(xn[:, k, :], f"xn{k}") for k in range(8)]
        act = sb("act", [128, FC, TT], BF16)
        self.act = [B(act[:, k, :], f"act{k}") for k in range(FC)]
        cat = sb("cat", [128, 8, TT], BF16)
        self.cat = [B(cat[:, k, :], f"cat{k}") for k in range(8)]
        self.wstage = Ring([B(sb(f"wst{i}", [128, WT_ELEMS], F32), f"wst{i}") for i in range(2)])
        self.wbf = Ring([B(sb(f"wbf{i}", [128, WT_ELEMS], BF16), f"wbf{i}") for i in range(6)])
        self.xin = B(sb("xin", [128, 4, D], F32), "xin")
        self.yout = B(sb("yout", [128, 4, D], F32), "yout") if False else self.xin
        self.mkT = B(sb("mkT", [128, 2, NMEM], BF16), "mkT")
        self.mv = B(sb("mv", [128, 2, 256], BF16), "mv")
        self.tmpf = Ring([B(sb(f"tmpf{i}", [128, TT], F32), f"tmpf{i}") for i in range(10)])
        self.tmpb = Ring([B(sb(f"tmpb{i}", [128, TT], BF16), f"tmpb{i}") for i in range(6)])
        xbr = sb("xbr", [128, 6, TT + 4], F32)
        self.xbr_t = xbr
        self.xbr = [B(xbr[:, c, :], f"xbr{c}") for c in range(6)]
        self.hstate = [B(sb(f"hstate{c}", [128, 1], F32), f"hstate{c}") for c in range(6)]
        qT = sb("qT", [128, 6, TT], BF16)
        self.qT = [B(qT[:, c, :], f"qT{c}") for c in range(6)]
        self.kst = Ring([B(sb(f"kst{i}", [128, 1024], BF16), f"kst{i}") for i in range(3)])
        self.vst = Ring([B(sb(f"vst{i}", [128, 8, 256], BF16), f"vst{i}") for i in range(3)])
        self.cum = B(sb("cum", [128, TT], F32), "cum")
        self.cumstate = B(sb("cumstate", [128, 1], F32), "cumstate")
        self.negcumT = B(sb("negcumT", [128, S // 128, 12], F32), "negcumT")
        self.vtok = B(sb("vtok", [128, 4, 6, 256], BF16), "vtok")
        self.ps = Ring([B(self.stack.enter_context(nc.psum_tensor(f"ps{i}", [128, 512], F32)), f"ps{i}")
                        for i in range(4)])
        self.psacc = Ring([B(self.stack.enter_context(nc.psum_tensor(f"pa{i}", [128, 512], F32)), f"pa{i}")
                           for i in range(4)])
        qmn = sb("qmn", [128, 2, TT], BF16)
        self.qmn = [B(qmn[:, c, :], f"qmn{c}") for c in range(2)]
        self.cq = [B(sb(f"cq{i}", [128, TT], F32), f"cq{i}") for i in range(4)]

    def vcol(self, key, off=0, n=1, p0=0, p1=128):
        c = self.vc[key] + off
        return self.vecs.ap[p0:p1, c:c + n]

    def wtile(self, key):
        fw = self.fw
        _, src, L, c0, ncols, k0, nk = self.wspec[key]
        if key[0] in ("rg", "ig"):
            nk, ncols = 1, 128
        n = nk * ncols
        idx = self.widx[key]
        wb = self.wbf.next()
        if key in self.wconv:
            db = self.wconv[key]
            fw.dma("sp", wb.ap[:, 0:n], db.ap, reads=[db], writes=[wb])
        else:
            st = self.wstage.next()
            fw.dma("sp", st.ap[:, 0:n], self.wts_d[idx, :, 0:n], writes=[st])
            eng = "act"
            self.ncast += 1
            self.copy(eng, wb, st, out_ap=wb.ap[:, 0:n], in_ap=st.ap[:, 0:n])
            if key[0] != "kv" and self.n_tiles > 1:
                db = Buf(self.wbf_d[idx, :, 0:n], "wd")
                self.wconv[key] = db
                fw.dma(eng, db.ap, wb.ap[:, 0:n], reads=[wb], writes=[db])
        return wb, wb.ap[:, 0:n].rearrange("p (k n) -> p k n", k=nk)

    def mm(self, ps, out_ap, lhsT, rhs, start, stop, reads, signal=None):
        self.fw.op("pe", lambda e: e.matmul(out_ap, lhsT, rhs, start=start, stop=stop),
                   reads=reads, writes=[ps], signal=(stop if signal is None else signal))

    def mm1(self, ps, out_ap, lhsT, rhs, start, stop, reads):
        self.fw.op("pe", lambda e: e.matmul(out_ap, lhsT, rhs, start=start, stop=stop),
                   reads=reads, writes=[ps], signal=True)

    def act_(self, func, out_b, in_b, out_ap=None, in_ap=None, extra_reads=(), **kw):
        o = out_b.ap if out_ap is None else out_ap
        i = in_b.ap if in_ap is None else in_ap
        self.fw.op("act", lambda e: e.activation(o, i, func, **kw), reads=[in_b] + list(extra_reads), writes=[out_b])

    def tt(self, eng, out_b, a_b, b_b, op, out_ap=None, a_ap=None, b_ap=None):
        o = out_b.ap if out_ap is None else out_ap
        a = a_b.ap if a_ap is None else a_ap
        b = b_b.ap if b_ap is None else b_ap
        self.fw.op(eng, lambda e: e.tensor_tensor(o, a, b, op), reads=[a_b, b_b], writes=[out_b])

    def ts(self, eng, out_b, in_b, s1, s2, op0, op1, out_ap=None, in_ap=None, extra_reads=()):
        o = out_b.ap if out_ap is None else out_ap
        i = in_b.ap if in_ap is None else in_ap
        if op1 is None:
            fn = lambda e: e.tensor_scalar(o, i, s1, None, op0)
        else:
            fn = lambda e: e.tensor_scalar(o, i, s1, s2, op0, op1)
        self.fw.op(eng, fn, reads=[in_b] + list(extra_reads), writes=[out_b])

    def stt(self, out_b, in0_b, scalar, in1_b, op0, op1, out_ap=None, in0_ap=None, in1_ap=None, extra_reads=()):
        o = out_b.ap if out_ap is None else out_ap
        a = in0_b.ap if in0_ap is None else in0_ap
        b = in1_b.ap if in1_ap is None else in1_ap
        self.fw.op("dve", lambda e: e.scalar_tensor_tensor(o, a, scalar, b, op0, op1),
                   reads=[in0_b, in1_b] + list(extra_reads), writes=[out_b])

    def copy(self, eng, out_b, in_b, out_ap=None, in_ap=None):
        o = out_b.ap if out_ap is None else out_ap
        i = in_b.ap if in_ap is None else in_ap
        if eng == "act":
            fn = lambda e: e.copy(o, i)
        else:
            fn = lambda e: e.tensor_copy(o, i)
        self.fw.op(eng, fn, reads=[in_b], writes=[out_b])

    def rmsnorm_fm(self, src, gkey, dst, n=TT):
        ps = self.ps.next()
        for k in range(8):
            sq = self.tmpb.next()
            self.act_(AF.Square, sq, src[k], out_ap=sq.ap[:, 0:n], in_ap=src[k].ap[:, 0:n])
            self.mm(ps, ps.ap[:, 0:n], self.onesblk.ap[:, 0:128], sq.ap[:, 0:n], k == 0, k == 7, [sq, self.onesblk],
                    signal=True)
        rstd = self.tmpf.next()
        self.act_(AF.Ln, rstd, ps, out_ap=rstd.ap[:, 0:n], in_ap=ps.ap[:, 0:n], bias=self.epsb.ap[:, 0:1],
                  scale=1.0 / D, extra_reads=[self.epsb])
        self.act_(AF.Exp, rstd, rstd, out_ap=rstd.ap[:, 0:n], in_ap=rstd.ap[:, 0:n], scale=-0.5)
        for k in range(8):
            self.stt(dst[k], src[k], self.vcol(gkey, k), rstd, ALU.mult, ALU.mult,
                     out_ap=dst[k].ap[:, 0:n], in0_ap=src[k].ap[:, 0:n], in1_ap=rstd.ap[:, 0:n],
                     extra_reads=[self.vecs])

    def headnorm(self, ps, gkey, out_b, out_ap, n=TT):
        sq = self.tmpb.next()
        self.act_(AF.Square, sq, ps, out_ap=sq.ap[:, 0:n], in_ap=ps.ap[:, 0:n])
        pn = self.ps.next()
        self.mm(pn, pn.ap[:, 0:n], self.onesblk.ap[:, 128:256], sq.ap[:, 0:n], True, True, [sq, self.onesblk])
        rstd = self.tmpf.next()
        self.act_(AF.Ln, rstd, pn, out_ap=rstd.ap[:, 0:n], in_ap=pn.ap[:, 0:n], bias=self.epsb.ap[:, 0:1],
                  scale=1.0 / HD, extra_reads=[self.epsb])
        self.act_(AF.Exp, rstd, rstd, out_ap=rstd.ap[:, 0:n], in_ap=rstd.ap[:, 0:n], scale=-0.5)
        self.stt(out_b, ps, self.vcol(gkey), rstd, ALU.mult, ALU.mult,
                 out_ap=out_ap, in0_ap=ps.ap[:, 0:n], in1_ap=rstd.ap[:, 0:n], extra_reads=[self.vecs])

    def load_x_dma(self, t):
        self.fw.dma("act", self.xin.ap, self.x_d[t * TT:(t + 1) * TT, :].rearrange("(c p) d -> p c d", p=128),
                    writes=[self.xin])

    def load_x(self, t):
        fw = self.fw
        for k in range(8):
            ps = self.ps.next()
            for tc in range(4):
                o = ps.ap[:, tc * 128:(tc + 1) * 128]
                i = self.xin.ap[:, tc, k * 128:(k + 1) * 128]
                fw.op("pe", lambda e, o=o, i=i: e.transpose(o, i, self.ident.ap),
                      reads=[self.xin, self.ident], writes=[ps], signal=(tc == 3))
            self.copy("act" if k % 2 else "dve", self.hT[k], ps)

    def store_out(self, t):
        fw = self.fw
        for tc in range(4):
            for half in range(2):
                ps = self.ps.next()
                for kk in range(4):
                    k = half * 4 + kk
                    o = ps.ap[:, kk * 128:(kk + 1) * 128]
                    i = self.hT[k].ap[:, tc * 128:(tc + 1) * 128]
                    fw.op("pe", lambda e, o=o, i=i: e.transpose(o, i, self.ident.ap),
                          reads=[self.hT[k], self.ident], writes=[ps], signal=(kk == 3))
                stg = self.tmpf.next()
                self.copy("act" if half else "dve", stg, ps)
                r0 = t * TT + tc * 128
                fw.dma("act", self.out_d[r0:r0 + 128, half * 512:(half + 1) * 512], stg.ap, reads=[stg])

    def ffn(self, L, nm, gname):
        fw = self.fw
        self.rmsnorm_fm(self.hT, (gname, L), self.xn)
        for g in range(DFF // 256):
            wg_b, wg = self.wtile((nm + "i", L, "g", g))
            wu_b, wu = self.wtile((nm + "i", L, "u", g))
            if g == 0:
                grp = []
                for fc in range(2):
                    grp.append((self.ps.next(), wg_b, wg, fc))
                    grp.append((self.ps.next(), wu_b, wu, fc))
                for k in range(8):
                    for (pb, wb_, w_, fc) in grp:
                        self.mm(pb, pb.ap, w_[:, k, fc * 128:(fc + 1) * 128], self.xn[k].ap, k == 0, k == 7,
                                [wb_, self.xn[k]])
                for fc in range(2):
                    sg = self.tmpf.next()
                    self.act_(AF.Silu, sg, grp[2 * fc][0])
                    self.tt("dve", self.act[fc], sg, grp[2 * fc + 1][0], ALU.mult)
                continue
            for fc in range(2):
                f = g * 2 + fc
                pg = self.ps.next()
                pu = self.ps.next()
                for k in range(8):
                    self.mm(pg, pg.ap, wg[:, k, fc * 128:(fc + 1) * 128], self.xn[k].ap, k == 0, k == 7,
                            [wg_b, self.xn[k]])
                for k in range(8):
                    self.mm(pu, pu.ap, wu[:, k, fc * 128:(fc + 1) * 128], self.xn[k].ap, k == 0, k == 7,
                            [wu_b, self.xn[k]])
                sg = self.tmpf.next()
                self.act_(AF.Silu, sg, pg)
                self.tt("dve", self.act[f], sg, pu, ALU.mult)
        for oc in range(8):
            w0b, w0 = self.wtile((nm + "o", L, oc, 0))
            w1b, w1 = self.wtile((nm + "o", L, oc, 1))
            py = self.ps.next()
            for k in range(FC):
                wb, w = (w0b, w0) if k < 11 else (w1b, w1)
                self.mm(py, py.ap, w[:, k % 11, :], self.act[k].ap, k == 0, k == FC - 1, [wb, self.act[k]])
            self.stt(self.hT[oc], py, 0.5, self.hT[oc], ALU.mult, ALU.add)

    def _prologue(self):
        fw = self.fw
        fw.dma("act", self.ident.ap, self.ident_d, writes=[self.ident])
        fw.dma("act", self.tri.ap, self.tri_d, writes=[self.tri])
        fw.dma("act", self.oh.ap, self.oh_d, writes=[self.oh])
        fw.dma("act", self.swapm.ap, self.swap_d, writes=[self.swapm])
        fw.dma("act", self.vecs.ap, self.vecs_d, writes=[self.vecs])
        st = self.tmpf.next()
        fw.dma("act", st.ap[:, 0:256], self.onesblk_d, writes=[st])
        self.copy("dve", self.onesblk, st, in_ap=st.ap[:, 0:256])
        fw.op("dve", lambda e: e.memset(self.epsb.ap, EPS), writes=[self.epsb])
        fw.op("dve", lambda e: e.memset(self.ones12.ap, 1.0), writes=[self.ones12])
        if "mix0" in self.stages or "mix1" in self.stages:
            self._prologue_mix()

    def _tile(self, t):
        st = self.stages
        if t == 0:
            self.load_x_dma(0)
        self.load_x(t)
        if "ffn1a" in st:
            self.ffn(0, "f1", "ffn1_norm_g")
        if "mix0" in st:
            self.mix_lru(t)
        if "ffn2a" in st:
            self.ffn(0, "f2", "ffn2_norm_g")
        if "ffn1b" in st:
            self.ffn(1, "f1", "ffn1_norm_g")
        if "mix1" in st:
            self.mix_fox(t)
        if t + 1 < self.n_tiles:
            self.load_x_dma(t + 1)
        if "ffn2b" in st:
            self.ffn(1, "f2", "ffn2_norm_g")
        self.store_out(t)

    def _emit(self):
        nc = self.nc
        fw = self.fw
        with nc.Block() as block:
            @block.tensor
            def _(e):
                fw.replay("pe", e)

            @block.scalar
            def _(e):
                fw.replay("act", e)

            @block.vector
            def _(e):
                fw.replay("dve", e)

            @block.gpsimd
            def _(e):
                fw.replay("pool", e)

            @block.sync
            def _(e):
                fw.replay("sp", e)
        self.stack.close()


_CACHE = {}


def host_inputs(inp):
    wts, _ = pack_weights(inp)
    vecs = pack_vecs(inp)
    c = make_consts()
    tri = np.where(np.arange(128)[:, None] <= np.arange(128)[None, :], 0.0, NEG).astype(np.float32)
    oh = np.zeros((128, 16), np.float32)
    for h in range(12):
        oh[h, h] = 1.0
    sw = np.zeros((128, 128), np.float32)
    for k in range(128):
        sw[k, (k + 64) % 128] = 1.0
    shared = {"wts": wts, "vecs": vecs, "ident": c["ident"], "ones_blk": c["ones_blk"], "tri": tri, "onehot": oh,
              "swapm": sw}
    return shared


def kernel(**inputs):
    inp = {k: np.asarray(v) for k, v in inputs.items()}
    shared = host_inputs(inp)
    if "prog" not in _CACHE:
        _CACHE["prog"] = Prog()
    prog = _CACHE["prog"]
    x = np.ascontiguousarray(inp["x"], dtype=np.float32)
    mem = np.ascontiguousarray(inp["mem"], dtype=np.float32)
    in_maps = []
    for b in range(8):
        m = dict(shared)
        m["x"] = x[b]
        m["mem"] = mem[b]
        in_maps.append(m)
    res = run_bass_kernel_spmd(prog.nc, in_maps, core_ids=list(range(8)))
    out = np.stack([np.asarray(r["out"], dtype=np.float32).reshape(S, D) for r in res.results], axis=0)
    return out


def _add_mixers():
    def _prologue_mix(self):
        fw = self.fw
        t = self.tmpf.next()
        lam = self.vecs.ap[:, self.vc["lam"]:self.vc["lam"] + 6]
        one = self.ones12.ap[:, 0:1]
        fw.op("act", lambda e: e.activation(t.ap[:, 0:6], lam, AF.Exp, scale=-1.0), reads=[self.vecs], writes=[t])
        fw.op("act", lambda e: e.activation(t.ap[:, 0:6], t.ap[:, 0:6], AF.Ln, bias=one), reads=[t, self.ones12], writes=[t])
        self.ts("dve", self.cL, t, -8.0, None, ALU.mult, None, out_ap=self.cL.ap[:, 0:6], in_ap=t.ap[:, 0:6])
        self.ts("dve", self.cL, t, -16.0, None, ALU.mult, None, out_ap=self.cL.ap[:, 6:12], in_ap=t.ap[:, 0:6])
        self.ts("dve", self.negbf, self.vecs, -1.0, None, ALU.mult, None, in_ap=self.vcol("b_f"))
        for c in range(6):
            fw.op("dve", lambda e, a=self.hstate[c].ap: e.memset(a, 0.0), writes=[self.hstate[c]])
            fw.op("dve", lambda e, a=self.xbr[c].ap[:, 0:4]: e.memset(a, 0.0), writes=[self.xbr[c]])
        fw.op("dve", lambda e: e.memset(self.cumstate.ap, 0.0), writes=[self.cumstate])
        fw.op("pool", lambda e: e.memset(self.vtok.ap, 1.0), writes=[self.vtok])
        fw.dma("act", self.xin.ap[:, 0:2, :], self.mem_d.rearrange("(c p) d -> p c d", p=128), writes=[self.xin])
        for k in range(8):
            ps = self.ps.next()
            for tc in range(2):
                o = ps.ap[:, tc * 128:(tc + 1) * 128]
                i = self.xin.ap[:, tc, k * 128:(k + 1) * 128]
                fw.op("pe", lambda e, o=o, i=i: e.transpose(o, i, self.ident.ap),
                      reads=[self.xin, self.ident], writes=[ps], signal=(tc == 1))
            self.copy("act" if k % 2 else "dve", self.hT[k], ps, out_ap=self.hT[k].ap[:, 0:256], in_ap=ps.ap[:, 0:256])
        self.rmsnorm_fm(self.hT, "mem_norm_g", self.xn, n=NMEM)
        wkb, wk = self.wtile(("kv", 0))
        wvb, wv = self.wtile(("kv", 1))
        for c in range(2):
            ps = self.ps.next()
            for k in range(8):
                self.mm(ps, ps.ap[:, 0:NMEM], wk[:, k, c * 128:(c + 1) * 128], self.xn[k].ap[:, 0:NMEM], k == 0, k == 7,
                        [wkb, self.xn[k]])
            self.headnorm(ps, "mem_k_norm_g", self.mkT, self.mkT.ap[:, c, :], n=NMEM)
        for nch in range(2):
            ps = self.ps.next()
            for k in range(8):
                self.mm(ps, ps.ap[:, 0:256], self.xn[k].ap[:, nch * 128:(nch + 1) * 128], wv[:, k, :], k == 0, k == 7,
                        [wvb, self.xn[k]])
            self.copy("act", self.mv, ps, out_ap=self.mv.ap[:, nch, :], in_ap=ps.ap[:, 0:256])

    def cross_attn(self, L):
        for c in range(2):
            o = self.psacc.next()
            den = self.psacc.next()
            for h2 in range(2):
                h = 2 * c + h2
                b0 = 64 * h2
                for nch in range(2):
                    s = self.ps.next()
                    self.mm1(s, s.ap, self.mkT.ap[b0:b0 + 64, c, nch * 128:(nch + 1) * 128],
                             self.qmn[c].ap[b0:b0 + 64, :], True, True, [self.mkT, self.qmn[c]])
                    e_ = self.tmpb.next()
                    self.act_(AF.Exp, e_, s, scale=0.125)
                    self.mm1(o, o.ap[b0:b0 + 64, :], self.mv.ap[:, nch, h * 64:(h + 1) * 64], e_.ap,
                             nch == 0, nch == 1, [self.mv, e_])
                    self.mm1(den, den.ap[b0:b0 + 64, :], self.onesblk.ap[:, 0:64], e_.ap,
                             nch == 0, nch == 1, [self.onesblk, e_])
            rec = self.tmpf.next()
            self.act_(AF.Ln, rec, den)
            self.act_(AF.Exp, rec, rec, scale=-1.0)
            self.tt("dve", self.cat[6 + c], o, rec, ALU.mult)

    def out_proj(self, L):
        for g in range(4):
            wb, w = self.wtile(("mo", L, g))
            for fc in range(2):
                oc = 2 * g + fc
                ps = self.ps.next()
                for k in range(8):
                    self.mm(ps, ps.ap, w[:, k, fc * 128:(fc + 1) * 128], self.cat[k].ap, k == 0, k == 7,
                            [wb, self.cat[k]])
                self.tt("dve", self.hT[oc], ps, self.hT[oc], ALU.add)

    def mix_lru(self, t):
        fw = self.fw
        self.rmsnorm_fm(self.hT, ("mix_norm_g", 0), self.xn)
        if t > 0:
            for c in range(6):
                b = self.xbr[c]
                fw.op("pool", lambda e, b=b: e.tensor_copy(b.ap[:, 0:3], b.ap[:, TT:TT + 3]), reads=[b], writes=[b])
        for g in range(7):
            wb, w = self.wtile(("mi", 0, g))
            for fc in range(2):
                oc = 2 * g + fc
                ps = self.ps.next()
                for k in range(8):
                    self.mm(ps, ps.ap, w[:, k, fc * 128:(fc + 1) * 128], self.xn[k].ap, k == 0, k == 7, [wb, self.xn[k]])
                if oc < 6:
                    self.copy("act", self.xbr[oc], ps, out_ap=self.xbr[oc].ap[:, 3:TT + 3])
                elif oc < 12:
                    c = oc - 6
                    u = self.tmpf.next()
                    self.act_(AF.Square, u, ps)
                    self.ts("dve", u, u, 0.044715, 1.0, ALU.mult, ALU.add)
                    self.tt("dve", u, u, ps, ALU.mult)
                    self.act_(AF.Sigmoid, u, u, scale=1.5957691216057308)
                    self.tt("dve", self.cat[c], u, ps, ALU.mult)
                else:
                    c = oc - 12
                    self.headnorm(ps, ("memq_norm_g", 0), self.qmn[c], self.qmn[c].ap)
        one = self.ones12.ap[:, 0:1]
        for c0 in (0, 2, 4):
            cs = (c0, c0 + 1)
            acc, xcb, pr, pi, r, gi, a, m, hs = {}, {}, {}, {}, {}, {}, {}, {}, {}
            for c in cs:
                xb = self.xbr[c]
                acc[c] = self.tmpf.next()
                self.ts("dve", acc[c], xb, self.vcol("conv_w", c), self.vcol("conv_b", c), ALU.mult, ALU.add,
                        in_ap=xb.ap[:, 0:TT], extra_reads=[self.vecs])
                for tap in range(1, 4):
                    self.stt(acc[c], xb, self.vcol("conv_w", tap * 6 + c), acc[c], ALU.mult, ALU.add,
                             in0_ap=xb.ap[:, tap:tap + TT], extra_reads=[self.vecs])
            for c in cs:
                xcb[c] = self.tmpb.next()
                self.copy("pool", xcb[c], acc[c])
            for c in cs:
                wrb, wr = self.wtile(("rg", c))
                wib, wi = self.wtile(("ig", c))
                pr[c] = self.ps.next()
                self.mm(pr[c], pr[c].ap, wr[:, 0, :], xcb[c].ap, True, True, [wrb, xcb[c]])
                pi[c] = self.ps.next()
                self.mm(pi[c], pi[c].ap, wi[:, 0, :], xcb[c].ap, True, True, [wib, xcb[c]])
            for c in cs:
                r[c] = self.tmpf.next()
                self.act_(AF.Sigmoid, r[c], pr[c], bias=self.vcol("b_rg", c), extra_reads=[self.vecs])
                gi[c] = self.tmpf.next()
                self.act_(AF.Sigmoid, gi[c], pi[c], bias=self.vcol("b_ig", c), extra_reads=[self.vecs])
            for c in cs:
                a[c] = self.tmpf.next()
                self.act_(AF.Exp, a[c], r[c], scale=self.cL.ap[:, c:c + 1], extra_reads=[self.cL])
                m[c] = self.tmpf.next()
                self.act_(AF.Exp, m[c], r[c], scale=self.cL.ap[:, 6 + c:7 + c], extra_reads=[self.cL])
            for c in cs:
                self.tt("pool", gi[c], gi[c], acc[c], ALU.mult)
            for c in cs:
                self.act_(AF.Sqrt, m[c], m[c], scale=-1.0, bias=one, extra_reads=[self.ones12])
            for c in cs:
                self.tt("dve", gi[c], gi[c], m[c], ALU.mult)
            for c in cs:
                hs[c] = self.tmpf.next()
                hst = self.hstate[c]
                fw.op("dve", lambda e, hs=hs[c], a=a[c], gi=gi[c], hst=hst: e.tensor_tensor_scan(
                    hs.ap, a.ap, gi.ap, hst.ap, ALU.mult, ALU.add), reads=[a[c], gi[c], hst], writes=[hs[c]])
            for c in cs:
                self.copy("pool", self.hstate[c], hs[c], in_ap=hs[c].ap[:, TT - 1:TT])
                self.tt("dve", self.cat[c], hs[c], self.cat[c], ALU.mult)
        self.cross_attn(0)
        self.out_proj(0)

    Prog._prologue_mix = _prologue_mix
    Prog.cross_attn = cross_attn
    Prog.out_proj = out_proj
    Prog.mix_lru = mix_lru


_add_mixers()


def _add_fox():
    def mix_fox(self, t):
        fw = self.fw
        self.rmsnorm_fm(self.hT, ("mix_norm_g", 1), self.xn)
        for g in range(6):
            wb, w = self.wtile(("mi", 1, g))
            for fc in range(2):
                oc = 2 * g + fc
                ps = self.ps.next()
                for k in range(8):
                    self.mm(ps, ps.ap, w[:, k, fc * 128:(fc + 1) * 128], self.xn[k].ap, k == 0, k == 7, [wb, self.xn[k]])
                if oc < 6:
                    self.headnorm(ps, "fox_q_g", self.qT[oc], self.qT[oc].ap)
                else:
                    c = oc - 6
                    kn = self.tmpb.next()
                    self.headnorm(ps, "fox_k_g", kn, kn.ap)
                    kb = Buf(self.kc_d[c, :, t * TT:(t + 1) * TT], f"kc{c}_{t}")
                    self.kc_bufs[(c, t)] = kb
                    fw.dma("act", kb.ap, kn.ap, reads=[kn], writes=[kb])
        for g in range(3):
            wb, w = self.wtile(("mv", g))
            for tc in range(4):
                ps = self.ps.next()
                for k in range(8):
                    self.mm(ps, ps.ap[:, 0:256], self.xn[k].ap[:, tc * 128:(tc + 1) * 128], w[:, k, :], k == 0, k == 7,
                            [wb, self.xn[k]])
                pv4 = ps.ap[:, 0:256].rearrange("p (c h d) -> p c h d", c=2, h=2)
                self.copy("act", self.vtok, ps, out_ap=self.vtok.ap[:, tc, 2 * g:2 * g + 2, 0:64], in_ap=pv4[:, :, 0, :])
                self.copy("dve", self.vtok, ps, out_ap=self.vtok.ap[:, tc, 2 * g:2 * g + 2, 192:256], in_ap=pv4[:, :, 1, :])
        for c in range(6):
            vb = Buf(self.vc_d[c, :, 4 * t:4 * t + 4, :], f"vc{c}_{t}")
            self.vc_bufs[(c, t)] = vb
            fw.dma("act", vb.ap, self.vtok.ap[:, :, c, :], reads=[self.vtok], writes=[vb])
        wb, w = self.wtile(("mf",))
        ps = self.ps.next()
        for k in range(8):
            self.mm(ps, ps.ap[0:12, :], w[:, k, 0:12], self.xn[k].ap, k == 0, k == 7, [wb, self.xn[k]])
        lf = self.tmpf.next()
        self.act_(AF.Exp, lf, ps, out_ap=lf.ap[0:12, :], in_ap=ps.ap[0:12, :], scale=-1.0, bias=self.negbf.ap[0:12, :],
                  extra_reads=[self.negbf])
        self.act_(AF.Ln, lf, lf, out_ap=lf.ap[0:12, :], in_ap=lf.ap[0:12, :], bias=self.ones12.ap[0:12, 0:1],
                  extra_reads=[self.ones12])
        fw.op("dve", lambda e, lf=lf: e.tensor_tensor_scan(self.cum.ap[0:12, :], self.ones12.ap[0:12, :], lf.ap[0:12, :],
                                                            self.cumstate.ap[0:12, :], ALU.mult, ALU.subtract),
              reads=[lf, self.ones12, self.cumstate], writes=[self.cum])
        self.copy("pool", self.cumstate, self.cum, out_ap=self.cumstate.ap[0:12, :], in_ap=self.cum.ap[0:12, TT - 1:TT])
        for tc in range(4):
            tp = self.ps.next()
            fw.op("pe", lambda e, tp=tp, tc=tc: e.transpose(tp.ap[:, 0:12], self.cum.ap[0:12, tc * 128:(tc + 1) * 128],
                                                           self.ident.ap[0:12, 0:12]),
                  reads=[self.cum, self.ident], writes=[tp])
            self.ts("dve", self.negcumT, tp, -1.0, None, ALU.mult, None,
                    out_ap=self.negcumT.ap[:, 4 * t + tc, :], in_ap=tp.ap[:, 0:12])
        wb, w = self.wtile(("mq",))
        for c in range(2):
            ps = self.ps.next()
            for k in range(8):
                self.mm(ps, ps.ap, w[:, k, c * 128:(c + 1) * 128], self.xn[k].ap, k == 0, k == 7, [wb, self.xn[k]])
            self.headnorm(ps, ("memq_norm_g", 1), self.qmn[c], self.qmn[c].ap)
        nkeys = (t + 1) * TT

        def prep(c):
            for h2 in range(2):
                h = 2 * c + h2
                sel = self.tmpf.next()
                self.ts("dve", sel, self.cum, self.oh.ap[0:12, h:h + 1], None, ALU.mult, None,
                        out_ap=sel.ap[0:12, :], in_ap=self.cum.ap[0:12, :], extra_reads=[self.oh])
                pb = self.ps.next()
                self.mm1(pb, pb.ap, self.ones12.ap[0:12, 0:128], sel.ap[0:12, :], True, True, [self.ones12, sel])
                self.copy("act", self.cq[2 * (c % 2) + h2], pb)

        def finalize(c, o, den):
            rec = self.tmpf.next()
            self.act_(AF.Ln, rec, o, out_ap=rec.ap[0:64, :], in_ap=o.ap[64:128, :])
            self.act_(AF.Ln, rec, den, out_ap=rec.ap[64:128, :], in_ap=den.ap[0:64, :])
            self.act_(AF.Exp, rec, rec, scale=-1.0)
            self.tt("dve", self.cat[c], o, rec, ALU.mult, out_ap=self.cat[c].ap[0:64, :], a_ap=o.ap[0:64, :],
                    b_ap=rec.ap[0:64, :])
            self.tt("dve", self.cat[c], den, rec, ALU.mult, out_ap=self.cat[c].ap[64:128, :], a_ap=den.ap[64:128, :],
                    b_ap=rec.ap[64:128, :])

        pending_fin = None
        prep(0)
        for c in range(6):
            o = self.psacc.next()
            den = self.psacc.next()
            if c + 1 < 6:
                prep(c + 1)
            blocks = []
            kvb = {}
            for kb0 in range(0, nkeys, 1024):
                wk = min(1024, nkeys - kb0)
                for k8 in range(wk // 128):
                    for h2 in range(2):
                        blocks.append((kb0, wk, h2, k8, kb0 // 128 + k8))

            def kv_load(kb0, wk):
                if kb0 not in kvb:
                    kbuf = self.kst.next()
                    vbuf = self.vst.next()
                    tiles = range(kb0 // TT, (kb0 + wk) // TT)
                    fw.dma("sp", kbuf.ap[:, 0:wk], self.kc_d[c, :, kb0:kb0 + wk],
                           reads=[self.kc_bufs[(c, tt_)] for tt_ in tiles], writes=[kbuf])
                    fw.dma("sp", vbuf.ap[:, 0:wk // 128, :], self.vc_d[c, :, kb0 // 128:(kb0 + wk) // 128, :],
                           reads=[self.vc_bufs[(c, tt_)] for tt_ in tiles], writes=[vbuf])
                    kvb[kb0] = (kbuf, vbuf)
                return kvb[kb0]

            def emit_qk(blk):
                kb0, wk, h2, k8, kcg = blk
                kbuf, vbuf = kv_load(kb0, wk)
                h = 2 * c + h2
                b0 = 64 * h2
                diag = kcg - 4 * t
                q0 = 128 * diag if diag >= 0 else 0
                s = self.ps.next()
                self.mm1(s, s.ap[:, q0:TT], kbuf.ap[b0:b0 + 64, k8 * 128:(k8 + 1) * 128],
                         self.qT[c].ap[b0:b0 + 64, q0:TT], True, True, [kbuf, self.qT[c]])
                tm = self.tmpf.next()
                self.stt(tm, s, 0.125, self.cq[2 * (c % 2) + h2], ALU.mult, ALU.add, out_ap=tm.ap[:, q0:TT],
                         in0_ap=s.ap[:, q0:TT], in1_ap=self.cq[2 * (c % 2) + h2].ap[:, q0:TT])
                if diag >= 0:
                    self.tt("pool", tm, tm, self.tri, ALU.add, out_ap=tm.ap[:, q0:q0 + 128],
                            a_ap=tm.ap[:, q0:q0 + 128])
                p = self.tmpb.next()
                self.act_(AF.Exp, p, tm, out_ap=p.ap[:, q0:TT], in_ap=tm.ap[:, q0:TT],
                          bias=self.negcumT.ap[:, kcg, h:h + 1], extra_reads=[self.negcumT])
                return p, q0

            def emit_pv(blk, p, q0):
                kb0, wk, h2, k8, kcg = blk
                kbuf, vbuf = kvb[kb0]
                b0 = 64 * h2
                first = kcg == 0
                last = kcg == 4 * t + 3
                acc = o if h2 == 0 else den
                self.mm1(acc, acc.ap[:, q0:TT], vbuf.ap[:, k8, h2 * 128:(h2 + 1) * 128], p.ap[:, q0:TT], first, last,
                         [vbuf, p])

            pend = []
            npair = len(blocks) // 2
            LAP = 2
            for j in range(npair + LAP):
                if j < npair:
                    pend.append(emit_qk(blocks[2 * j]))
                    pend.append(emit_qk(blocks[2 * j + 1]))
                if j == 0 and pending_fin is not None:
                    finalize(*pending_fin)
                    pending_fin = None
                if j >= LAP:
                    emit_pv(blocks[2 * (j - LAP)], *pend[2 * (j - LAP)])
                    emit_pv(blocks[2 * (j - LAP) + 1], *pend[2 * (j - LAP) + 1])
            pending_fin = (c, o, den)
        finalize(*pending_fin)
        self.cross_attn(1)
        self.out_proj(1)

    Prog.mix_fox = mix_fox


_add_fox()
```
